# Optimizing a Trainium2 kernel written in Bass

```python
import jax, jax.numpy as jnp
from jax import lax
import numpy as np

D_MODEL = 2048
BATCH = 8
SEQ = 2048
DEPTH = 1

D_MIX = D_MODEL
D_CONV = D_MIX // 2
D_ATTN = D_MIX - D_CONV
HEAD_DIM = 64
N_HEADS = D_ATTN // HEAD_DIM
CONV_WIDTH = 3
DILATED_BRANCHES = ((128, 1), (512, 4), (2048, 16))
N_PROJ = 4 * D_CONV + 4 * D_ATTN
EPS = 1e-6
NEG_INF = -1e30

kernel_name = "hybrid_conv_dilated_attn_block"


def rms_norm(x, gain):
    xf = x.astype(jnp.float32)
    y = xf * lax.rsqrt(jnp.mean(xf * xf, axis=-1, keepdims=True) + EPS)
    return (y * gain.astype(jnp.float32)).astype(x.dtype)


def alibi_slopes(n_heads):
    return 2.0 ** (-8.0 * jnp.arange(1, n_heads + 1, dtype=jnp.float32) / n_heads)


def short_conv_centred(u, w, b):
    ch = u.shape[-1]
    pad = CONV_WIDTH // 2
    y = lax.conv_general_dilated(
        u, w[:, None, :].astype(u.dtype), window_strides=(1,), padding=((pad, pad),),
        dimension_numbers=("NWC", "WIO", "NWC"), feature_group_count=ch)
    return y + b.astype(u.dtype)


def dilated_branch(q, k, v, slopes, window, dilation):
    bsz, n_heads, seq, hd = q.shape
    r = dilation
    n = (window // 2) // r
    L = seq // r
    nb = -(-L // n)
    Lp = nb * n

    def to_res(t):
        return t.reshape(bsz, n_heads, L, r, hd).transpose(0, 1, 3, 2, 4)

    qb = jnp.pad(to_res(q), ((0, 0), (0, 0), (0, 0), (0, Lp - L), (0, 0)))
    qb = qb.reshape(bsz, n_heads, r, nb, n, hd)
    kv_pad = ((0, 0), (0, 0), (0, 0), (n, Lp - L + n), (0, 0))
    kb = jnp.pad(to_res(k), kv_pad).reshape(bsz, n_heads, r, nb + 2, n, hd)
    vb = jnp.pad(to_res(v), kv_pad).reshape(bsz, n_heads, r, nb + 2, n, hd)
    shifts = (slice(0, nb), slice(1, nb + 1), slice(2, nb + 2))

    s = jnp.concatenate(
        [jnp.einsum("bhrnqd,bhrnkd->bhrnqk", qb, kb[:, :, :, sl]) for sl in shifts],
        axis=-1).astype(jnp.float32) * (hd ** -0.5)

    qi = jnp.arange(n)
    ki = jnp.arange(3 * n)
    blk = jnp.arange(nb)
    off = ki[None, :] - n - qi[:, None]
    key_idx = blk[:, None] * n + ki[None, :] - n
    valid = (jnp.abs(off) <= n)[None] & ((key_idx >= 0) & (key_idx < L))[:, None, :]
    dist = (r * jnp.abs(off)).astype(jnp.float32)
    bias = -slopes[:, None, None] * dist[None]
    s = jnp.where(valid[None, None, None], s + bias[None, :, None, None], NEG_INF)

    m = jnp.max(s, axis=-1, keepdims=True)
    p = jnp.exp(s - m)
    den = jnp.sum(p, axis=-1, keepdims=True)
    o = jnp.einsum("bhrnqk,bhrnkd->bhrnqd", p[..., 0:n], vb[:, :, :, shifts[0]].astype(jnp.float32))
    o = o + jnp.einsum("bhrnqk,bhrnkd->bhrnqd", p[..., n:2 * n], vb[:, :, :, shifts[1]].astype(jnp.float32))
    o = o + jnp.einsum("bhrnqk,bhrnkd->bhrnqd", p[..., 2 * n:], vb[:, :, :, shifts[2]].astype(jnp.float32))
    o = o / den
    lse = (m + jnp.log(den))[..., 0]

    o = o.reshape(bsz, n_heads, r, Lp, hd)[:, :, :, :L]
    o = o.transpose(0, 1, 3, 2, 4).reshape(bsz, n_heads, seq, hd)
    lse = lse.reshape(bsz, n_heads, r, Lp)[..., :L]
    lse = lse.transpose(0, 1, 3, 2).reshape(bsz, n_heads, seq)
    return o, lse


def dilated_mixture_attention(q, k, v):
    slopes = alibi_slopes(q.shape[1])
    outs, lses = [], []
    for window, dilation in DILATED_BRANCHES:
        o, lse = dilated_branch(q, k, v, slopes, window, dilation)
        outs.append(o)
        lses.append(lse)
    alpha = jax.nn.softmax(jnp.stack(lses, axis=0), axis=0)
    return jnp.sum(alpha[..., None] * jnp.stack(outs, axis=0), axis=0)


def hybrid_layer(x, mod, g_pre, w_in, conv_w, conv_b, g_conv, g_attn, w_out, g_post):
    bsz, seq, _ = x.shape
    shift, scale, gate = jnp.split(mod, 3, axis=-1)
    h = rms_norm(x, g_pre) * (1.0 + scale[:, None, :]) + shift[:, None, :]

    proj = jnp.einsum("bsd,dn->bsn", h, w_in)
    cuts = np.cumsum([D_CONV, D_CONV, D_CONV, D_CONV, D_ATTN, D_ATTN, D_ATTN])
    u, b_gate, c_gate, z_c, q, k, v, z_a = jnp.split(proj, cuts, axis=-1)

    y_c = b_gate * short_conv_centred(c_gate * u, conv_w, conv_b)
    y_c = rms_norm(y_c, g_conv) * jax.nn.silu(z_c)

    def heads(t):
        return t.reshape(bsz, seq, N_HEADS, HEAD_DIM).transpose(0, 2, 1, 3)
    o = dilated_mixture_attention(heads(q), heads(k), heads(v))
    y_a = o.transpose(0, 2, 1, 3).reshape(bsz, seq, D_ATTN).astype(x.dtype)
    y_a = rms_norm(y_a, g_attn) * jax.nn.silu(z_a)

    y = jnp.einsum("bsn,nd->bsd", jnp.concatenate([y_c, y_a], axis=-1), w_out)
    return x + gate[:, None, :] * rms_norm(y, g_post)


def setup_inputs(seed: int = 0) -> dict:
    key = jax.random.key(seed)
    ks = jax.random.split(key, 14)
    f32 = jnp.float32
    x = jax.random.normal(ks[0], (BATCH, SEQ, D_MODEL), f32)
    c = jax.random.normal(ks[1], (BATCH, D_MODEL), f32)
    w_ada = jax.random.normal(ks[2], (DEPTH, D_MODEL, 3 * D_MODEL), f32) * D_MODEL ** -0.5
    b_ada = 0.02 * jax.random.normal(ks[3], (DEPTH, 3 * D_MODEL), f32)
    g_pre = 1.0 + 0.05 * jax.random.normal(ks[4], (DEPTH, D_MODEL), f32)
    w_in = jax.random.normal(ks[5], (DEPTH, D_MODEL, N_PROJ), f32) * D_MODEL ** -0.5
    conv_w = jax.random.normal(ks[6], (DEPTH, CONV_WIDTH, D_CONV), f32) * CONV_WIDTH ** -0.5
    conv_b = 0.02 * jax.random.normal(ks[7], (DEPTH, D_CONV), f32)
    g_conv = 1.0 + 0.05 * jax.random.normal(ks[8], (DEPTH, D_CONV), f32)
    g_attn = 1.0 + 0.05 * jax.random.normal(ks[9], (DEPTH, D_ATTN), f32)
    w_out = jax.random.normal(ks[10], (DEPTH, D_MIX, D_MODEL), f32) * D_MIX ** -0.5
    g_post = 1.0 + 0.05 * jax.random.normal(ks[11], (DEPTH, D_MODEL), f32)
    return {"x": x, "c": c, "w_ada": w_ada, "b_ada": b_ada, "g_pre": g_pre,
            "w_in": w_in, "conv_w": conv_w, "conv_b": conv_b, "g_conv": g_conv,
            "g_attn": g_attn, "w_out": w_out, "g_post": g_post}


def reference(x, c, w_ada, b_ada, g_pre, w_in, conv_w, conv_b, g_conv, g_attn, w_out, g_post):
    c_act = jax.nn.silu(c)
    for layer in range(DEPTH):
        mod = jnp.einsum("bd,dn->bn", c_act, w_ada[layer]) + b_ada[layer]
        x = hybrid_layer(x, mod, g_pre[layer], w_in[layer], conv_w[layer], conv_b[layer],
                         g_conv[layer], g_attn[layer], w_out[layer], g_post[layer])
    return x
```

```python
import numpy as np
import concourse.bass as bass
import concourse.mybir as mybir
from concourse.bass_utils import run_bass_kernel_spmd

F32 = mybir.dt.float32
BF16 = mybir.dt.bfloat16
AF = mybir.ActivationFunctionType
ALU = mybir.AluOpType
AX = mybir.AxisListType

D = 2048
S = 2048
NPROJ = 8192
NMOD = 6144
KC = 16
EPS = 1e-6
NEG8 = -8.0 * 30000.0
N_CORES = 8
DBG = {"pair": -1}


class _Op:
    __slots__ = ("eng", "fn", "deps", "dma_key", "dma_n", "signal", "sig")

    def __init__(self, eng, fn):
        self.eng = eng
        self.fn = fn
        self.deps = []
        self.dma_key = None
        self.dma_n = 0
        self.signal = False
        self.sig = 0


class _DmaTok:
    __slots__ = ("key", "n")

    def __init__(self, key, n):
        self.key = key
        self.n = n


class Sched:
    ENGS = ("pe", "act", "dve", "pool", "sp")

    def __init__(self):
        self.streams = {e: [] for e in self.ENGS}
        self.res_w = {}
        self.res_r = {}
        self.dma_cnt = {}
        self.last = {}

    @staticmethod
    def _grp(tok):
        return ("dma", tok.key) if isinstance(tok, _DmaTok) else tok.eng

    def add(self, eng, fn, reads=(), writes=(), dma_key=None, extra_deps=()):
        op = _Op(eng, fn)
        deps = {}
        for r in reads:
            for t in self.res_w.get(r, {}).values():
                deps[id(t)] = t
        for w in writes:
            for t in self.res_w.get(w, {}).values():
                deps[id(t)] = t
            for t in self.res_r.get(w, {}).values():
                deps[id(t)] = t
        for t in extra_deps:
            deps[id(t)] = t
        if dma_key is not None:
            n = self.dma_cnt.get(dma_key, 0) + 1
            self.dma_cnt[dma_key] = n
            op.dma_key = dma_key
            op.dma_n = n
            tok = _DmaTok(dma_key, n)
        else:
            tok = op
        g = self._grp(tok)
        for r in reads:
            self.res_r.setdefault(r, {})[g] = tok
        for w in writes:
            self.res_w[w] = {g: tok}
            self.res_r[w] = {}
        deps.pop(id(op), None)
        op.deps = list(deps.values())
        self.streams[eng].append(op)
        if fn is not None:
            self.last[g] = tok
        return tok

    def barrier(self):
        toks = list(self.last.values())
        for e in self.ENGS:
            self.add(e, None, extra_deps=toks)

    def finalize(self):
        for e in self.ENGS:
            for op in self.streams[e]:
                for d in op.deps:
                    if isinstance(d, _Op):
                        if d.eng == "pe" and op.eng == "pe":
                            continue
                        d.signal = True
        for e in self.ENGS:
            c = 0
            for op in self.streams[e]:
                if op.signal:
                    c += 1
                    op.sig = c

    def emit(self, eng_name, e, eng_sems, dma_sems):
        waited = {}
        for op in self.streams[eng_name]:
            need = {}
            for d in op.deps:
                if isinstance(d, _Op):
                    if d.eng == "pe" and eng_name == "pe":
                        continue
                    g, v = d.eng, d.sig
                else:
                    g = ("dma", d.key)
                    v = 16 * (self.dma_cnt[d.key] if d.key == "const" else d.n)
                if v > need.get(g, 0):
                    need[g] = v
            for g, v in need.items():
                if waited.get(g, 0) >= v:
                    continue
                waited[g] = v
                sem = dma_sems[g[1]] if isinstance(g, tuple) else eng_sems[g]
                e.wait_ge(sem, v)
            if op.fn is None:
                continue
            ins = op.fn(e)
            if op.dma_key is not None:
                ins.then_inc(dma_sems[op.dma_key], 16)
            elif op.signal:
                ins.then_inc(eng_sems[eng_name], 1)


def _slopes():
    return [2.0 ** (-8.0 * (h + 1) / 16.0) for h in range(16)]


def build_nc(stop_after=None, npairs=8):
    nc = bass.Bass("TRN2", target_bir_lowering=False)
    x_d = nc.dram_tensor("x", [S, D], F32, kind="ExternalInput").ap()
    c_d = nc.dram_tensor("c_lay", [128, KC], F32, kind="ExternalInput").ap()
    wada_d = nc.dram_tensor("w_ada", [D, NMOD], F32, kind="ExternalInput").ap()
    bada_d = nc.dram_tensor("b_row", [1, NMOD], F32, kind="ExternalInput").ap()
    gpre_d = nc.dram_tensor("gpre_lay", [128, KC], F32, kind="ExternalInput").ap()
    win_d = nc.dram_tensor("w_in", [D, NPROJ], F32, kind="ExternalInput").ap()
    convw_d = nc.dram_tensor("convw_lay", [128, 8, 3], F32, kind="ExternalInput").ap()
    convb_d = nc.dram_tensor("convb_lay", [128, 8], F32, kind="ExternalInput").ap()
    gconv_d = nc.dram_tensor("gconv_lay", [128, 8], F32, kind="ExternalInput").ap()
    gattn_d = nc.dram_tensor("gattn_lay", [128, 8], F32, kind="ExternalInput").ap()
    wout_d = nc.dram_tensor("w_out", [D, D], F32, kind="ExternalInput").ap()
    gpost_d = nc.dram_tensor("gpost_row", [1, D], F32, kind="ExternalInput").ap()
    out_d = nc.dram_tensor("out", [S, D], F32, kind="ExternalOutput").ap()
    dbg_d = None
    if stop_after is not None:
        dbg_d = nc.dram_tensor("dbg", [128, 16, 2048], BF16, kind="ExternalOutput").ap()

    sc = Sched()
    UF = 20992

    import contextlib
    with contextlib.ExitStack() as es:
        def sb(name, shape, dt):
            return es.enter_context(nc.sbuf_tensor(name, shape, dt))

        hT = sb("hT", [128, KC, 2048], BF16)
        yTa = sb("yTa", [128, 8, 2048], BF16)
        U = sb("U", [128, UF], F32)
        wsl = sb("wsl", [128, 4, KC, 128], BF16)
        ident_f = sb("ident_f", [128, 128], F32)
        ident_b = sb("ident_b", [128, 128], BF16)
        ones_f = sb("ones_f", [128, 128], F32)
        ones_b = sb("ones_b", [128, 128], BF16)
        base_f = sb("base_f", [128, 256], F32)
        mask8 = sb("mask8", [128, 256], F32)
        sm = sb("sm", [128, 512], F32)
        cab = sb("cab", [128, KC], BF16)
        pb = [es.enter_context(nc.psum_tensor(f"pb{i}", [128, 512], F32)) for i in range(8)]

        c_sb = sm[:, 0:16]
        gpre = sm[:, 16:32]
        a_sb = sm[:, 32:48]
        s_sb = sm[:, 48:64]
        gate_lay = sm[:, 64:80]
        convw = sm[:, 80:104].rearrange("p (f t) -> p f t", t=3)
        convb = sm[:, 104:112]
        gconv = sm[:, 112:120]
        gattn = sm[:, 120:128]
        ssq_x = sm[:, 128:144]
        rstd_x = sm[:, 144:160]
        ssq_a = sm[:, 160:176]
        ssq_c = sm[:, 176:192]
        ra = sm[:, 192:208]
        rc = sm[:, 208:224]
        ssq_p = sm[:, 224:288]
        ssq_p1 = sm[:, 288:304]
        rp = sm[:, 304:320]
        tmp16 = sm[:, 320:336]

        def uview(off, dt, shape):
            n = 1
            for s_ in shape:
                n *= s_
            nb = n * (4 if dt == F32 else 2)
            assert off % 4 == 0 and nb % 4 == 0 and off + nb <= UF * 4, (off, nb)
            a = U[:, off // 4:(off + nb) // 4]
            if dt != F32:
                a = a.bitcast(dt)
            if len(shape) == 2:
                return a.rearrange("p (a b) -> p a b", b=shape[1])
            if len(shape) == 3:
                return a.rearrange("p (a b c) -> p a b c", b=shape[1], c=shape[2])
            return a

        def psbf(bank):
            return pb[bank][:, :].bitcast(BF16)

        sc.add("pool", lambda e: e.iota(base_f[:, :], [[1, 256]], base=-64, channel_multiplier=-1, allow_small_or_imprecise_dtypes=True),
               writes=["base_f"])
        sc.add("dve", lambda e: e.tensor_single_scalar(out=ident_f[:, :], in_=base_f[:, 64:192], scalar=0.0, op=ALU.is_equal),
               reads=["base_f"], writes=["ident_f"])
        sc.add("dve", lambda e: e.tensor_copy(out=ident_b[:, :], in_=ident_f[:, :]), reads=["ident_f"], writes=["ident_b"])
        sc.add("dve", lambda e: e.memset(ones_f[:, :], 1.0), writes=["ones_f"])
        sc.add("dve", lambda e: e.memset(ones_b[:, :], 1.0), writes=["ones_b"])
        sc.add("dve", lambda e: e.tensor_scalar(out=mask8[:, :], in0=base_f[:, :], scalar1=-1.0, scalar2=None, op0=ALU.mult),
               reads=["ident_f", "base_f"], writes=["mask8"])
        sc.add("dve", lambda e: e.tensor_tensor(out=base_f[:, :], in0=base_f[:, :], in1=mask8[:, :], op=ALU.max),
               reads=["mask8"], writes=["base_f"])
        sc.add("dve", lambda e: e.tensor_scalar(out=mask8[:, :], in0=base_f[:, :], scalar1=64.5, scalar2=NEG8,
                                                op0=ALU.is_gt, op1=ALU.mult),
               reads=["base_f"], writes=["mask8"])

        def cdma(dst, src, key):
            sc.add("sp", lambda e: e.dma_start(out=dst, in_=src), writes=[key], dma_key="const")

        cdma(c_sb, c_d[:, :], "c_sb")
        cdma(gpre, gpre_d[:, :], "gpre")
        cdma(convw, convw_d[:, :, :], "convw")
        cdma(convb, convb_d[:, :], "convb")
        cdma(gconv, gconv_d[:, :], "gconv")
        cdma(gattn, gattn_d[:, :], "gattn")

        def dump_small():
            sc.barrier()
            sc.add("sp", lambda e: e.dma_start(out=dbg_d[:, 0, 0:256], in_=sm[:, 0:256].bitcast(BF16)[:, 0:256]), dma_key="out")
            sc.add("sp", lambda e: e.dma_start(out=out_d[0:128, 0:256], in_=base_f[:, :]), dma_key="out")
            sc.add("sp", lambda e: e.dma_start(out=out_d[128:256, 0:256], in_=mask8[:, :]), dma_key="out")
            sc.add("sp", lambda e: e.dma_start(out=out_d[256:384, 0:128], in_=ident_f[:, :]), dma_key="out")
            sc.add("sp", lambda e: e.dma_start(out=out_d[384:512, 0:512], in_=sm[:, :]), dma_key="out")
        done = False
        if stop_after == "const":
            dump_small()
            done = True
        mod_row = U[0:1, 0:NMOD]
        b_row = U[0:1, NMOD:2 * NMOD]
        xs = [uview(2 * NMOD * 4 + 8192 * i, F32, [2048]) for i in range(2)]
        xn = [yTa[:, i, :] for i in range(2)]
        if not done:
          cdma(b_row, bada_d[:, :], "b_row")
          sc.add("act", lambda e: e.activation(out=cab[:, :], in_=c_sb, func=AF.Silu), reads=["c_sb"], writes=["cab"])

          def wa_slot(s_):
              return hT[:, 4 * s_:4 * s_ + 4, :].rearrange("p a (b n) -> p (a b) n", n=512)

          for s_ in range(12):
              slot = s_ % 4
              keys = [("hT", k) for k in range(4 * slot, 4 * slot + 4)]
              dst = wa_slot(slot)
              src = wada_d[:, 512 * s_:512 * s_ + 512].rearrange("(k p) n -> p k n", p=128)
              sc.add("pool", (lambda e, dst=dst, src=src: e.dma_start(out=dst, in_=src)),
                     writes=keys, dma_key=("wa", slot))
              bank = s_ % 2
              for k in range(KC):
                  sc.add("pe", (lambda e, bank=bank, k=k, dst=dst: e.matmul(
                      pb[bank][0:1, :], lhsT=cab[:, k:k + 1], rhs=dst[:, k, :], start=(k == 0), stop=(k == KC - 1))),
                      reads=keys + ["cab"], writes=[("pb", bank)])
              sc.add("dve", (lambda e, bank=bank, s_=s_: e.tensor_tensor(
                  out=mod_row[:, 512 * s_:512 * s_ + 512], in0=pb[bank][0:1, :],
                  in1=b_row[:, 512 * s_:512 * s_ + 512], op=ALU.add)),
                  reads=[("pb", bank), "b_row"], writes=[("mod", s_)])
          for j in range(48):
              sc.add("pe", (lambda e, j=j: e.matmul(pb[2][:, j:j + 1], lhsT=mod_row[:, 128 * j:128 * j + 128],
                                                    rhs=ones_f[0:1, 0:1], start=True, stop=True)),
                     reads=[("mod", j // 4), "ones_f"], writes=[("pb", 2)])
          sc.add("dve", lambda e: e.tensor_copy(out=s_sb, in_=pb[2][:, 0:16]), reads=[("pb", 2)], writes=["s_sb"])
          sc.add("dve", lambda e: e.scalar_tensor_tensor(out=a_sb, in0=pb[2][:, 16:32], scalar=1.0, in1=gpre,
                                                         op0=ALU.add, op1=ALU.mult),
                 reads=[("pb", 2), "gpre"], writes=["a_sb"])
          sc.add("dve", lambda e: e.tensor_copy(out=gate_lay, in_=pb[2][:, 32:48]), reads=[("pb", 2)], writes=["gate_lay"])

          xs = [uview(2 * NMOD * 4 + 8192 * i, F32, [2048]) for i in range(2)]
          xn = [yTa[:, i, :] for i in range(2)]
          for i in range(0 if stop_after == "mod" else 16):
              b2 = i % 2
              sc.add("sp", (lambda e, i=i, b2=b2: e.dma_start(out=xs[b2], in_=x_d[128 * i:128 * i + 128, :])),
                     writes=[("xs", b2)], dma_key=("xs", b2))
              sc.add("act", (lambda e, i=i, b2=b2: e.activation(out=xn[b2], in_=xs[b2], func=AF.Square,
                                                                accum_out=ssq_x[:, i:i + 1])),
                     reads=[("xs", b2)], writes=[("xn", b2), ("ssq_x", i)])
              sc.add("dve", (lambda e, i=i: e.tensor_scalar(out=rstd_x[:, i:i + 1], in0=ssq_x[:, i:i + 1], scalar1=1.0 / D,
                                                            scalar2=EPS, op0=ALU.mult, op1=ALU.add)),
                     reads=[("ssq_x", i)], writes=[("rstd_x", i)])
              sc.add("act", (lambda e, i=i: e.activation(out=rstd_x[:, i:i + 1], in_=rstd_x[:, i:i + 1], func=AF.Sqrt)),
                     reads=[("rstd_x", i)], writes=[("rstd_x", i)])
              sc.add("dve", (lambda e, i=i: e.reciprocal(out=rstd_x[:, i:i + 1], in_=rstd_x[:, i:i + 1])),
                     reads=[("rstd_x", i)], writes=[("rstd_x", i)])
              sc.add("dve", (lambda e, i=i, b2=b2: e.tensor_scalar(out=xn[b2], in0=xs[b2], scalar1=rstd_x[:, i:i + 1],
                                                                   scalar2=None, op0=ALU.mult)),
                     reads=[("xs", b2), ("rstd_x", i)], writes=[("xn", b2)])
              for g in range(2):
                  bank = 4 + 2 * b2 + g
                  for kk in range(8):
                      k = 8 * g + kk
                      sc.add("pe", (lambda e, bank=bank, kk=kk, k=k, b2=b2: e.transpose(
                          psbf(bank)[:, 128 * kk:128 * kk + 128], xn[b2][:, 128 * k:128 * k + 128], ident_b[:, :])),
                          reads=[("xn", b2), "ident_b"], writes=[("pb", bank)])
                  for kk in range(8):
                      k = 8 * g + kk
                      if g == 0:
                          sc.add("act", (lambda e, bank=bank, kk=kk, k=k, i=i: e.activation(
                              out=hT[:, k, 128 * i:128 * i + 128], in_=psbf(bank)[:, 128 * kk:128 * kk + 128],
                              func=AF.Identity, bias=s_sb[:, k:k + 1], scale=a_sb[:, k:k + 1])),
                              reads=[("pb", bank), "a_sb", "s_sb"], writes=[("hT", k)])
                      else:
                          sc.add("dve", (lambda e, bank=bank, kk=kk, k=k, i=i: e.tensor_scalar(
                              out=hT[:, k, 128 * i:128 * i + 128], in0=psbf(bank)[:, 128 * kk:128 * kk + 128],
                              scalar1=a_sb[:, k:k + 1], scalar2=s_sb[:, k:k + 1], op0=ALU.mult, op1=ALU.add)),
                              reads=[("pb", bank), "a_sb", "s_sb"], writes=[("hT", k)])

        def dump(src_ap, nk):
            sc.barrier()
            sc.add("sp", lambda e: e.dma_start(out=dbg_d[:, 0:nk, :], in_=src_ap), dma_key="out")
            sc.add("sp", lambda e: e.dma_start(out=out_d[0:128, :], in_=xs[0]), dma_key="out")

        if stop_after == "mod" and not done:
            dump_small()
            done = True
        if stop_after == "hT" and not done:
            dump(hT[:, :, :], 16)
            done = True

        state = {"wslot": 0, "bank": 0, "nbank": 3}

        order = [32, 40, 48]
        for p_ in range(npairs):
            if p_ >= 1:
                order.append(56 + p_ - 1)
            if p_ + 1 < npairs:
                order += [32 + p_ + 1, 40 + p_ + 1, 48 + p_ + 1]
        order.append(56 + npairs - 1)
        for f_ in range(8):
            order += [f_, 16 + f_, 8 + f_, 24 + f_]
        state["issued"] = 0

        def issue_w(upto):
            while state["issued"] < min(upto, len(order)):
                n_ = state["issued"]
                j_ = order[n_]
                slot_ = n_ % 4
                src = win_d[:, 128 * j_:128 * j_ + 128].rearrange("(k p) n -> p k n", p=128)
                sc.add("pool", (lambda e, slot_=slot_, src=src: e.dma_start(out=wsl[:, slot_], in_=src)),
                       writes=[("wsl", slot_)], dma_key=("wsl", slot_))
                state["issued"] += 1

        def in_proj_units(j, evac):
            n_ = state["wslot"]
            assert order[n_] == j, (n_, j, order[n_])
            slot = n_ % 4
            state["wslot"] += 1
            issue_w(n_ + 4)
            for t in range(4):
                bank = state["bank"] % state["nbank"]
                state["bank"] += 1
                for k in range(KC):
                    sc.add("pe", (lambda e, bank=bank, k=k, t=t, slot=slot: e.matmul(
                        pb[bank][:, :], lhsT=wsl[:, slot, k, :], rhs=hT[:, k, 512 * t:512 * t + 512],
                        start=(k == 0), stop=(k == KC - 1))),
                        reads=[("wsl", slot), ("hT", k)], writes=[("pb", bank)])
                    if k == KC - 1:
                        evac(t, bank)
                    if k % 2 == 1:
                        yield

        def in_proj(j, evac):
            for _ in in_proj_units(j, evac):
                pass

        def merge(a_gens, b_gen):
            import itertools
            A = itertools.chain(*a_gens)
            B = b_gen if b_gen is not None else iter(())
            a_done = b_done = False
            while not (a_done and b_done):
                if not b_done:
                    try:
                        next(B)
                    except StopIteration:
                        b_done = True
                for _ in range(2):
                    if not a_done:
                        try:
                            next(A)
                        except StopIteration:
                            a_done = True

        slopes = _slopes()

        if not done:
            sc.barrier()
            state["nbank"] = 2
            state["bank"] = 0
            qTb = [uview(0, BF16, [2048]), uview(4096, BF16, [2048])]
            kTb = [uview(8192, BF16, [2048]), uview(12288, BF16, [2048])]
            vT = uview(16384, BF16, [2048])
            Vaug = uview(20480, BF16, [3, 16, 192])
            oaccb = [uview(38912, F32, [2048]), uview(47104, F32, [2048])]
            opair = uview(55296, F32, [2048])
            pT2 = [uview(63488 + 1024 * i, BF16, [512]) for i in range(4)]
            expb2 = [uview(67584 + 1024 * i, BF16, [512]) for i in range(2)]
            bias8 = [uview(69632 + 1024 * i, F32, [256]) for i in range(2)]
            rden = [uview(71680 + 2048 * i, F32, [512]) for i in range(2)]
            sqa = uview(75776, BF16, [2048])
            zraw = uview(79872, BF16, [2048])

            sc.add("dve", lambda e: e.memset(Vaug[:, :, :, 64:128], 1.0), writes=["Vaug"])
            cnt = {"ss": 0, "blk": 0, "bias": 0, "fin": 0, "zz": 0}
            ORD = ((1, 16), (4, 4), (16, 1))
            LOOKU = 2

            def mk_ev_q(pp):
                def ev(t, bank):
                    sc.add("act", (lambda e: e.activation(out=qTb[pp % 2][:, 512 * t:512 * t + 512], in_=pb[bank][:, :], func=AF.Copy)),
                           reads=[("pb", bank)], writes=[("qT", pp % 2, t)])
                return ev

            def mk_ev_k(pp):
                def ev(t, bank):
                    sc.add("dve", (lambda e: e.tensor_copy(out=kTb[pp % 2][:, 512 * t:512 * t + 512], in_=pb[bank][:, :])),
                           reads=[("pb", bank)], writes=[("kT", pp % 2, t)])
                return ev

            def ev_v(t, bank):
                sc.add("act", (lambda e: e.activation(out=vT[:, 512 * t:512 * t + 512], in_=pb[bank][:, :], func=AF.Copy)),
                       reads=[("pb", bank)], writes=[("vT", t)])

            def mk_ev_z(pp):
                def ev(t, bank):
                    tc = slice(512 * t, 512 * t + 512)
                    sc.add("act", (lambda e: e.activation(out=zraw[:, tc], in_=pb[bank][:, :], func=AF.Copy)),
                           reads=[("pb", bank)], writes=[("zraw", t)])
                    if t < 3:
                        return
                    allz = [("zraw", t_) for t_ in range(4)]
                    allo = [("opair", t_, h_) for t_ in range(4) for h_ in range(2)]
                    sc.add("act", (lambda e: e.activation(out=zraw[:, :], in_=zraw[:, :], func=AF.Silu)),
                           reads=allz, writes=allz)
                    sc.add("dve", (lambda e: e.scalar_tensor_tensor(
                        out=yTa[:, pp, :], in0=opair[:, :], scalar=gattn[:, pp:pp + 1], in1=zraw[:, :],
                        op0=ALU.mult, op1=ALU.mult)),
                        reads=allo + allz + ["gattn"], writes=[("yTa", pp, t_) for t_ in range(4)])
                    sc.add("pool", (lambda e: e.tensor_tensor(out=sqa[:, :], in0=opair[:, :], in1=opair[:, :], op=ALU.mult)),
                           reads=allo, writes=["sqa"])
                    for col in range(16):
                        sc.add("pe", (lambda e, col=col: e.matmul(
                            pb[7][:, col:col + 1], lhsT=sqa[:, 128 * col:128 * col + 128], rhs=ones_b[:, 0:1],
                            start=True, stop=True)),
                            reads=["sqa", "ones_b"], writes=[("pb", 7)])
                    if pp == 0:
                        sc.add("dve", lambda e: e.tensor_copy(out=ssq_a, in_=pb[7][:, 0:16]),
                               reads=[("pb", 7)], writes=["ssq_a"])
                    else:
                        sc.add("dve", lambda e: e.tensor_tensor(out=ssq_a, in0=pb[7][:, 0:16], in1=ssq_a, op=ALU.add),
                               reads=[("pb", 7), "ssq_a"], writes=["ssq_a"])
                return ev

            def attention_steps(p):
                qT = qTb[p % 2]
                kT = kTb[p % 2]
                allv = [("vT", t) for t in range(4)]
                for oi, (r, T) in enumerate(ORD):
                    for g in range(2):
                        for cc in range(8):
                            c = 8 * g + cc
                            res, ci = c // T, c % T
                            st0 = res + r * 128 * ci
                            cols = slice(st0, st0 + r * 127 + 1, r)
                            sc.add("pe", (lambda e, cc=cc, cols=cols: e.transpose(
                                psbf(7)[:, 128 * cc:128 * cc + 128], vT[:, cols], ident_b[:, :])),
                                reads=allv + ["ident_b"], writes=[("pb", 7)])
                        src3 = psbf(7).rearrange("p (c f) -> p c f", f=128)
                        sc.add("dve", (lambda e, oi=oi, g=g, src3=src3: e.tensor_copy(
                            out=Vaug[:, oi, 8 * g:8 * g + 8, 0:64], in_=src3[:, :, 0:64])),
                            reads=[("pb", 7)], writes=[("Vaug", oi, g, 0)])
                        sc.add("dve", (lambda e, oi=oi, g=g, src3=src3: e.tensor_copy(
                            out=Vaug[:, oi, 8 * g:8 * g + 8, 128:192], in_=src3[:, :, 64:128])),
                            reads=[("pb", 7)], writes=[("Vaug", oi, g, 1)])
                        yield

                tiles = []
                for hh in range(2):
                    for oi, (r, T) in enumerate(ORD):
                        for res in range(r):
                            for c in range(T):
                                tiles.append(dict(hh=hh, oi=oi, r=r, T=T, res=res, c=c,
                                                  first_of_branch=(res == 0 and c == 0),
                                                  last_of_head=(oi == 2 and res == r - 1 and c == T - 1)))
                blk_slots = {}

                def scores(unit, uidx):
                    tA = unit[0]
                    hh, oi, r = tA["hh"], tA["oi"], tA["r"]
                    h = 2 * p + hh
                    L = S // r
                    R = slice(64 * hh, 64 * hh + 64)
                    if tA["first_of_branch"]:
                        bslot = cnt["bias"] % 2
                        cnt["bias"] += 1
                        coef = -8.0 * slopes[h] * r
                        sc.add("dve", (lambda e: e.scalar_tensor_tensor(
                            out=bias8[bslot], in0=base_f[:, :], scalar=coef, in1=mask8[:, :], op0=ALU.mult, op1=ALU.add)),
                            reads=["base_f", "mask8"], writes=[("bias8", bslot)])
                        for hf in range(2):
                            sc.add("act", (lambda e, hf=hf: e.activation(out=expb2[bslot][:, 256 * hf:256 * hf + 256], in_=bias8[bslot],
                                                                      func=AF.Exp, scale=0.125)),
                                   reads=[("bias8", bslot)], writes=[("expb2", bslot)])
                        cnt["cur_bias", hh, oi] = bslot
                    bslot = cnt["cur_bias", hh, oi]
                    bank = 2 + cnt["ss"] % 3
                    s4 = cnt["ss"] % 4
                    cnt["ss"] += 1
                    lo, hi = None, None
                    for idx, tl in enumerate(unit):
                        T, res, c = tl["T"], tl["res"], tl["c"]
                        q0 = max(0, 128 * c - 64)
                        q1 = min(L, 128 * c + 192)
                        j0 = q0 - (128 * c - 64)
                        nq = q1 - q0
                        kcols = slice(res + r * 128 * c, res + r * (128 * c + 127) + 1, r)
                        qcols = slice(res + r * q0, res + r * (q1 - 1) + 1, r)
                        cb = 256 * idx + j0
                        tl.update(j0=j0, nq=nq, s4=s4, cbase=256 * idx)
                        if lo is None:
                            lo = cb
                        hi = cb + nq
                        ssap = pb[bank][:, cb:cb + nq]
                        tq = sorted({(res + r * q0) // 512, (res + r * (q1 - 1)) // 512} | (set(range(4)) if r > 1 else set()))
                        tk = sorted({(res + r * 128 * c) // 512, (res + r * (128 * c + 127)) // 512} | (set(range(4)) if r > 1 else set()))
                        sc.add("pe", (lambda e, ssap=ssap, kcols=kcols, qcols=qcols: e.matmul(
                            ssap, lhsT=kT[R, kcols], rhs=qT[R, qcols], start=True, stop=True)),
                            reads=[("qT", p % 2, t_) for t_ in tq] + [("kT", p % 2, t_) for t_ in tk], writes=[("pb", bank)])
                    sc.add("act", (lambda e: e.activation(out=pT2[s4][:, lo:hi], in_=pb[bank][:, lo:hi], func=AF.Exp, scale=0.125)),
                           reads=[("pb", bank)], writes=[("pT2", s4)])
                    meng = "dve" if uidx % 2 == 0 else "pool"
                    sc.add(meng, (lambda e: e.tensor_tensor(out=pT2[s4][:, lo:hi], in0=pT2[s4][:, lo:hi], in1=expb2[bslot][:, lo:hi], op=ALU.mult)),
                           reads=[("pT2", s4), ("expb2", bslot)], writes=[("pT2", s4)])

                def pv(tl):
                    hh, oi, r, T, res, c = tl["hh"], tl["oi"], tl["r"], tl["T"], tl["res"], tl["c"]
                    j0, nq, s4, cbase = tl["j0"], tl["nq"], tl["s4"], tl["cbase"]
                    L = S // r
                    oacc = oaccb[hh]
                    lhsT_v = Vaug[:, oi, res * T + c, 64 * hh:64 * hh + 128]
                    vkeys = [("Vaug", oi, (res * T + c) // 8, 0), ("Vaug", oi, (res * T + c) // 8, 1), "Vaug"]
                    for half in range(2):
                        bI = c + half
                        jlo = max(128 * half, j0)
                        jhi = min(128 * half + 128, j0 + nq)
                        if T == 1:
                            if half == 1:
                                continue
                            jlo, jhi = j0, j0 + nq
                        nb_ = jhi - jlo
                        if half == 0:
                            st_, sp_ = (c == 0), True
                        else:
                            st_, sp_ = True, (c == T - 1)
                        bkey = (hh, oi, res, bI)
                        if st_:
                            blk_slots[bkey] = 5 + cnt["blk"] % 2
                            cnt["blk"] += 1
                        bs = blk_slots[bkey]
                        oap = pb[bs][:, 0:nb_]
                        sc.add("pe", (lambda e, oap=oap, jlo=jlo, jhi=jhi, st_=st_, sp_=sp_: e.matmul(
                            oap, lhsT=lhsT_v, rhs=pT2[s4][:, cbase + jlo:cbase + jhi], start=st_, stop=sp_)),
                            reads=[("pT2", s4)] + vkeys, writes=[("pb", bs)])
                        if sp_:
                            p0 = max(0, 128 * bI - 64)
                            p1 = min(L, 128 * bI + 64)
                            if T == 1:
                                p0, p1 = 0, L
                            ocols = slice(res + r * p0, res + r * (p1 - 1) + 1, r)
                            npos = p1 - p0
                            okeys = [("oacc", hh, t_) for t_ in (range(4) if r > 1 else sorted({p0 // 512, (p1 - 1) // 512}))]
                            oap2 = pb[bs][:, 0:npos]
                            if oi == 0:
                                sc.add("dve", (lambda e, oap2=oap2, ocols=ocols: e.tensor_copy(out=oacc[:, ocols], in_=oap2)),
                                       reads=[("pb", bs)], writes=okeys)
                            else:
                                sc.add("dve", (lambda e, oap2=oap2, ocols=ocols: e.tensor_tensor(
                                    out=oacc[:, ocols], in0=oap2, in1=oacc[:, ocols], op=ALU.add)),
                                    reads=[("pb", bs)] + okeys, writes=okeys)

                def finalize(hh):
                    oacc = oaccb[hh]
                    R = slice(64 * hh, 64 * hh + 64)
                    DR = slice(64 * (1 - hh), 64 * (1 - hh) + 64)
                    for t in range(4):
                        fs = cnt["fin"] % 2
                        cnt["fin"] += 1
                        tc = slice(512 * t, 512 * t + 512)
                        sc.add("act", (lambda e, fs=fs, tc=tc: e.activation(out=rden[fs][DR, :], in_=oacc[DR, tc], func=AF.Ln)),
                               reads=[("oacc", hh, t)], writes=[("rden", fs)])
                        sc.add("act", (lambda e, fs=fs: e.activation(out=rden[fs][DR, :], in_=rden[fs][DR, :], func=AF.Exp, scale=-1.0)),
                               reads=[("rden", fs)], writes=[("rden", fs)])
                        sc.add("dve", (lambda e, fs=fs: e.tensor_copy(out=rden[fs][R, :], in_=rden[fs][DR, :])),
                               reads=[("rden", fs)], writes=[("rden", fs)])
                        sc.add("pool", (lambda e, fs=fs, tc=tc: e.tensor_tensor(
                            out=opair[R, tc], in0=oacc[R, tc], in1=rden[fs][R, :], op=ALU.mult)),
                            reads=[("oacc", hh, t), ("rden", fs)], writes=[("opair", t, hh)])

                units = [tiles[i:i + 2] for i in range(0, len(tiles), 2)]
                for un in units:
                    assert (un[0]["hh"], un[0]["oi"]) == (un[1]["hh"], un[1]["oi"])
                n = len(units)
                for i in range(min(LOOKU, n)):
                    scores(units[i], i)
                yield
                for i in range(n):
                    if i + LOOKU < n:
                        scores(units[i + LOOKU], i + LOOKU)
                    for tl in units[i]:
                        pv(tl)
                        if tl["last_of_head"]:
                            finalize(tl["hh"])
                    yield

            DBG["pair"] = 0
            merge([in_proj_units(32, mk_ev_q(0)), in_proj_units(40, mk_ev_k(0)), in_proj_units(48, ev_v)], None)
            for p in range(npairs):
                DBG["pair"] = p
                a = []
                if p >= 1:
                    a.append(in_proj_units(56 + p - 1, mk_ev_z(p - 1)))
                if p + 1 < npairs:
                    a += [in_proj_units(32 + p + 1, mk_ev_q(p + 1)), in_proj_units(40 + p + 1, mk_ev_k(p + 1)),
                          in_proj_units(48 + p + 1, ev_v)]
                merge(a, attention_steps(p))
            in_proj(56 + npairs - 1, mk_ev_z(npairs - 1))
            state["nbank"] = 3

            if stop_after == "yTa":
                dump(yTa[:, :, :], 8)
                done = True

        if not done:
            sc.barrier()
            u_sb = uview(0, BF16, [2048])
            cu = uview(4096, F32, [2050])
            acc = uview(12800, F32, [2048])
            yraw = uview(20992, F32, [2048])
            zsc = [uview(29184 + 1024 * i, BF16, [512]) for i in range(2)]
            sqc = [uview(31232 + 1024 * i, BF16, [512]) for i in range(2)]
            yTc = uview(34816, BF16, [8, 2048])
            sc.add("dve", lambda e: e.memset(cu[:, 0:1], 0.0), writes=[("cu", 0)])
            sc.add("dve", lambda e: e.memset(cu[:, 2049:2050], 0.0), writes=[("cu", 3)])
            cz = {"zz": 0}
            allcu = [("cu", t) for t in range(4)]
            allacc = [("acc", t) for t in range(4)]
            for f in range(8):
                def ev_u(t, bank):
                    sc.add("act", (lambda e, t=t, bank=bank: e.activation(out=u_sb[:, 512 * t:512 * t + 512], in_=pb[bank][:, :],
                                                                          func=AF.Copy)),
                           reads=[("pb", bank)], writes=[("u_sb", t)])

                def ev_cg(t, bank):
                    sc.add("dve", (lambda e, t=t, bank=bank: e.tensor_tensor(
                        out=cu[:, 1 + 512 * t:1 + 512 * t + 512], in0=pb[bank][:, :], in1=u_sb[:, 512 * t:512 * t + 512],
                        op=ALU.mult)),
                        reads=[("pb", bank), ("u_sb", t)], writes=[("cu", t)])

                in_proj(f, ev_u)
                in_proj(16 + f, ev_cg)
                sc.add("act", (lambda e, f=f: e.activation(out=acc[:, :], in_=cu[:, 1:2049], func=AF.Identity,
                                                           bias=convb[:, f:f + 1], scale=convw[:, f, 1:2])),
                       reads=allcu + ["convw", "convb"], writes=allacc)
                sc.add("dve", (lambda e, f=f: e.scalar_tensor_tensor(out=acc[:, :], in0=cu[:, 0:2048], scalar=convw[:, f, 0:1],
                                                                    in1=acc[:, :], op0=ALU.mult, op1=ALU.add)),
                       reads=allcu + allacc + ["convw"], writes=allacc)
                sc.add("dve", (lambda e, f=f: e.scalar_tensor_tensor(out=acc[:, :], in0=cu[:, 2:2050], scalar=convw[:, f, 2:3],
                                                                     in1=acc[:, :], op0=ALU.mult, op1=ALU.add)),
                       reads=allcu + allacc + ["convw"], writes=allacc)

                def ev_bg(t, bank):
                    sc.add("dve", (lambda e, t=t, bank=bank: e.tensor_tensor(
                        out=yraw[:, 512 * t:512 * t + 512], in0=pb[bank][:, :], in1=acc[:, 512 * t:512 * t + 512], op=ALU.mult)),
                        reads=[("pb", bank), ("acc", t)], writes=[("yraw", t)])

                in_proj(8 + f, ev_bg)

                def ev_zc(t, bank, f=f):
                    zs_ = cz["zz"] % 2
                    cz["zz"] += 1
                    tc = slice(512 * t, 512 * t + 512)
                    sc.add("act", (lambda e, zs_=zs_, bank=bank: e.activation(out=zsc[zs_], in_=pb[bank][:, :], func=AF.Silu)),
                           reads=[("pb", bank)], writes=[("zsc", zs_)])
                    sc.add("dve", (lambda e, zs_=zs_, tc=tc: e.scalar_tensor_tensor(
                        out=yTc[:, f, tc], in0=yraw[:, tc], scalar=gconv[:, f:f + 1], in1=zsc[zs_],
                        op0=ALU.mult, op1=ALU.mult)),
                        reads=[("yraw", t), ("zsc", zs_), "gconv"], writes=[("yTc", f, t)])
                    sc.add("pool", (lambda e, zs_=zs_, tc=tc: e.tensor_tensor(
                        out=sqc[zs_], in0=yraw[:, tc], in1=yraw[:, tc], op=ALU.mult)),
                        reads=[("yraw", t)], writes=[("sqc", zs_)])
                    for i4 in range(4):
                        col = 16 + 4 * t + i4
                        sc.add("pe", (lambda e, zs_=zs_, i4=i4, col=col: e.matmul(
                            pb[7][:, col:col + 1], lhsT=sqc[zs_][:, 128 * i4:128 * i4 + 128], rhs=ones_b[:, 0:1],
                            start=True, stop=True)),
                            reads=[("sqc", zs_), "ones_b"], writes=[("pb", 7)])

                in_proj(24 + f, ev_zc)
                if f == 0:
                    sc.add("dve", lambda e: e.tensor_copy(out=ssq_c, in_=pb[7][:, 16:32]), reads=[("pb", 7)], writes=["ssq_c"])
                else:
                    sc.add("dve", lambda e: e.tensor_tensor(out=ssq_c, in0=pb[7][:, 16:32], in1=ssq_c, op=ALU.add),
                           reads=[("pb", 7), "ssq_c"], writes=["ssq_c"])

            if stop_after == "yTc":
                sc.barrier()
                sc.add("sp", lambda e: e.dma_start(out=dbg_d[:, 0:8, :], in_=yTc[:, :, :]), dma_key="out")
                sc.add("sp", lambda e: e.dma_start(out=dbg_d[:, 8:16, :], in_=yTa[:, :, :]), dma_key="out")
                done = True

        if not done:
            sc.barrier()
            xt = [uview(8192 * i, F32, [2048]) for i in range(2)]
            yv = [uview(16384 + 8192 * i, F32, [2048]) for i in range(2)]
            junk = uview(32768, BF16, [512])
            gg = wsl[:, :, :, :].rearrange("p a b c -> p (a b c)").bitcast(F32)[:, 0:2048]
            wout = hT
            for src, dst, nm in ((ssq_a, ra, "ra"), (ssq_c, rc, "rc")):
                sc.add("dve", (lambda e, src=src, dst=dst: e.tensor_scalar(out=dst, in0=src, scalar1=1.0 / 1024.0, scalar2=EPS,
                                                                           op0=ALU.mult, op1=ALU.add)), writes=[nm])
                sc.add("act", (lambda e, dst=dst: e.activation(out=dst, in_=dst, func=AF.Sqrt)), reads=[nm], writes=[nm])
                sc.add("dve", (lambda e, dst=dst: e.reciprocal(out=dst, in_=dst)), reads=[nm], writes=[nm])
            for k in range(KC):
                sc.add("pool", (lambda e, k=k: e.dma_start(out=wout[:, k, :], in_=wout_d[128 * k:128 * k + 128, :])),
                       writes=[("wout", k)], dma_key=("wout", k))
            sc.add("sp", lambda e: e.dma_start(out=gg, in_=gpost_d[:, :].partition_broadcast(128)), writes=["gg"], dma_key="gg")
            dg = [yv[1][:, 0:128], yv[1][:, 128:256]]
            for k in range(KC):
                d2 = k % 2
                sc.add("dve", (lambda e, k=k, d2=d2: e.tensor_scalar(out=dg[d2], in0=ident_f[:, :], scalar1=gate_lay[:, k:k + 1],
                                                                    scalar2=None, op0=ALU.mult)),
                       reads=["ident_f", "gate_lay"], writes=[("dg", d2)])
                sc.add("pe", (lambda e, k=k, d2=d2: e.matmul(pb[k // 4][:, 128 * (k % 4):128 * (k % 4) + 128], lhsT=ones_f[:, :],
                                                            rhs=dg[d2], start=True, stop=True)),
                       reads=[("dg", d2), "ones_f"], writes=[("pb", k // 4)])
                if k % 4 == 3:
                    g4 = k // 4
                    sc.add("dve", (lambda e, g4=g4: e.tensor_tensor(out=gg[:, 512 * g4:512 * g4 + 512], in0=pb[g4][:, :],
                                                                   in1=gg[:, 512 * g4:512 * g4 + 512], op=ALU.mult)),
                           reads=[("pb", g4), "gg"], writes=["gg"])
            for i in range(16):
                b2 = i % 2
                tcol = slice(128 * i, 128 * i + 128)
                sc.add("sp", (lambda e, i=i, b2=b2: e.dma_start(out=xt[b2], in_=x_d[128 * i:128 * i + 128, :])),
                       writes=[("xt", b2)], dma_key=("xt", b2))
                for hf in range(2):
                    for k in range(KC):
                        grp = 0 if k < 8 else 1
                        lhsT = yTc[:, k, tcol] if k < 8 else yTa[:, k - 8, tcol]
                        for n in range(2):
                            bank = 4 * hf + 2 * grp + n
                            dc = slice(1024 * hf + 512 * n, 1024 * hf + 512 * n + 512)
                            sc.add("pe", (lambda e, bank=bank, lhsT=lhsT, k=k, dc=dc: e.matmul(
                                pb[bank][:, :], lhsT=lhsT, rhs=wout[:, k, dc], start=(k % 8 == 0), stop=(k % 8 == 7))),
                                reads=[("wout", k)], writes=[("pb", bank)])
                    for n in range(2):
                        dc = slice(1024 * hf + 512 * n, 1024 * hf + 512 * n + 512)
                        q4 = 2 * hf + n
                        extra = [("dg", 0), ("dg", 1)] if b2 == 1 else []
                        sc.add("act", (lambda e, hf=hf, n=n, dc=dc, b2=b2, i=i: e.activation(
                            out=yv[b2][:, dc], in_=pb[4 * hf + n][:, :], func=AF.Identity, scale=rc[:, i:i + 1])),
                            reads=[("pb", 4 * hf + n), "rc"] + extra, writes=[("yv", b2, q4)] + extra)
                        sc.add("dve", (lambda e, hf=hf, n=n, dc=dc, b2=b2, i=i: e.scalar_tensor_tensor(
                            out=yv[b2][:, dc], in0=pb[4 * hf + 2 + n][:, :], scalar=ra[:, i:i + 1], in1=yv[b2][:, dc],
                            op0=ALU.mult, op1=ALU.add)),
                            reads=[("pb", 4 * hf + 2 + n), "ra", ("yv", b2, q4)], writes=[("yv", b2, q4)])
                        sc.add("act", (lambda e, dc=dc, b2=b2, i=i, q4=q4: e.activation(
                            out=junk, in_=yv[b2][:, dc], func=AF.Square, accum_out=ssq_p[:, 4 * i + q4:4 * i + q4 + 1])),
                            reads=[("yv", b2, q4)], writes=["junk", ("ssq_p", i, q4)])
                allyv = [("yv", b2, q4) for q4 in range(4)]
                sc.add("dve", (lambda e, i=i: e.reduce_sum(out=ssq_p1[:, i:i + 1], in_=ssq_p[:, 4 * i:4 * i + 4], axis=AX.X)),
                       reads=[("ssq_p", i, q4) for q4 in range(4)], writes=[("rp", i)])
                sc.add("dve", (lambda e, i=i: e.tensor_scalar(out=rp[:, i:i + 1], in0=ssq_p1[:, i:i + 1], scalar1=1.0 / D,
                                                              scalar2=EPS, op0=ALU.mult, op1=ALU.add)),
                       reads=[("rp", i)], writes=[("rp", i)])
                sc.add("act", (lambda e, i=i: e.activation(out=rp[:, i:i + 1], in_=rp[:, i:i + 1], func=AF.Sqrt)),
                       reads=[("rp", i)], writes=[("rp", i)])
                sc.add("dve", (lambda e, i=i: e.reciprocal(out=rp[:, i:i + 1], in_=rp[:, i:i + 1])),
                       reads=[("rp", i)], writes=[("rp", i)])
                sc.add("dve", (lambda e, i=i, b2=b2: e.scalar_tensor_tensor(
                    out=yv[b2], in0=yv[b2], scalar=rp[:, i:i + 1], in1=gg, op0=ALU.mult, op1=ALU.mult)),
                    reads=allyv + [("rp", i), "gg"], writes=allyv)
                sc.add("pool", (lambda e, b2=b2: e.tensor_tensor(out=yv[b2], in0=yv[b2], in1=xt[b2], op=ALU.add)),
                       reads=allyv + [("xt", b2)], writes=allyv)
                sc.add("act", (lambda e, i=i, b2=b2: e.dma_start(out=out_d[128 * i:128 * i + 128, :], in_=yv[b2])),
                       reads=allyv, writes=[("outdram", i)], dma_key=("out", b2))

        sc.barrier()

        sc.finalize()
        dma_keys = list(sc.dma_cnt.keys())
        eng_sems = {e: es.enter_context(nc.semaphore(f"sem_{e}")) for e in Sched.ENGS}
        dma_sems = {k: es.enter_context(nc.semaphore(f"dsem_{i}")) for i, k in enumerate(dma_keys)}
        block = es.enter_context(nc.Block())

        @block.tensor
        def _(e):
            sc.emit("pe", e, eng_sems, dma_sems)

        @block.scalar
        def _(e):
            sc.emit("act", e, eng_sems, dma_sems)

        @block.vector
        def _(e):
            sc.emit("dve", e, eng_sems, dma_sems)

        @block.gpsimd
        def _(e):
            sc.emit("pool", e, eng_sems, dma_sems)

        @block.sync
        def _(e):
            sc.emit("sp", e, eng_sems, dma_sems)

    return nc


def _lay(v, nchunk):
    return np.ascontiguousarray(np.asarray(v, np.float32).reshape(nchunk, 128).T)


def make_in_maps(x, c, w_ada, b_ada, g_pre, w_in, conv_w, conv_b, g_conv, g_attn, w_out, g_post):
    x = np.asarray(x, np.float32)
    c = np.asarray(c, np.float32)
    w_ada0 = np.ascontiguousarray(np.asarray(w_ada, np.float32)[0])
    w_in0 = np.ascontiguousarray(np.asarray(w_in, np.float32)[0])
    w_out0 = np.ascontiguousarray(np.asarray(w_out, np.float32)[0])
    cw = np.asarray(conv_w, np.float32)[0]
    convw_lay = np.ascontiguousarray(cw.reshape(3, 8, 128).transpose(2, 1, 0))
    shared = {
        "w_ada": w_ada0,
        "b_row": np.ascontiguousarray(np.asarray(b_ada, np.float32)[0].reshape(1, NMOD)),
        "gpre_lay": _lay(np.asarray(g_pre)[0], 16),
        "w_in": w_in0,
        "convw_lay": convw_lay,
        "convb_lay": _lay(np.asarray(conv_b)[0], 8),
        "gconv_lay": _lay(np.asarray(g_conv)[0], 8),
        "gattn_lay": _lay(np.asarray(g_attn)[0], 8),
        "w_out": w_out0,
        "gpost_row": np.ascontiguousarray(np.asarray(g_post, np.float32)[0].reshape(1, D)),
    }
    maps = []
    for b in range(N_CORES):
        m = dict(shared)
        m["x"] = np.ascontiguousarray(x[b])
        m["c_lay"] = _lay(c[b], 16)
        maps.append(m)
    return maps


_NC_CACHE = {}


def kernel(x, c, w_ada, b_ada, g_pre, w_in, conv_w, conv_b, g_conv, g_attn, w_out, g_post):
    if "nc" not in _NC_CACHE:
        _NC_CACHE["nc"] = build_nc()
    nc = _NC_CACHE["nc"]
    in_maps = make_in_maps(x, c, w_ada, b_ada, g_pre, w_in, conv_w, conv_b, g_conv, g_attn, w_out, g_post)
    res = run_bass_kernel_spmd(nc, in_maps, core_ids=list(range(N_CORES)))
    out = np.stack([np.asarray(r["out"], np.float32) for r in res.results], axis=0)
    return out
```

```python
import numpy as np
import concourse.bass as bass
import concourse.mybir as mybir
from concourse.bass_utils import run_bass_kernel_spmd

F32 = mybir.dt.float32
BF16 = mybir.dt.bfloat16
AF = mybir.ActivationFunctionType
ALU = mybir.AluOpType
AX = mybir.AxisListType

D = 2048
S = 2048
NPROJ = 8192
NMOD = 6144
KC = 16
EPS = 1e-6
NEG8 = -8.0 * 30000.0
N_CORES = 8
DBG = {"pair": -1}


class _Op:
    __slots__ = ("eng", "fn", "deps", "dma_key", "dma_n", "signal", "sig")

    def __init__(self, eng, fn):
        self.eng = eng
        self.fn = fn
        self.deps = []
        self.dma_key = None
        self.dma_n = 0
        self.signal = False
        self.sig = 0


class _DmaTok:
    __slots__ = ("key", "n")

    def __init__(self, key, n):
        self.key = key
        self.n = n


class Sched:
    ENGS = ("pe", "act", "dve", "pool", "sp")

    def __init__(self):
        self.streams = {e: [] for e in self.ENGS}
        self.res_w = {}
        self.res_r = {}
        self.dma_cnt = {}
        self.last = {}

    @staticmethod
    def _grp(tok):
        return ("dma", tok.key) if isinstance(tok, _DmaTok) else tok.eng

    def add(self, eng, fn, reads=(), writes=(), dma_key=None, extra_deps=()):
        op = _Op(eng, fn)
        deps = {}
        for r in reads:
            for t in self.res_w.get(r, {}).values():
                deps[id(t)] = t
        for w in writes:
            for t in self.res_w.get(w, {}).values():
                deps[id(t)] = t
            for t in self.res_r.get(w, {}).values():
                deps[id(t)] = t
        for t in extra_deps:
            deps[id(t)] = t
        if dma_key is not None:
            n = self.dma_cnt.get(dma_key, 0) + 1
            self.dma_cnt[dma_key] = n
            op.dma_key = dma_key
            op.dma_n = n
            tok = _DmaTok(dma_key, n)
        else:
            tok = op
        g = self._grp(tok)
        for r in reads:
            self.res_r.setdefault(r, {})[g] = tok
        for w in writes:
            self.res_w[w] = {g: tok}
            self.res_r[w] = {}
        deps.pop(id(op), None)
        op.deps = list(deps.values())
        self.streams[eng].append(op)
        if fn is not None:
            self.last[g] = tok
        return tok

    def barrier(self):
        toks = list(self.last.values())
        for e in self.ENGS:
            self.add(e, None, extra_deps=toks)

    def finalize(self):
        for e in self.ENGS:
            for op in self.streams[e]:
                for d in op.deps:
                    if isinstance(d, _Op):
                        if d.eng == "pe" and op.eng == "pe":
                            continue
                        d.signal = True
        for e in self.ENGS:
            c = 0
            for op in self.streams[e]:
                if op.signal:
                    c += 1
                    op.sig = c

    def emit(self, eng_name, e, eng_sems, dma_sems):
        waited = {}
        for op in self.streams[eng_name]:
            need = {}
            for d in op.deps:
                if isinstance(d, _Op):
                    if d.eng == "pe" and eng_name == "pe":
                        continue
                    g, v = d.eng, d.sig
                else:
                    g = ("dma", d.key)
                    v = 16 * (self.dma_cnt[d.key] if d.key == "const" else d.n)
                if v > need.get(g, 0):
                    need[g] = v
            for g, v in need.items():
                if waited.get(g, 0) >= v:
                    continue
                waited[g] = v
                sem = dma_sems[g[1]] if isinstance(g, tuple) else eng_sems[g]
                e.wait_ge(sem, v)
            if op.fn is None:
                continue
            ins = op.fn(e)
            if op.dma_key is not None:
                ins.then_inc(dma_sems[op.dma_key], 16)
            elif op.signal:
                ins.then_inc(eng_sems[eng_name], 1)


def _slopes():
    return [2.0 ** (-8.0 * (h + 1) / 16.0) for h in range(16)]


def build_nc(stop_after=None, npairs=8):
    nc = bass.Bass("TRN2", target_bir_lowering=False)
    x_d = nc.dram_tensor("x", [S, D], F32, kind="ExternalInput").ap()
    c_d = nc.dram_tensor("c_lay", [128, KC], F32, kind="ExternalInput").ap()
    wada_d = nc.dram_tensor("w_ada", [D, NMOD], F32, kind="ExternalInput").ap()
    bada_d = nc.dram_tensor("b_row", [1, NMOD], F32, kind="ExternalInput").ap()
    gpre_d = nc.dram_tensor("gpre_lay", [128, KC], F32, kind="ExternalInput").ap()
    win_d = nc.dram_tensor("w_in", [D, NPROJ], F32, kind="ExternalInput").ap()
    convw_d = nc.dram_tensor("convw_lay", [128, 8, 3], F32, kind="ExternalInput").ap()
    convb_d = nc.dram_tensor("convb_lay", [128, 8], F32, kind="ExternalInput").ap()
    gconv_d = nc.dram_tensor("gconv_lay", [128, 8], F32, kind="ExternalInput").ap()
    gattn_d = nc.dram_tensor("gattn_lay", [128, 8], F32, kind="ExternalInput").ap()
    wout_d = nc.dram_tensor("w_out", [D, D], F32, kind="ExternalInput").ap()
    gpost_d = nc.dram_tensor("gpost_row", [1, D], F32, kind="ExternalInput").ap()
    out_d = nc.dram_tensor("out", [S, D], F32, kind="ExternalOutput").ap()
    dbg_d = None
    if stop_after is not None:
        dbg_d = nc.dram_tensor("dbg", [128, 16, 2048], BF16, kind="ExternalOutput").ap()

    sc = Sched()
    UF = 20992

    import contextlib
    with contextlib.ExitStack() as es:
        def sb(name, shape, dt):
            return es.enter_context(nc.sbuf_tensor(name, shape, dt))

        hT = sb("hT", [128, KC, 2048], BF16)
        yTa = sb("yTa", [128, 8, 2048], BF16)
        U = sb("U", [128, UF], F32)
        wsl = sb("wsl", [128, 4, KC, 128], BF16)
        ident_f = sb("ident_f", [128, 128], F32)
        ident_b = sb("ident_b", [128, 128], BF16)
        ones_f = sb("ones_f", [128, 128], F32)
        ones_b = sb("ones_b", [128, 128], BF16)
        base_f = sb("base_f", [128, 256], F32)
        mask8 = sb("mask8", [128, 256], F32)
        sm = sb("sm", [128, 512], F32)
        cab = sb("cab", [128, KC], BF16)
        pb = [es.enter_context(nc.psum_tensor(f"pb{i}", [128, 512], F32)) for i in range(8)]

        c_sb = sm[:, 0:16]
        gpre = sm[:, 16:32]
        a_sb = sm[:, 32:48]
        s_sb = sm[:, 48:64]
        gate_lay = sm[:, 64:80]
        convw = sm[:, 80:104].rearrange("p (f t) -> p f t", t=3)
        convb = sm[:, 104:112]
        gconv = sm[:, 112:120]
        gattn = sm[:, 120:128]
        ssq_x = sm[:, 128:144]
        rstd_x = sm[:, 144:160]
        ssq_a = sm[:, 160:176]
        ssq_c = sm[:, 176:192]
        ra = sm[:, 192:208]
        rc = sm[:, 208:224]
        ssq_p = sm[:, 224:288]
        ssq_p1 = sm[:, 288:304]
        rp = sm[:, 304:320]
        tmp16 = sm[:, 320:336]

        def uview(off, dt, shape):
            n = 1
            for s_ in shape:
                n *= s_
            nb = n * (4 if dt == F32 else 2)
            assert off % 4 == 0 and nb % 4 == 0 and off + nb <= UF * 4, (off, nb)
            a = U[:, off // 4:(off + nb) // 4]
            if dt != F32:
                a = a.bitcast(dt)
            if len(shape) == 2:
                return a.rearrange("p (a b) -> p a b", b=shape[1])
            if len(shape) == 3:
                return a.rearrange("p (a b c) -> p a b c", b=shape[1], c=shape[2])
            return a

        def psbf(bank):
            return pb[bank][:, :].bitcast(BF16)

        sc.add("pool", lambda e: e.iota(base_f[:, :], [[1, 256]], base=-64, channel_multiplier=-1, allow_small_or_imprecise_dtypes=True),
               writes=["base_f"])
        sc.add("dve", lambda e: e.tensor_single_scalar(out=ident_f[:, :], in_=base_f[:, 64:192], scalar=0.0, op=ALU.is_equal),
               reads=["base_f"], writes=["ident_f"])
        sc.add("dve", lambda e: e.tensor_copy(out=ident_b[:, :], in_=ident_f[:, :]), reads=["ident_f"], writes=["ident_b"])
        sc.add("dve", lambda e: e.memset(ones_f[:, :], 1.0), writes=["ones_f"])
        sc.add("dve", lambda e: e.memset(ones_b[:, :], 1.0), writes=["ones_b"])
        sc.add("dve", lambda e: e.tensor_scalar(out=mask8[:, :], in0=base_f[:, :], scalar1=-1.0, scalar2=None, op0=ALU.mult),
               reads=["ident_f", "base_f"], writes=["mask8"])
        sc.add("dve", lambda e: e.tensor_tensor(out=base_f[:, :], in0=base_f[:, :], in1=mask8[:, :], op=ALU.max),
               reads=["mask8"], writes=["base_f"])
        sc.add("dve", lambda e: e.tensor_scalar(out=mask8[:, :], in0=base_f[:, :], scalar1=64.5, scalar2=NEG8,
                                                op0=ALU.is_gt, op1=ALU.mult),
               reads=["base_f"], writes=["mask8"])

        def cdma(dst, src, key):
            sc.add("sp", lambda e: e.dma_start(out=dst, in_=src), writes=[key], dma_key="const")

        cdma(c_sb, c_d[:, :], "c_sb")
        cdma(gpre, gpre_d[:, :], "gpre")
        cdma(convw, convw_d[:, :, :], "convw")
        cdma(convb, convb_d[:, :], "convb")
        cdma(gconv, gconv_d[:, :], "gconv")
        cdma(gattn, gattn_d[:, :], "gattn")

        def dump_small():
            sc.barrier()
            sc.add("sp", lambda e: e.dma_start(out=dbg_d[:, 0, 0:256], in_=sm[:, 0:256].bitcast(BF16)[:, 0:256]), dma_key="out")
            sc.add("sp", lambda e: e.dma_start(out=out_d[0:128, 0:256], in_=base_f[:, :]), dma_key="out")
            sc.add("sp", lambda e: e.dma_start(out=out_d[128:256, 0:256], in_=mask8[:, :]), dma_key="out")
            sc.add("sp", lambda e: e.dma_start(out=out_d[256:384, 0:128], in_=ident_f[:, :]), dma_key="out")
            sc.add("sp", lambda e: e.dma_start(out=out_d[384:512, 0:512], in_=sm[:, :]), dma_key="out")
        done = False
        if stop_after == "const":
            dump_small()
            done = True
        mod_row = U[0:1, 0:NMOD]
        b_row = U[0:1, NMOD:2 * NMOD]
        xs = [uview(2 * NMOD * 4 + 8192 * i, F32, [2048]) for i in range(2)]
        xn = [yTa[:, i, :] for i in range(2)]
        if not done:
          cdma(b_row, bada_d[:, :], "b_row")
          sc.add("act", lambda e: e.activation(out=cab[:, :], in_=c_sb, func=AF.Silu), reads=["c_sb"], writes=["cab"])

          def wa_slot(s_):
              return hT[:, 4 * s_:4 * s_ + 4, :].rearrange("p a (b n) -> p (a b) n", n=512)

          for s_ in range(12):
              slot = s_ % 4
              keys = [("hT", k) for k in range(4 * slot, 4 * slot + 4)]
              dst = wa_slot(slot)
              src = wada_d[:, 512 * s_:512 * s_ + 512].rearrange("(k p) n -> p k n", p=128)
              sc.add("pool", (lambda e, dst=dst, src=src: e.dma_start(out=dst, in_=src)),
                     writes=keys, dma_key=("wa", slot))
              bank = s_ % 2
              for k in range(KC):
                  sc.add("pe", (lambda e, bank=bank, k=k, dst=dst: e.matmul(
                      pb[bank][0:1, :], lhsT=cab[:, k:k + 1], rhs=dst[:, k, :], start=(k == 0), stop=(k == KC - 1))),
                      reads=keys + ["cab"], writes=[("pb", bank)])
              sc.add("dve", (lambda e, bank=bank, s_=s_: e.tensor_tensor(
                  out=mod_row[:, 512 * s_:512 * s_ + 512], in0=pb[bank][0:1, :],
                  in1=b_row[:, 512 * s_:512 * s_ + 512], op=ALU.add)),
                  reads=[("pb", bank), "b_row"], writes=[("mod", s_)])
          for j in range(48):
              sc.add("pe", (lambda e, j=j: e.matmul(pb[2][:, j:j + 1], lhsT=mod_row[:, 128 * j:128 * j + 128],
                                                    rhs=ones_f[0:1, 0:1], start=True, stop=True)),
                     reads=[("mod", j // 4), "ones_f"], writes=[("pb", 2)])
          sc.add("dve", lambda e: e.tensor_copy(out=s_sb, in_=pb[2][:, 0:16]), reads=[("pb", 2)], writes=["s_sb"])
          sc.add("dve", lambda e: e.scalar_tensor_tensor(out=a_sb, in0=pb[2][:, 16:32], scalar=1.0, in1=gpre,
                                                         op0=ALU.add, op1=ALU.mult),
                 reads=[("pb", 2), "gpre"], writes=["a_sb"])
          sc.add("dve", lambda e: e.tensor_copy(out=gate_lay, in_=pb[2][:, 32:48]), reads=[("pb", 2)], writes=["gate_lay"])

          xs = [uview(2 * NMOD * 4 + 8192 * i, F32, [2048]) for i in range(2)]
          xn = [yTa[:, i, :] for i in range(2)]
          for i in range(0 if stop_after == "mod" else 16):
              b2 = i % 2
              sc.add("sp", (lambda e, i=i, b2=b2: e.dma_start(out=xs[b2], in_=x_d[128 * i:128 * i + 128, :])),
                     writes=[("xs", b2)], dma_key=("xs", b2))
              sc.add("act", (lambda e, i=i, b2=b2: e.activation(out=xn[b2], in_=xs[b2], func=AF.Square,
                                                                accum_out=ssq_x[:, i:i + 1])),
                     reads=[("xs", b2)], writes=[("xn", b2), ("ssq_x", i)])
              sc.add("dve", (lambda e, i=i: e.tensor_scalar(out=rstd_x[:, i:i + 1], in0=ssq_x[:, i:i + 1], scalar1=1.0 / D,
                                                            scalar2=EPS, op0=ALU.mult, op1=ALU.add)),
                     reads=[("ssq_x", i)], writes=[("rstd_x", i)])
              sc.add("act", (lambda e, i=i: e.activation(out=rstd_x[:, i:i + 1], in_=rstd_x[:, i:i + 1], func=AF.Sqrt)),
                     reads=[("rstd_x", i)], writes=[("rstd_x", i)])
              sc.add("dve", (lambda e, i=i: e.reciprocal(out=rstd_x[:, i:i + 1], in_=rstd_x[:, i:i + 1])),
                     reads=[("rstd_x", i)], writes=[("rstd_x", i)])
              sc.add("dve", (lambda e, i=i, b2=b2: e.tensor_scalar(out=xn[b2], in0=xs[b2], scalar1=rstd_x[:, i:i + 1],
                                                                   scalar2=None, op0=ALU.mult)),
                     reads=[("xs", b2), ("rstd_x", i)], writes=[("xn", b2)])
              for g in range(2):
                  bank = 4 + 2 * b2 + g
                  for kk in range(8):
                      k = 8 * g + kk
                      sc.add("pe", (lambda e, bank=bank, kk=kk, k=k, b2=b2: e.transpose(
                          psbf(bank)[:, 128 * kk:128 * kk + 128], xn[b2][:, 128 * k:128 * k + 128], ident_b[:, :])),
                          reads=[("xn", b2), "ident_b"], writes=[("pb", bank)])
                  for kk in range(8):
                      k = 8 * g + kk
                      if g == 0:
                          sc.add("act", (lambda e, bank=bank, kk=kk, k=k, i=i: e.activation(
                              out=hT[:, k, 128 * i:128 * i + 128], in_=psbf(bank)[:, 128 * kk:128 * kk + 128],
                              func=AF.Identity, bias=s_sb[:, k:k + 1], scale=a_sb[:, k:k + 1])),
                              reads=[("pb", bank), "a_sb", "s_sb"], writes=[("hT", k)])
                      else:
                          sc.add("dve", (lambda e, bank=bank, kk=kk, k=k, i=i: e.tensor_scalar(
                              out=hT[:, k, 128 * i:128 * i + 128], in0=psbf(bank)[:, 128 * kk:128 * kk + 128],
                              scalar1=a_sb[:, k:k + 1], scalar2=s_sb[:, k:k + 1], op0=ALU.mult, op1=ALU.add)),
                              reads=[("pb", bank), "a_sb", "s_sb"], writes=[("hT", k)])

        def dump(src_ap, nk):
            sc.barrier()
            sc.add("sp", lambda e: e.dma_start(out=dbg_d[:, 0:nk, :], in_=src_ap), dma_key="out")
            sc.add("sp", lambda e: e.dma_start(out=out_d[0:128, :], in_=xs[0]), dma_key="out")

        if stop_after == "mod" and not done:
            dump_small()
            done = True
        if stop_after == "hT" and not done:
            dump(hT[:, :, :], 16)
            done = True

        state = {"wslot": 0, "bank": 0, "nbank": 3}

        order = [32, 40, 48]
        for p_ in range(npairs):
            if p_ >= 1:
                order.append(56 + p_ - 1)
            if p_ + 1 < npairs:
                order += [32 + p_ + 1, 40 + p_ + 1, 48 + p_ + 1]
        order.append(56 + npairs - 1)
        for f_ in range(8):
            order += [f_, 16 + f_, 8 + f_, 24 + f_]
        state["issued"] = 0

        def issue_w(upto):
            while state["issued"] < min(upto, len(order)):
                n_ = state["issued"]
                j_ = order[n_]
                slot_ = n_ % 4
                src = win_d[:, 128 * j_:128 * j_ + 128].rearrange("(k p) n -> p k n", p=128)
                sc.add("pool", (lambda e, slot_=slot_, src=src: e.dma_start(out=wsl[:, slot_], in_=src)),
                       writes=[("wsl", slot_)], dma_key=("wsl", slot_))
                state["issued"] += 1

        def in_proj_units(j, evac):
            n_ = state["wslot"]
            assert order[n_] == j, (n_, j, order[n_])
            slot = n_ % 4
            state["wslot"] += 1
            issue_w(n_ + 4)
            for t in range(4):
                bank = state["bank"] % state["nbank"]
                state["bank"] += 1
                for k in range(KC):
                    sc.add("pe", (lambda e, bank=bank, k=k, t=t, slot=slot: e.matmul(
                        pb[bank][:, :], lhsT=wsl[:, slot, k, :], rhs=hT[:, k, 512 * t:512 * t + 512],
                        start=(k == 0), stop=(k == KC - 1))),
                        reads=[("wsl", slot), ("hT", k)], writes=[("pb", bank)])
                    if k == KC - 1:
                        evac(t, bank)
                    if k % 2 == 1:
                        yield

        def in_proj(j, evac):
            for _ in in_proj_units(j, evac):
                pass

        def merge(a_gens, b_gen):
            import itertools
            A = itertools.chain(*a_gens)
            B = b_gen if b_gen is not None else iter(())
            a_done = b_done = False
            while not (a_done and b_done):
                if not b_done:
                    try:
                        next(B)
                    except StopIteration:
                        b_done = True
                for _ in range(8):
                    if not a_done:
                        try:
                            next(A)
                        except StopIteration:
                            a_done = True

        slopes = _slopes()

        if not done:
            sc.barrier()
            state["nbank"] = 2
            state["bank"] = 0
            qTb = [uview(0, BF16, [2048]), uview(4096, BF16, [2048])]
            kTb = [uview(8192, BF16, [2048]), uview(12288, BF16, [2048])]
            vT = uview(16384, BF16, [2048])
            Vaug = uview(20480, BF16, [3, 16, 192])
            oaccb = [uview(38912, F32, [2048]), uview(47104, F32, [2048])]
            opair = uview(55296, F32, [2048])
            pT2 = [uview(63488 + 1024 * i, BF16, [512]) for i in range(4)]
            expb2 = [uview(67584 + 1024 * i, BF16, [512]) for i in range(2)]
            bias8 = [uview(69632 + 1024 * i, F32, [256]) for i in range(2)]
            rden = [uview(71680 + 2048 * i, F32, [512]) for i in range(2)]
            sqa = uview(75776, BF16, [2048])
            zraw = uview(79872, BF16, [2048])

            sc.add("dve", lambda e: e.memset(Vaug[:, :, :, 64:128], 1.0), writes=["Vaug"])
            cnt = {"ss": 0, "blk": 0, "bias": 0, "fin": 0, "zz": 0}
            ORD = ((1, 16), (4, 4), (16, 1))
            LOOKU = 3
            BURST = 3

            def mk_ev_q(pp):
                def ev(t, bank):
                    sc.add("act", (lambda e: e.activation(out=qTb[pp % 2][:, 512 * t:512 * t + 512], in_=pb[bank][:, :], func=AF.Copy)),
                           reads=[("pb", bank)], writes=[("qT", pp % 2, t)])
                return ev

            def mk_ev_k(pp):
                def ev(t, bank):
                    sc.add("dve", (lambda e: e.tensor_copy(out=kTb[pp % 2][:, 512 * t:512 * t + 512], in_=pb[bank][:, :])),
                           reads=[("pb", bank)], writes=[("kT", pp % 2, t)])
                return ev

            def ev_v(t, bank):
                sc.add("act", (lambda e: e.activation(out=vT[:, 512 * t:512 * t + 512], in_=pb[bank][:, :], func=AF.Copy)),
                       reads=[("pb", bank)], writes=[("vT", t)])

            def mk_ev_z(pp):
                def ev(t, bank):
                    tc = slice(512 * t, 512 * t + 512)
                    sc.add("act", (lambda e: e.activation(out=zraw[:, tc], in_=pb[bank][:, :], func=AF.Copy)),
                           reads=[("pb", bank)], writes=[("zraw", t)])
                    if t < 3:
                        return
                    allz = [("zraw", t_) for t_ in range(4)]
                    allo = [("opair", t_, h_) for t_ in range(4) for h_ in range(2)]
                    sc.add("act", (lambda e: e.activation(out=zraw[:, :], in_=zraw[:, :], func=AF.Silu)),
                           reads=allz, writes=allz)
                    sc.add("dve", (lambda e: e.scalar_tensor_tensor(
                        out=yTa[:, pp, :], in0=opair[:, :], scalar=gattn[:, pp:pp + 1], in1=zraw[:, :],
                        op0=ALU.mult, op1=ALU.mult)),
                        reads=allo + allz + ["gattn"], writes=[("yTa", pp, t_) for t_ in range(4)])
                    sc.add("pool", (lambda e: e.tensor_tensor(out=sqa[:, :], in0=opair[:, :], in1=opair[:, :], op=ALU.mult)),
                           reads=allo, writes=["sqa"])
                    for col in range(16):
                        sc.add("pe", (lambda e, col=col: e.matmul(
                            pb[7][:, col:col + 1], lhsT=sqa[:, 128 * col:128 * col + 128], rhs=ones_b[:, 0:1],
                            start=True, stop=True)),
                            reads=["sqa", "ones_b"], writes=[("pb", 7)])
                    if pp == 0:
                        sc.add("dve", lambda e: e.tensor_copy(out=ssq_a, in_=pb[7][:, 0:16]),
                               reads=[("pb", 7)], writes=["ssq_a"])
                    else:
                        sc.add("dve", lambda e: e.tensor_tensor(out=ssq_a, in0=pb[7][:, 0:16], in1=ssq_a, op=ALU.add),
                               reads=[("pb", 7), "ssq_a"], writes=["ssq_a"])
                return ev

            def attention_steps(p):
                qT = qTb[p % 2]
                kT = kTb[p % 2]
                allv = [("vT", t) for t in range(4)]
                for oi, (r, T) in enumerate(ORD):
                    for g in range(2):
                        for cc in range(8):
                            c = 8 * g + cc
                            res, ci = c // T, c % T
                            st0 = res + r * 128 * ci
                            cols = slice(st0, st0 + r * 127 + 1, r)
                            sc.add("pe", (lambda e, cc=cc, cols=cols: e.transpose(
                                psbf(7)[:, 128 * cc:128 * cc + 128], vT[:, cols], ident_b[:, :])),
                                reads=allv + ["ident_b"], writes=[("pb", 7)])
                        src3 = psbf(7).rearrange("p (c f) -> p c f", f=128)
                        sc.add("dve", (lambda e, oi=oi, g=g, src3=src3: e.tensor_copy(
                            out=Vaug[:, oi, 8 * g:8 * g + 8, 0:64], in_=src3[:, :, 0:64])),
                            reads=[("pb", 7)], writes=[("Vaug", oi, g, 0)])
                        sc.add("dve", (lambda e, oi=oi, g=g, src3=src3: e.tensor_copy(
                            out=Vaug[:, oi, 8 * g:8 * g + 8, 128:192], in_=src3[:, :, 64:128])),
                            reads=[("pb", 7)], writes=[("Vaug", oi, g, 1)])
                        if (2 * oi + g) % 3 == 2:
                            yield

                tiles = []
                for hh in range(2):
                    for oi, (r, T) in enumerate(ORD):
                        for res in range(r):
                            for c in range(T):
                                tiles.append(dict(hh=hh, oi=oi, r=r, T=T, res=res, c=c,
                                                  first_of_branch=(res == 0 and c == 0),
                                                  last_of_head=(oi == 2 and res == r - 1 and c == T - 1)))
                grp_bank = {}
                grp_done = {}

                def scores(unit, uidx):
                    tA = unit[0]
                    hh, oi, r = tA["hh"], tA["oi"], tA["r"]
                    h = 2 * p + hh
                    L = S // r
                    R = slice(64 * hh, 64 * hh + 64)
                    if tA["first_of_branch"]:
                        bslot = cnt["bias"] % 2
                        cnt["bias"] += 1
                        coef = -8.0 * slopes[h] * r
                        sc.add("dve", (lambda e: e.scalar_tensor_tensor(
                            out=bias8[bslot], in0=base_f[:, :], scalar=coef, in1=mask8[:, :], op0=ALU.mult, op1=ALU.add)),
                            reads=["base_f", "mask8"], writes=[("bias8", bslot)])
                        for hf in range(2):
                            sc.add("act", (lambda e, hf=hf: e.activation(out=expb2[bslot][:, 256 * hf:256 * hf + 256], in_=bias8[bslot],
                                                                      func=AF.Exp, scale=0.125)),
                                   reads=[("bias8", bslot)], writes=[("expb2", bslot)])
                        cnt["cur_bias", hh, oi] = bslot
                    bslot = cnt["cur_bias", hh, oi]
                    bank = 2 + cnt["ss"] % 3
                    s4 = cnt["ss"] % 4
                    cnt["ss"] += 1
                    lo, hi = None, None
                    for idx, tl in enumerate(unit):
                        T, res, c = tl["T"], tl["res"], tl["c"]
                        q0 = max(0, 128 * c - 64)
                        q1 = min(L, 128 * c + 192)
                        j0 = q0 - (128 * c - 64)
                        nq = q1 - q0
                        kcols = slice(res + r * 128 * c, res + r * (128 * c + 127) + 1, r)
                        qcols = slice(res + r * q0, res + r * (q1 - 1) + 1, r)
                        cb = 256 * idx + j0
                        tl.update(j0=j0, nq=nq, s4=s4, cbase=256 * idx)
                        if lo is None:
                            lo = cb
                        hi = cb + nq
                        ssap = pb[bank][:, cb:cb + nq]
                        tq = sorted({(res + r * q0) // 512, (res + r * (q1 - 1)) // 512} | (set(range(4)) if r > 1 else set()))
                        tk = sorted({(res + r * 128 * c) // 512, (res + r * (128 * c + 127)) // 512} | (set(range(4)) if r > 1 else set()))
                        sc.add("pe", (lambda e, ssap=ssap, kcols=kcols, qcols=qcols: e.matmul(
                            ssap, lhsT=kT[R, kcols], rhs=qT[R, qcols], start=True, stop=True)),
                            reads=[("qT", p % 2, t_) for t_ in tq] + [("kT", p % 2, t_) for t_ in tk], writes=[("pb", bank)])
                    sc.add("act", (lambda e: e.activation(out=pT2[s4][:, lo:hi], in_=pb[bank][:, lo:hi], func=AF.Exp, scale=0.125)),
                           reads=[("pb", bank)], writes=[("pT2", s4)])
                    meng = "dve" if uidx % 2 == 0 else "pool"
                    sc.add(meng, (lambda e: e.tensor_tensor(out=pT2[s4][:, lo:hi], in0=pT2[s4][:, lo:hi], in1=expb2[bslot][:, lo:hi], op=ALU.mult)),
                           reads=[("pT2", s4), ("expb2", bslot)], writes=[("pT2", s4)])

                def pv(tl):
                    hh, oi, r, T, res, c = tl["hh"], tl["oi"], tl["r"], tl["T"], tl["res"], tl["c"]
                    j0, nq, s4, cbase = tl["j0"], tl["nq"], tl["s4"], tl["cbase"]
                    L = S // r
                    oacc = oaccb[hh]
                    lhsT_v = Vaug[:, oi, res * T + c, 64 * hh:64 * hh + 128]
                    vkeys = [("Vaug", oi, (res * T + c) // 8, 0), ("Vaug", oi, (res * T + c) // 8, 1), "Vaug"]
                    jbase = 128 * c - 64
                    for half in range(2):
                        bI = c + half
                        if T == 1:
                            if half == 1:
                                continue
                            p0, p1, st_, sp_ = 0, L, True, True
                        else:
                            p0 = max(0, 128 * bI - 64)
                            p1 = min(L, 128 * bI + 64)
                            if half == 0:
                                st_, sp_ = (c == 0), True
                            else:
                                st_, sp_ = True, (c == T - 1)
                        pos = p0
                        while pos < p1:
                            nxt = min(p1, (pos // 512 + 1) * 512)
                            if r == 16:
                                gkey = (hh, oi, res // 4, 0)
                                col0 = 128 * (res % 4) + pos
                                extent = 512
                            else:
                                gkey = (hh, oi, res, pos // 512)
                                col0 = pos % 512
                                extent = min(L, (pos // 512 + 1) * 512) - (pos // 512) * 512
                            if gkey not in grp_bank:
                                grp_bank[gkey] = 5 + cnt["blk"] % 2
                                cnt["blk"] += 1
                                grp_done[gkey] = 0
                            bank = grp_bank[gkey]
                            oap = pb[bank][:, col0:col0 + (nxt - pos)]
                            ja, jb = pos - jbase, nxt - jbase
                            sc.add("pe", (lambda e, oap=oap, ja=ja, jb=jb, st_=st_, sp_=sp_: e.matmul(
                                oap, lhsT=lhsT_v, rhs=pT2[s4][:, cbase + ja:cbase + jb], start=st_, stop=sp_)),
                                reads=[("pT2", s4)] + vkeys, writes=[("pb", bank)])
                            if sp_:
                                grp_done[gkey] += nxt - pos
                                if grp_done[gkey] == extent:
                                    if r == 16:
                                        gi = res // 4
                                        oview = oacc.rearrange("p (q k) -> p q k", k=16)[:, :, 4 * gi:4 * gi + 4]
                                        iview = pb[bank][:, 0:512].rearrange("p (k q) -> p q k", k=4)
                                        okeys = [("oacc", hh, t_) for t_ in range(4)]
                                    else:
                                        g0 = (pos // 512) * 512
                                        oview = oacc[:, slice(res + r * g0, res + r * (g0 + extent - 1) + 1, r)]
                                        iview = pb[bank][:, 0:extent]
                                        okeys = [("oacc", hh, t_) for t_ in (range(4) if r > 1 else [g0 // 512])]
                                    if oi == 0:
                                        sc.add("act", (lambda e, oview=oview, iview=iview: e.activation(out=oview, in_=iview, func=AF.Copy)),
                                               reads=[("pb", bank)], writes=okeys)
                                    else:
                                        sc.add("dve", (lambda e, oview=oview, iview=iview: e.tensor_tensor(
                                            out=oview, in0=iview, in1=oview, op=ALU.add)),
                                            reads=[("pb", bank)] + okeys, writes=okeys)
                            pos = nxt

                def finalize(hh):
                    oacc = oaccb[hh]
                    R = slice(64 * hh, 64 * hh + 64)
                    DR = slice(64 * (1 - hh), 64 * (1 - hh) + 64)
                    for t in range(4):
                        fs = cnt["fin"] % 2
                        cnt["fin"] += 1
                        tc = slice(512 * t, 512 * t + 512)
                        sc.add("act", (lambda e, fs=fs, tc=tc: e.activation(out=rden[fs][DR, :], in_=oacc[DR, tc], func=AF.Ln)),
                               reads=[("oacc", hh, t)], writes=[("rden", fs)])
                        sc.add("act", (lambda e, fs=fs: e.activation(out=rden[fs][DR, :], in_=rden[fs][DR, :], func=AF.Exp, scale=-1.0)),
                               reads=[("rden", fs)], writes=[("rden", fs)])
                        sc.add("dve", (lambda e, fs=fs: e.tensor_copy(out=rden[fs][R, :], in_=rden[fs][DR, :])),
                               reads=[("rden", fs)], writes=[("rden", fs)])
                        sc.add("pool", (lambda e, fs=fs, tc=tc: e.tensor_tensor(
                            out=opair[R, tc], in0=oacc[R, tc], in1=rden[fs][R, :], op=ALU.mult)),
                            reads=[("oacc", hh, t), ("rden", fs)], writes=[("opair", t, hh)])

                units = [tiles[i:i + 2] for i in range(0, len(tiles), 2)]
                for un in units:
                    assert (un[0]["hh"], un[0]["oi"]) == (un[1]["hh"], un[1]["oi"])
                n = len(units)
                for i in range(min(LOOKU, n)):
                    scores(units[i], i)
                yield
                for i0 in range(0, n, BURST):
                    for i in range(i0, min(i0 + BURST, n)):
                        for tl in units[i]:
                            pv(tl)
                            if tl["last_of_head"]:
                                finalize(tl["hh"])
                    for i in range(i0 + LOOKU, min(i0 + LOOKU + BURST, n)):
                        scores(units[i], i)
                    yield

            DBG["pair"] = 0
            merge([in_proj_units(32, mk_ev_q(0)), in_proj_units(40, mk_ev_k(0)), in_proj_units(48, ev_v)], None)
            for p in range(npairs):
                DBG["pair"] = p
                a = []
                if p >= 1:
                    a.append(in_proj_units(56 + p - 1, mk_ev_z(p - 1)))
                if p + 1 < npairs:
                    a += [in_proj_units(32 + p + 1, mk_ev_q(p + 1)), in_proj_units(40 + p + 1, mk_ev_k(p + 1)),
                          in_proj_units(48 + p + 1, ev_v)]
                merge(a, attention_steps(p))
            in_proj(56 + npairs - 1, mk_ev_z(npairs - 1))
            state["nbank"] = 3

            if stop_after == "yTa":
                dump(yTa[:, :, :], 8)
                done = True

        if not done:
            sc.barrier()
            u_sb = uview(0, BF16, [2048])
            cu = uview(4096, F32, [2050])
            acc = uview(12800, F32, [2048])
            yraw = uview(20992, F32, [2048])
            zsc = [uview(29184 + 1024 * i, BF16, [512]) for i in range(2)]
            sqc = [uview(31232 + 1024 * i, BF16, [512]) for i in range(2)]
            yTc = uview(34816, BF16, [8, 2048])
            sc.add("dve", lambda e: e.memset(cu[:, 0:1], 0.0), writes=[("cu", 0)])
            sc.add("dve", lambda e: e.memset(cu[:, 2049:2050], 0.0), writes=[("cu", 3)])
            cz = {"zz": 0}
            allcu = [("cu", t) for t in range(4)]
            allacc = [("acc", t) for t in range(4)]
            for f in range(8):
                def ev_u(t, bank):
                    sc.add("act", (lambda e, t=t, bank=bank: e.activation(out=u_sb[:, 512 * t:512 * t + 512], in_=pb[bank][:, :],
                                                                          func=AF.Copy)),
                           reads=[("pb", bank)], writes=[("u_sb", t)])

                def ev_cg(t, bank):
                    sc.add("dve", (lambda e, t=t, bank=bank: e.tensor_tensor(
                        out=cu[:, 1 + 512 * t:1 + 512 * t + 512], in0=pb[bank][:, :], in1=u_sb[:, 512 * t:512 * t + 512],
                        op=ALU.mult)),
                        reads=[("pb", bank), ("u_sb", t)], writes=[("cu", t)])

                in_proj(f, ev_u)
                in_proj(16 + f, ev_cg)
                sc.add("act", (lambda e, f=f: e.activation(out=acc[:, :], in_=cu[:, 1:2049], func=AF.Identity,
                                                           bias=convb[:, f:f + 1], scale=convw[:, f, 1:2])),
                       reads=allcu + ["convw", "convb"], writes=allacc)
                sc.add("dve", (lambda e, f=f: e.scalar_tensor_tensor(out=acc[:, :], in0=cu[:, 0:2048], scalar=convw[:, f, 0:1],
                                                                    in1=acc[:, :], op0=ALU.mult, op1=ALU.add)),
                       reads=allcu + allacc + ["convw"], writes=allacc)
                sc.add("dve", (lambda e, f=f: e.scalar_tensor_tensor(out=acc[:, :], in0=cu[:, 2:2050], scalar=convw[:, f, 2:3],
                                                                     in1=acc[:, :], op0=ALU.mult, op1=ALU.add)),
                       reads=allcu + allacc + ["convw"], writes=allacc)

                def ev_bg(t, bank):
                    sc.add("dve", (lambda e, t=t, bank=bank: e.tensor_tensor(
                        out=yraw[:, 512 * t:512 * t + 512], in0=pb[bank][:, :], in1=acc[:, 512 * t:512 * t + 512], op=ALU.mult)),
                        reads=[("pb", bank), ("acc", t)], writes=[("yraw", t)])

                in_proj(8 + f, ev_bg)

                def ev_zc(t, bank, f=f):
                    zs_ = cz["zz"] % 2
                    cz["zz"] += 1
                    tc = slice(512 * t, 512 * t + 512)
                    sc.add("act", (lambda e, zs_=zs_, bank=bank: e.activation(out=zsc[zs_], in_=pb[bank][:, :], func=AF.Silu)),
                           reads=[("pb", bank)], writes=[("zsc", zs_)])
                    sc.add("dve", (lambda e, zs_=zs_, tc=tc: e.scalar_tensor_tensor(
                        out=yTc[:, f, tc], in0=yraw[:, tc], scalar=gconv[:, f:f + 1], in1=zsc[zs_],
                        op0=ALU.mult, op1=ALU.mult)),
                        reads=[("yraw", t), ("zsc", zs_), "gconv"], writes=[("yTc", f, t)])
                    sc.add("pool", (lambda e, zs_=zs_, tc=tc: e.tensor_tensor(
                        out=sqc[zs_], in0=yraw[:, tc], in1=yraw[:, tc], op=ALU.mult)),
                        reads=[("yraw", t)], writes=[("sqc", zs_)])
                    for i4 in range(4):
                        col = 16 + 4 * t + i4
                        sc.add("pe", (lambda e, zs_=zs_, i4=i4, col=col: e.matmul(
                            pb[7][:, col:col + 1], lhsT=sqc[zs_][:, 128 * i4:128 * i4 + 128], rhs=ones_b[:, 0:1],
                            start=True, stop=True)),
                            reads=[("sqc", zs_), "ones_b"], writes=[("pb", 7)])

                in_proj(24 + f, ev_zc)
                if f == 0:
                    sc.add("dve", lambda e: e.tensor_copy(out=ssq_c, in_=pb[7][:, 16:32]), reads=[("pb", 7)], writes=["ssq_c"])
                else:
                    sc.add("dve", lambda e: e.tensor_tensor(out=ssq_c, in0=pb[7][:, 16:32], in1=ssq_c, op=ALU.add),
                           reads=[("pb", 7), "ssq_c"], writes=["ssq_c"])

            if stop_after == "yTc":
                sc.barrier()
                sc.add("sp", lambda e: e.dma_start(out=dbg_d[:, 0:8, :], in_=yTc[:, :, :]), dma_key="out")
                sc.add("sp", lambda e: e.dma_start(out=dbg_d[:, 8:16, :], in_=yTa[:, :, :]), dma_key="out")
                done = True

        if not done:
            sc.barrier()
            xt = [uview(8192 * i, F32, [2048]) for i in range(2)]
            yv = [uview(16384 + 8192 * i, F32, [2048]) for i in range(2)]
            junk = uview(32768, BF16, [512])
            gg = wsl[:, :, :, :].rearrange("p a b c -> p (a b c)").bitcast(F32)[:, 0:2048]
            wout = hT
            for src, dst, nm in ((ssq_a, ra, "ra"), (ssq_c, rc, "rc")):
                sc.add("dve", (lambda e, src=src, dst=dst: e.tensor_scalar(out=dst, in0=src, scalar1=1.0 / 1024.0, scalar2=EPS,
                                                                           op0=ALU.mult, op1=ALU.add)), writes=[nm])
                sc.add("act", (lambda e, dst=dst: e.activation(out=dst, in_=dst, func=AF.Sqrt)), reads=[nm], writes=[nm])
                sc.add("dve", (lambda e, dst=dst: e.reciprocal(out=dst, in_=dst)), reads=[nm], writes=[nm])
            for k in range(KC):
                sc.add("pool", (lambda e, k=k: e.dma_start(out=wout[:, k, :], in_=wout_d[128 * k:128 * k + 128, :])),
                       writes=[("wout", k)], dma_key=("wout", k))
            sc.add("sp", lambda e: e.dma_start(out=gg, in_=gpost_d[:, :].partition_broadcast(128)), writes=["gg"], dma_key="gg")
            dg = [yv[1][:, 0:128], yv[1][:, 128:256]]
            for k in range(KC):
                d2 = k % 2
                sc.add("dve", (lambda e, k=k, d2=d2: e.tensor_scalar(out=dg[d2], in0=ident_f[:, :], scalar1=gate_lay[:, k:k + 1],
                                                                    scalar2=None, op0=ALU.mult)),
                       reads=["ident_f", "gate_lay"], writes=[("dg", d2)])
                sc.add("pe", (lambda e, k=k, d2=d2: e.matmul(pb[k // 4][:, 128 * (k % 4):128 * (k % 4) + 128], lhsT=ones_f[:, :],
                                                            rhs=dg[d2], start=True, stop=True)),
                       reads=[("dg", d2), "ones_f"], writes=[("pb", k // 4)])
                if k % 4 == 3:
                    g4 = k // 4
                    sc.add("dve", (lambda e, g4=g4: e.tensor_tensor(out=gg[:, 512 * g4:512 * g4 + 512], in0=pb[g4][:, :],
                                                                   in1=gg[:, 512 * g4:512 * g4 + 512], op=ALU.mult)),
                           reads=[("pb", g4), "gg"], writes=["gg"])
            for i in range(16):
                b2 = i % 2
                tcol = slice(128 * i, 128 * i + 128)
                sc.add("sp", (lambda e, i=i, b2=b2: e.dma_start(out=xt[b2], in_=x_d[128 * i:128 * i + 128, :])),
                       writes=[("xt", b2)], dma_key=("xt", b2))
                for hf in range(2):
                    for k in range(KC):
                        grp = 0 if k < 8 else 1
                        lhsT = yTc[:, k, tcol] if k < 8 else yTa[:, k - 8, tcol]
                        for n in range(2):
                            bank = 4 * hf + 2 * grp + n
                            dc = slice(1024 * hf + 512 * n, 1024 * hf + 512 * n + 512)
                            sc.add("pe", (lambda e, bank=bank, lhsT=lhsT, k=k, dc=dc: e.matmul(
                                pb[bank][:, :], lhsT=lhsT, rhs=wout[:, k, dc], start=(k % 8 == 0), stop=(k % 8 == 7))),
                                reads=[("wout", k)], writes=[("pb", bank)])
                    for n in range(2):
                        dc = slice(1024 * hf + 512 * n, 1024 * hf + 512 * n + 512)
                        q4 = 2 * hf + n
                        extra = [("dg", 0), ("dg", 1)] if b2 == 1 else []
                        sc.add("act", (lambda e, hf=hf, n=n, dc=dc, b2=b2, i=i: e.activation(
                            out=yv[b2][:, dc], in_=pb[4 * hf + n][:, :], func=AF.Identity, scale=rc[:, i:i + 1])),
                            reads=[("pb", 4 * hf + n), "rc"] + extra, writes=[("yv", b2, q4)] + extra)
                        sc.add("dve", (lambda e, hf=hf, n=n, dc=dc, b2=b2, i=i: e.scalar_tensor_tensor(
                            out=yv[b2][:, dc], in0=pb[4 * hf + 2 + n][:, :], scalar=ra[:, i:i + 1], in1=yv[b2][:, dc],
                            op0=ALU.mult, op1=ALU.add)),
                            reads=[("pb", 4 * hf + 2 + n), "ra", ("yv", b2, q4)], writes=[("yv", b2, q4)])
                        sc.add("act", (lambda e, dc=dc, b2=b2, i=i, q4=q4: e.activation(
                            out=junk, in_=yv[b2][:, dc], func=AF.Square, accum_out=ssq_p[:, 4 * i + q4:4 * i + q4 + 1])),
                            reads=[("yv", b2, q4)], writes=["junk", ("ssq_p", i, q4)])
                allyv = [("yv", b2, q4) for q4 in range(4)]
                sc.add("dve", (lambda e, i=i: e.reduce_sum(out=ssq_p1[:, i:i + 1], in_=ssq_p[:, 4 * i:4 * i + 4], axis=AX.X)),
                       reads=[("ssq_p", i, q4) for q4 in range(4)], writes=[("rp", i)])
                sc.add("dve", (lambda e, i=i: e.tensor_scalar(out=rp[:, i:i + 1], in0=ssq_p1[:, i:i + 1], scalar1=1.0 / D,
                                                              scalar2=EPS, op0=ALU.mult, op1=ALU.add)),
                       reads=[("rp", i)], writes=[("rp", i)])
                sc.add("act", (lambda e, i=i: e.activation(out=rp[:, i:i + 1], in_=rp[:, i:i + 1], func=AF.Sqrt)),
                       reads=[("rp", i)], writes=[("rp", i)])
                sc.add("dve", (lambda e, i=i: e.reciprocal(out=rp[:, i:i + 1], in_=rp[:, i:i + 1])),
                       reads=[("rp", i)], writes=[("rp", i)])
                sc.add("dve", (lambda e, i=i, b2=b2: e.scalar_tensor_tensor(
                    out=yv[b2], in0=yv[b2], scalar=rp[:, i:i + 1], in1=gg, op0=ALU.mult, op1=ALU.mult)),
                    reads=allyv + [("rp", i), "gg"], writes=allyv)
                sc.add("pool", (lambda e, b2=b2: e.tensor_tensor(out=yv[b2], in0=yv[b2], in1=xt[b2], op=ALU.add)),
                       reads=allyv + [("xt", b2)], writes=allyv)
                sc.add("act", (lambda e, i=i, b2=b2: e.dma_start(out=out_d[128 * i:128 * i + 128, :], in_=yv[b2])),
                       reads=allyv, writes=[("outdram", i)], dma_key=("out", b2))

        sc.barrier()

        sc.finalize()
        dma_keys = list(sc.dma_cnt.keys())
        eng_sems = {e: es.enter_context(nc.semaphore(f"sem_{e}")) for e in Sched.ENGS}
        dma_sems = {k: es.enter_context(nc.semaphore(f"dsem_{i}")) for i, k in enumerate(dma_keys)}
        block = es.enter_context(nc.Block())

        @block.tensor
        def _(e):
            sc.emit("pe", e, eng_sems, dma_sems)

        @block.scalar
        def _(e):
            sc.emit("act", e, eng_sems, dma_sems)

        @block.vector
        def _(e):
            sc.emit("dve", e, eng_sems, dma_sems)

        @block.gpsimd
        def _(e):
            sc.emit("pool", e, eng_sems, dma_sems)

        @block.sync
        def _(e):
            sc.emit("sp", e, eng_sems, dma_sems)

    return nc


def _lay(v, nchunk):
    return np.ascontiguousarray(np.asarray(v, np.float32).reshape(nchunk, 128).T)


def make_in_maps(x, c, w_ada, b_ada, g_pre, w_in, conv_w, conv_b, g_conv, g_attn, w_out, g_post):
    x = np.asarray(x, np.float32)
    c = np.asarray(c, np.float32)
    w_ada0 = np.ascontiguousarray(np.asarray(w_ada, np.float32)[0])
    w_in0 = np.ascontiguousarray(np.asarray(w_in, np.float32)[0])
    w_out0 = np.ascontiguousarray(np.asarray(w_out, np.float32)[0])
    cw = np.asarray(conv_w, np.float32)[0]
    convw_lay = np.ascontiguousarray(cw.reshape(3, 8, 128).transpose(2, 1, 0))
    shared = {
        "w_ada": w_ada0,
        "b_row": np.ascontiguousarray(np.asarray(b_ada, np.float32)[0].reshape(1, NMOD)),
        "gpre_lay": _lay(np.asarray(g_pre)[0], 16),
        "w_in": w_in0,
        "convw_lay": convw_lay,
        "convb_lay": _lay(np.asarray(conv_b)[0], 8),
        "gconv_lay": _lay(np.asarray(g_conv)[0], 8),
        "gattn_lay": _lay(np.asarray(g_attn)[0], 8),
        "w_out": w_out0,
        "gpost_row": np.ascontiguousarray(np.asarray(g_post, np.float32)[0].reshape(1, D)),
    }
    maps = []
    for b in range(N_CORES):
        m = dict(shared)
        m["x"] = np.ascontiguousarray(x[b])
        m["c_lay"] = _lay(c[b], 16)
        maps.append(m)
    return maps


_NC_CACHE = {}


def kernel(x, c, w_ada, b_ada, g_pre, w_in, conv_w, conv_b, g_conv, g_attn, w_out, g_post):
    if "nc" not in _NC_CACHE:
        _NC_CACHE["nc"] = build_nc()
    nc = _NC_CACHE["nc"]
    in_maps = make_in_maps(x, c, w_ada, b_ada, g_pre, w_in, conv_w, conv_b, g_conv, g_attn, w_out, g_post)
    res = run_bass_kernel_spmd(nc, in_maps, core_ids=list(range(N_CORES)))
    out = np.stack([np.asarray(r["out"], np.float32) for r in res.results], axis=0)
    return out
```

```python
import numpy as np
import concourse.bass as bass
import concourse.mybir as mybir
from concourse.bass_utils import run_bass_kernel_spmd

F32 = mybir.dt.float32
BF16 = mybir.dt.bfloat16
AF = mybir.ActivationFunctionType
ALU = mybir.AluOpType
AX = mybir.AxisListType

D = 2048
S = 2048
NPROJ = 8192
NMOD = 6144
KC = 16
EPS = 1e-6
NEG8 = -8.0 * 30000.0
N_CORES = 8
DBG = {"pair": -1}


class _Op:
    __slots__ = ("eng", "fn", "deps", "dma_key", "dma_n", "signal", "sig")

    def __init__(self, eng, fn):
        self.eng = eng
        self.fn = fn
        self.deps = []
        self.dma_key = None
        self.dma_n = 0
        self.signal = False
        self.sig = 0


class _DmaTok:
    __slots__ = ("key", "n")

    def __init__(self, key, n):
        self.key = key
        self.n = n


class Sched:
    ENGS = ("pe", "act", "dve", "pool", "sp")

    def __init__(self):
        self.streams = {e: [] for e in self.ENGS}
        self.res_w = {}
        self.res_r = {}
        self.dma_cnt = {}
        self.last = {}

    @staticmethod
    def _grp(tok):
        return ("dma", tok.key) if isinstance(tok, _DmaTok) else tok.eng

    def add(self, eng, fn, reads=(), writes=(), dma_key=None, extra_deps=()):
        op = _Op(eng, fn)
        deps = {}
        for r in reads:
            for t in self.res_w.get(r, {}).values():
                deps[id(t)] = t
        for w in writes:
            for t in self.res_w.get(w, {}).values():
                deps[id(t)] = t
            for t in self.res_r.get(w, {}).values():
                deps[id(t)] = t
        for t in extra_deps:
            deps[id(t)] = t
        if dma_key is not None:
            n = self.dma_cnt.get(dma_key, 0) + 1
            self.dma_cnt[dma_key] = n
            op.dma_key = dma_key
            op.dma_n = n
            tok = _DmaTok(dma_key, n)
        else:
            tok = op
        g = self._grp(tok)
        for r in reads:
            self.res_r.setdefault(r, {})[g] = tok
        for w in writes:
            self.res_w[w] = {g: tok}
            self.res_r[w] = {}
        deps.pop(id(op), None)
        op.deps = list(deps.values())
        self.streams[eng].append(op)
        if fn is not None:
            self.last[g] = tok
        return tok

    def barrier(self):
        toks = list(self.last.values())
        for e in self.ENGS:
            self.add(e, None, extra_deps=toks)

    def finalize(self):
        for e in self.ENGS:
            for op in self.streams[e]:
                for d in op.deps:
                    if isinstance(d, _Op):
                        if d.eng == "pe" and op.eng == "pe":
                            continue
                        d.signal = True
        for e in self.ENGS:
            c = 0
            for op in self.streams[e]:
                if op.signal:
                    c += 1
                    op.sig = c

    def emit(self, eng_name, e, eng_sems, dma_sems):
        waited = {}
        for op in self.streams[eng_name]:
            need = {}
            for d in op.deps:
                if isinstance(d, _Op):
                    if d.eng == "pe" and eng_name == "pe":
                        continue
                    g, v = d.eng, d.sig
                else:
                    g = ("dma", d.key)
                    v = 16 * (self.dma_cnt[d.key] if d.key == "const" else d.n)
                if v > need.get(g, 0):
                    need[g] = v
            for g, v in need.items():
                if waited.get(g, 0) >= v:
                    continue
                waited[g] = v
                sem = dma_sems[g[1]] if isinstance(g, tuple) else eng_sems[g]
                e.wait_ge(sem, v)
            if op.fn is None:
                continue
            ins = op.fn(e)
            if op.dma_key is not None:
                ins.then_inc(dma_sems[op.dma_key], 16)
            elif op.signal:
                ins.then_inc(eng_sems[eng_name], 1)


def _slopes():
    return [2.0 ** (-8.0 * (h + 1) / 16.0) for h in range(16)]


def build_nc(stop_after=None, npairs=8):
    nc = bass.Bass("TRN2", target_bir_lowering=False)
    x_d = nc.dram_tensor("x", [S, D], F32, kind="ExternalInput").ap()
    c_d = nc.dram_tensor("c_lay", [128, KC], F32, kind="ExternalInput").ap()
    wada_d = nc.dram_tensor("w_ada", [D, NMOD], F32, kind="ExternalInput").ap()
    bada_d = nc.dram_tensor("b_row", [1, NMOD], F32, kind="ExternalInput").ap()
    gpre_d = nc.dram_tensor("gpre_lay", [128, KC], F32, kind="ExternalInput").ap()
    win_d = nc.dram_tensor("w_in", [D, NPROJ], F32, kind="ExternalInput").ap()
    convw_d = nc.dram_tensor("convw_lay", [128, 8, 3], F32, kind="ExternalInput").ap()
    convb_d = nc.dram_tensor("convb_lay", [128, 8], F32, kind="ExternalInput").ap()
    gconv_d = nc.dram_tensor("gconv_lay", [128, 8], F32, kind="ExternalInput").ap()
    gattn_d = nc.dram_tensor("gattn_lay", [128, 8], F32, kind="ExternalInput").ap()
    wout_d = nc.dram_tensor("w_out", [D, D], F32, kind="ExternalInput").ap()
    gpost_d = nc.dram_tensor("gpost_row", [1, D], F32, kind="ExternalInput").ap()
    out_d = nc.dram_tensor("out", [S, D], F32, kind="ExternalOutput").ap()
    dbg_d = None
    if stop_after is not None:
        dbg_d = nc.dram_tensor("dbg", [128, 16, 2048], BF16, kind="ExternalOutput").ap()

    sc = Sched()
    UF = 20992

    import contextlib
    with contextlib.ExitStack() as es:
        def sb(name, shape, dt):
            return es.enter_context(nc.sbuf_tensor(name, shape, dt))

        hT = sb("hT", [128, KC, 2048], BF16)
        yTa = sb("yTa", [128, 8, 2048], BF16)
        U = sb("U", [128, UF], F32)
        wsl = sb("wsl", [128, 4, KC, 128], BF16)
        ident_f = sb("ident_f", [128, 128], F32)
        ident_b = sb("ident_b", [128, 128], BF16)
        ones_f = sb("ones_f", [128, 128], F32)
        ones_b = sb("ones_b", [128, 128], BF16)
        base_f = sb("base_f", [128, 256], F32)
        mask8 = sb("mask8", [128, 256], F32)
        sm = sb("sm", [128, 512], F32)
        cab = sb("cab", [128, KC], BF16)
        pb = [es.enter_context(nc.psum_tensor(f"pb{i}", [128, 512], F32)) for i in range(8)]

        c_sb = sm[:, 0:16]
        gpre = sm[:, 16:32]
        a_sb = sm[:, 32:48]
        s_sb = sm[:, 48:64]
        gate_lay = sm[:, 64:80]
        convw = sm[:, 80:104].rearrange("p (f t) -> p f t", t=3)
        convb = sm[:, 104:112]
        gconv = sm[:, 112:120]
        gattn = sm[:, 120:128]
        ssq_x = sm[:, 128:144]
        rstd_x = sm[:, 144:160]
        ssq_a = sm[:, 160:176]
        ssq_c = sm[:, 176:192]
        ra = sm[:, 192:208]
        rc = sm[:, 208:224]
        ssq_p = sm[:, 224:288]
        ssq_p1 = sm[:, 288:304]
        rp = sm[:, 304:320]
        tmp16 = sm[:, 320:336]

        def uview(off, dt, shape):
            n = 1
            for s_ in shape:
                n *= s_
            nb = n * (4 if dt == F32 else 2)
            assert off % 4 == 0 and nb % 4 == 0 and off + nb <= UF * 4, (off, nb)
            a = U[:, off // 4:(off + nb) // 4]
            if dt != F32:
                a = a.bitcast(dt)
            if len(shape) == 2:
                return a.rearrange("p (a b) -> p a b", b=shape[1])
            if len(shape) == 3:
                return a.rearrange("p (a b c) -> p a b c", b=shape[1], c=shape[2])
            return a

        def psbf(bank):
            return pb[bank][:, :].bitcast(BF16)

        sc.add("pool", lambda e: e.iota(base_f[:, :], [[1, 256]], base=-64, channel_multiplier=-1, allow_small_or_imprecise_dtypes=True),
               writes=["base_f"])
        sc.add("dve", lambda e: e.tensor_single_scalar(out=ident_f[:, :], in_=base_f[:, 64:192], scalar=0.0, op=ALU.is_equal),
               reads=["base_f"], writes=["ident_f"])
        sc.add("dve", lambda e: e.tensor_copy(out=ident_b[:, :], in_=ident_f[:, :]), reads=["ident_f"], writes=["ident_b"])
        sc.add("dve", lambda e: e.memset(ones_f[:, :], 1.0), writes=["ones_f"])
        sc.add("dve", lambda e: e.memset(ones_b[:, :], 1.0), writes=["ones_b"])
        sc.add("dve", lambda e: e.tensor_scalar(out=mask8[:, :], in0=base_f[:, :], scalar1=-1.0, scalar2=None, op0=ALU.mult),
               reads=["ident_f", "base_f"], writes=["mask8"])
        sc.add("dve", lambda e: e.tensor_tensor(out=base_f[:, :], in0=base_f[:, :], in1=mask8[:, :], op=ALU.max),
               reads=["mask8"], writes=["base_f"])
        sc.add("dve", lambda e: e.tensor_scalar(out=mask8[:, :], in0=base_f[:, :], scalar1=64.5, scalar2=NEG8,
                                                op0=ALU.is_gt, op1=ALU.mult),
               reads=["base_f"], writes=["mask8"])

        def cdma(dst, src, key):
            sc.add("sp", lambda e: e.dma_start(out=dst, in_=src), writes=[key], dma_key="const")

        cdma(c_sb, c_d[:, :], "c_sb")
        cdma(gpre, gpre_d[:, :], "gpre")
        cdma(convw, convw_d[:, :, :], "convw")
        cdma(convb, convb_d[:, :], "convb")
        cdma(gconv, gconv_d[:, :], "gconv")
        cdma(gattn, gattn_d[:, :], "gattn")

        def dump_small():
            sc.barrier()
            sc.add("sp", lambda e: e.dma_start(out=dbg_d[:, 0, 0:256], in_=sm[:, 0:256].bitcast(BF16)[:, 0:256]), dma_key="out")
            sc.add("sp", lambda e: e.dma_start(out=out_d[0:128, 0:256], in_=base_f[:, :]), dma_key="out")
            sc.add("sp", lambda e: e.dma_start(out=out_d[128:256, 0:256], in_=mask8[:, :]), dma_key="out")
            sc.add("sp", lambda e: e.dma_start(out=out_d[256:384, 0:128], in_=ident_f[:, :]), dma_key="out")
            sc.add("sp", lambda e: e.dma_start(out=out_d[384:512, 0:512], in_=sm[:, :]), dma_key="out")
        done = False
        if stop_after == "const":
            dump_small()
            done = True
        mod_row = U[0:1, 0:NMOD]
        b_row = U[0:1, NMOD:2 * NMOD]
        xs = [uview(2 * NMOD * 4 + 8192 * i, F32, [2048]) for i in range(2)]
        xn = [yTa[:, i, :] for i in range(2)]
        if not done:
          cdma(b_row, bada_d[:, :], "b_row")
          sc.add("act", lambda e: e.activation(out=cab[:, :], in_=c_sb, func=AF.Silu), reads=["c_sb"], writes=["cab"])

          def wa_slot(s_):
              return yTa[:, 2 + 2 * s_:4 + 2 * s_, :].rearrange("p a (b n) -> p (a b) n", n=256)

          def emit_slice(s_):
              slot = s_ % 3
              dst = wa_slot(slot)
              src = wada_d[:, 256 * s_:256 * s_ + 256].rearrange("(k p) n -> p k n", p=128)
              sc.add("pool", (lambda e: e.dma_start(out=dst, in_=src)), writes=[("waslot", slot)], dma_key=("wa", slot))
              bank = s_ % 2
              for k in range(KC):
                  sc.add("pe", (lambda e, k=k: e.matmul(
                      pb[bank][0:1, 0:256], lhsT=cab[:, k:k + 1], rhs=dst[:, k, :], start=(k == 0), stop=(k == KC - 1))),
                      reads=[("waslot", slot), "cab"], writes=[("pb", bank)])
              sc.add("dve", (lambda e: e.tensor_tensor(
                  out=mod_row[:, 256 * s_:256 * s_ + 256], in0=pb[bank][0:1, 0:256],
                  in1=b_row[:, 256 * s_:256 * s_ + 256], op=ALU.add)),
                  reads=[("pb", bank), "b_row"], writes=[("mod", s_)])

          def emit_tile(i):
              b2 = i % 2
              sc.add("sp", (lambda e: e.dma_start(out=xs[b2], in_=x_d[128 * i:128 * i + 128, :])),
                     writes=[("xs", b2)], dma_key=("xs", b2))
              sc.add("act", (lambda e: e.activation(out=xn[b2], in_=xs[b2], func=AF.Square, accum_out=ssq_x[:, i:i + 1])),
                     reads=[("xs", b2)], writes=[("xn", b2), ("ssq_x", i)])
              sc.add("dve", (lambda e: e.tensor_scalar(out=rstd_x[:, i:i + 1], in0=ssq_x[:, i:i + 1], scalar1=1.0 / D,
                                                       scalar2=EPS, op0=ALU.mult, op1=ALU.add)),
                     reads=[("ssq_x", i)], writes=[("rstd_x", i)])
              sc.add("act", (lambda e: e.activation(out=rstd_x[:, i:i + 1], in_=rstd_x[:, i:i + 1], func=AF.Sqrt)),
                     reads=[("rstd_x", i)], writes=[("rstd_x", i)])
              sc.add("dve", (lambda e: e.reciprocal(out=rstd_x[:, i:i + 1], in_=rstd_x[:, i:i + 1])),
                     reads=[("rstd_x", i)], writes=[("rstd_x", i)])
              sc.add("dve", (lambda e: e.tensor_scalar(out=xn[b2], in0=xs[b2], scalar1=rstd_x[:, i:i + 1],
                                                       scalar2=None, op0=ALU.mult)),
                     reads=[("xs", b2), ("rstd_x", i)], writes=[("xn", b2)])
              for g in range(2):
                  bank = 4 + 2 * b2 + g
                  for kk in range(8):
                      k = 8 * g + kk
                      sc.add("pe", (lambda e, kk=kk, k=k, bank=bank: e.transpose(
                          psbf(bank)[:, 128 * kk:128 * kk + 128], xn[b2][:, 128 * k:128 * k + 128], ident_b[:, :])),
                          reads=[("xn", b2), "ident_b"], writes=[("pb", bank)])
                  src3 = psbf(bank).rearrange("p (c f) -> p c f", f=128)
                  dst3 = hT[:, 8 * g:8 * g + 8, 128 * i:128 * i + 128]
                  hkeys = [("hT", k) for k in range(8 * g, 8 * g + 8)]
                  sc.add("dve", (lambda e, src3=src3, dst3=dst3: e.tensor_copy(out=dst3, in_=src3)),
                         reads=[("pb", bank)], writes=hkeys)

          n_sl = 0
          for i in range(16):
              emit_tile(i)
              while n_sl < (24 * (i + 1)) // 16:
                  emit_slice(n_sl)
                  n_sl += 1
          for j in range(48):
              sc.add("pe", (lambda e, j=j: e.matmul(pb[2][:, j:j + 1], lhsT=mod_row[:, 128 * j:128 * j + 128],
                                                    rhs=ones_f[0:1, 0:1], start=True, stop=True)),
                     reads=[("mod", j // 2), "ones_f"], writes=[("pb", 2)])
          sc.add("dve", lambda e: e.tensor_copy(out=s_sb, in_=pb[2][:, 0:16]), reads=[("pb", 2)], writes=["s_sb"])
          sc.add("dve", lambda e: e.scalar_tensor_tensor(out=a_sb, in0=pb[2][:, 16:32], scalar=1.0, in1=gpre,
                                                         op0=ALU.add, op1=ALU.mult),
                 reads=[("pb", 2), "gpre"], writes=["a_sb"])
          sc.add("dve", lambda e: e.tensor_copy(out=gate_lay, in_=pb[2][:, 32:48]), reads=[("pb", 2)], writes=["gate_lay"])
          for k in range(KC):
              if k % 2 == 0:
                  sc.add("dve", (lambda e, k=k: e.tensor_scalar(out=hT[:, k, :], in0=hT[:, k, :], scalar1=a_sb[:, k:k + 1],
                                                                scalar2=s_sb[:, k:k + 1], op0=ALU.mult, op1=ALU.add)),
                         reads=[("hT", k), "a_sb", "s_sb"], writes=[("hT", k)])
              else:
                  sc.add("act", (lambda e, k=k: e.activation(out=hT[:, k, :], in_=hT[:, k, :], func=AF.Identity,
                                                             bias=s_sb[:, k:k + 1], scale=a_sb[:, k:k + 1])),
                         reads=[("hT", k), "a_sb", "s_sb"], writes=[("hT", k)])

        def dump(src_ap, nk):
            sc.barrier()
            sc.add("sp", lambda e: e.dma_start(out=dbg_d[:, 0:nk, :], in_=src_ap), dma_key="out")
            sc.add("sp", lambda e: e.dma_start(out=out_d[0:128, :], in_=xs[0]), dma_key="out")

        if stop_after == "mod" and not done:
            dump_small()
            done = True
        if stop_after == "hT" and not done:
            dump(hT[:, :, :], 16)
            done = True

        state = {"wslot": 0, "bank": 0, "nbank": 3}

        order = [32, 40, 48]
        for p_ in range(npairs):
            if p_ >= 1:
                order.append(56 + p_ - 1)
            if p_ + 1 < npairs:
                order += [32 + p_ + 1, 40 + p_ + 1, 48 + p_ + 1]
        order.append(56 + npairs - 1)
        for f_ in range(8):
            order += [f_, 16 + f_, 8 + f_, 24 + f_]
        state["issued"] = 0

        def issue_w(upto):
            while state["issued"] < min(upto, len(order)):
                n_ = state["issued"]
                j_ = order[n_]
                slot_ = n_ % 4
                src = win_d[:, 128 * j_:128 * j_ + 128].rearrange("(k p) n -> p k n", p=128)
                sc.add("pool", (lambda e, slot_=slot_, src=src: e.dma_start(out=wsl[:, slot_], in_=src)),
                       writes=[("wsl", slot_)], dma_key=("wsl", slot_))
                state["issued"] += 1

        def in_proj_units(j, evac):
            n_ = state["wslot"]
            assert order[n_] == j, (n_, j, order[n_])
            slot = n_ % 4
            state["wslot"] += 1
            issue_w(n_ + 4)
            for t in range(4):
                bank = state["bank"] % state["nbank"]
                state["bank"] += 1
                for k in range(KC):
                    sc.add("pe", (lambda e, bank=bank, k=k, t=t, slot=slot: e.matmul(
                        pb[bank][:, :], lhsT=wsl[:, slot, k, :], rhs=hT[:, k, 512 * t:512 * t + 512],
                        start=(k == 0), stop=(k == KC - 1))),
                        reads=[("wsl", slot), ("hT", k)], writes=[("pb", bank)])
                    if k == KC - 1:
                        evac(t, bank)
                    if k % 2 == 1:
                        yield

        def in_proj(j, evac):
            for _ in in_proj_units(j, evac):
                pass

        def merge(a_gens, b_gen):
            import itertools
            A = itertools.chain(*a_gens)
            B = b_gen if b_gen is not None else iter(())
            a_done = b_done = False
            while not (a_done and b_done):
                if not b_done:
                    try:
                        next(B)
                    except StopIteration:
                        b_done = True
                for _ in range(8):
                    if not a_done:
                        try:
                            next(A)
                        except StopIteration:
                            a_done = True

        slopes = _slopes()

        if not done:
            sc.barrier()
            state["nbank"] = 2
            state["bank"] = 0
            qTb = [uview(0, BF16, [2048]), uview(4096, BF16, [2048])]
            kTb = [uview(8192, BF16, [2048]), uview(12288, BF16, [2048])]
            vT = uview(16384, BF16, [2048])
            Vaug = uview(20480, BF16, [3, 16, 192])
            oaccb = [uview(38912, F32, [2048]), uview(47104, F32, [2048])]
            opair = uview(55296, F32, [2048])
            pT2 = [uview(63488 + 1024 * i, BF16, [512]) for i in range(4)]
            expb2 = [uview(67584 + 1024 * i, BF16, [512]) for i in range(2)]
            bias8 = [uview(69632 + 1024 * i, F32, [256]) for i in range(2)]
            rden = [uview(71680 + 2048 * i, F32, [512]) for i in range(2)]
            sqa = uview(75776, BF16, [2048])
            zraw = uview(79872, BF16, [2048])

            sc.add("dve", lambda e: e.memset(Vaug[:, :, :, 64:128], 1.0), writes=["Vaug"])
            cnt = {"ss": 0, "blk": 0, "bias": 0, "fin": 0, "zz": 0}
            ORD = ((1, 16), (4, 4), (16, 1))
            LOOKU = 3
            BURST = 3

            def mk_ev_q(pp):
                def ev(t, bank):
                    sc.add("act", (lambda e: e.activation(out=qTb[pp % 2][:, 512 * t:512 * t + 512], in_=pb[bank][:, :], func=AF.Copy)),
                           reads=[("pb", bank)], writes=[("qT", pp % 2, t)])
                return ev

            def mk_ev_k(pp):
                def ev(t, bank):
                    sc.add("dve", (lambda e: e.tensor_copy(out=kTb[pp % 2][:, 512 * t:512 * t + 512], in_=pb[bank][:, :])),
                           reads=[("pb", bank)], writes=[("kT", pp % 2, t)])
                return ev

            def ev_v(t, bank):
                sc.add("act", (lambda e: e.activation(out=vT[:, 512 * t:512 * t + 512], in_=pb[bank][:, :], func=AF.Copy)),
                       reads=[("pb", bank)], writes=[("vT", t)])

            def mk_ev_z(pp):
                def ev(t, bank):
                    tc = slice(512 * t, 512 * t + 512)
                    sc.add("act", (lambda e: e.activation(out=zraw[:, tc], in_=pb[bank][:, :], func=AF.Copy)),
                           reads=[("pb", bank)], writes=[("zraw", t)])
                    if t < 3:
                        return
                    allz = [("zraw", t_) for t_ in range(4)]
                    allo = [("opair", t_, h_) for t_ in range(4) for h_ in range(2)]
                    sc.add("act", (lambda e: e.activation(out=zraw[:, :], in_=zraw[:, :], func=AF.Silu)),
                           reads=allz, writes=allz)
                    sc.add("dve", (lambda e: e.scalar_tensor_tensor(
                        out=yTa[:, pp, :], in0=opair[:, :], scalar=gattn[:, pp:pp + 1], in1=zraw[:, :],
                        op0=ALU.mult, op1=ALU.mult)),
                        reads=allo + allz + ["gattn"], writes=[("yTa", pp, t_) for t_ in range(4)])
                    sc.add("pool", (lambda e: e.tensor_tensor(out=sqa[:, :], in0=opair[:, :], in1=opair[:, :], op=ALU.mult)),
                           reads=allo, writes=["sqa"])
                    for col in range(16):
                        sc.add("pe", (lambda e, col=col: e.matmul(
                            pb[7][:, col:col + 1], lhsT=sqa[:, 128 * col:128 * col + 128], rhs=ones_b[:, 0:1],
                            start=True, stop=True)),
                            reads=["sqa", "ones_b"], writes=[("pb", 7)])
                    if pp == 0:
                        sc.add("dve", lambda e: e.tensor_copy(out=ssq_a, in_=pb[7][:, 0:16]),
                               reads=[("pb", 7)], writes=["ssq_a"])
                    else:
                        sc.add("dve", lambda e: e.tensor_tensor(out=ssq_a, in0=pb[7][:, 0:16], in1=ssq_a, op=ALU.add),
                               reads=[("pb", 7), "ssq_a"], writes=["ssq_a"])
                return ev

            def attention_steps(p):
                qT = qTb[p % 2]
                kT = kTb[p % 2]
                allv = [("vT", t) for t in range(4)]
                for oi, (r, T) in enumerate(ORD):
                    for g in range(2):
                        for cc in range(8):
                            c = 8 * g + cc
                            res, ci = c // T, c % T
                            st0 = res + r * 128 * ci
                            cols = slice(st0, st0 + r * 127 + 1, r)
                            sc.add("pe", (lambda e, cc=cc, cols=cols: e.transpose(
                                psbf(7)[:, 128 * cc:128 * cc + 128], vT[:, cols], ident_b[:, :])),
                                reads=allv + ["ident_b"], writes=[("pb", 7)])
                        src3 = psbf(7).rearrange("p (c f) -> p c f", f=128)
                        sc.add("dve", (lambda e, oi=oi, g=g, src3=src3: e.tensor_copy(
                            out=Vaug[:, oi, 8 * g:8 * g + 8, 0:64], in_=src3[:, :, 0:64])),
                            reads=[("pb", 7)], writes=[("Vaug", oi, g, 0)])
                        sc.add("dve", (lambda e, oi=oi, g=g, src3=src3: e.tensor_copy(
                            out=Vaug[:, oi, 8 * g:8 * g + 8, 128:192], in_=src3[:, :, 64:128])),
                            reads=[("pb", 7)], writes=[("Vaug", oi, g, 1)])
                        if (2 * oi + g) % 3 == 2:
                            yield

                tiles = []
                for hh in range(2):
                    for oi, (r, T) in enumerate(ORD):
                        for res in range(r):
                            for c in range(T):
                                tiles.append(dict(hh=hh, oi=oi, r=r, T=T, res=res, c=c,
                                                  first_of_branch=(res == 0 and c == 0),
                                                  last_of_head=(oi == 2 and res == r - 1 and c == T - 1)))
                grp_bank = {}
                grp_done = {}

                def scores(unit, uidx):
                    tA = unit[0]
                    hh, oi, r = tA["hh"], tA["oi"], tA["r"]
                    h = 2 * p + hh
                    L = S // r
                    R = slice(64 * hh, 64 * hh + 64)
                    if tA["first_of_branch"]:
                        bslot = cnt["bias"] % 2
                        cnt["bias"] += 1
                        coef = -8.0 * slopes[h] * r
                        sc.add("dve", (lambda e: e.scalar_tensor_tensor(
                            out=bias8[bslot], in0=base_f[:, :], scalar=coef, in1=mask8[:, :], op0=ALU.mult, op1=ALU.add)),
                            reads=["base_f", "mask8"], writes=[("bias8", bslot)])
                        for hf in range(2):
                            sc.add("act", (lambda e, hf=hf: e.activation(out=expb2[bslot][:, 256 * hf:256 * hf + 256], in_=bias8[bslot],
                                                                      func=AF.Exp, scale=0.125)),
                                   reads=[("bias8", bslot)], writes=[("expb2", bslot)])
                        cnt["cur_bias", hh, oi] = bslot
                    bslot = cnt["cur_bias", hh, oi]
                    bank = 2 + cnt["ss"] % 3
                    s4 = cnt["ss"] % 4
                    cnt["ss"] += 1
                    lo, hi = None, None
                    for idx, tl in enumerate(unit):
                        T, res, c = tl["T"], tl["res"], tl["c"]
                        q0 = max(0, 128 * c - 64)
                        q1 = min(L, 128 * c + 192)
                        j0 = q0 - (128 * c - 64)
                        nq = q1 - q0
                        kcols = slice(res + r * 128 * c, res + r * (128 * c + 127) + 1, r)
                        qcols = slice(res + r * q0, res + r * (q1 - 1) + 1, r)
                        cb = 256 * idx + j0
                        tl.update(j0=j0, nq=nq, s4=s4, cbase=256 * idx)
                        if lo is None:
                            lo = cb
                        hi = cb + nq
                        ssap = pb[bank][:, cb:cb + nq]
                        tq = sorted({(res + r * q0) // 512, (res + r * (q1 - 1)) // 512} | (set(range(4)) if r > 1 else set()))
                        tk = sorted({(res + r * 128 * c) // 512, (res + r * (128 * c + 127)) // 512} | (set(range(4)) if r > 1 else set()))
                        sc.add("pe", (lambda e, ssap=ssap, kcols=kcols, qcols=qcols: e.matmul(
                            ssap, lhsT=kT[R, kcols], rhs=qT[R, qcols], start=True, stop=True)),
                            reads=[("qT", p % 2, t_) for t_ in tq] + [("kT", p % 2, t_) for t_ in tk], writes=[("pb", bank)])
                    sc.add("act", (lambda e: e.activation(out=pT2[s4][:, lo:hi], in_=pb[bank][:, lo:hi], func=AF.Exp, scale=0.125)),
                           reads=[("pb", bank)], writes=[("pT2", s4)])
                    meng = "dve"
                    sc.add(meng, (lambda e: e.tensor_tensor(out=pT2[s4][:, lo:hi], in0=pT2[s4][:, lo:hi], in1=expb2[bslot][:, lo:hi], op=ALU.mult)),
                           reads=[("pT2", s4), ("expb2", bslot)], writes=[("pT2", s4)])

                def pv(tl):
                    hh, oi, r, T, res, c = tl["hh"], tl["oi"], tl["r"], tl["T"], tl["res"], tl["c"]
                    j0, nq, s4, cbase = tl["j0"], tl["nq"], tl["s4"], tl["cbase"]
                    L = S // r
                    oacc = oaccb[hh]
                    lhsT_v = Vaug[:, oi, res * T + c, 64 * hh:64 * hh + 128]
                    vkeys = [("Vaug", oi, (res * T + c) // 8, 0), ("Vaug", oi, (res * T + c) // 8, 1), "Vaug"]
                    jbase = 128 * c - 64
                    for half in range(2):
                        bI = c + half
                        if T == 1:
                            if half == 1:
                                continue
                            p0, p1, st_, sp_ = 0, L, True, True
                        else:
                            p0 = max(0, 128 * bI - 64)
                            p1 = min(L, 128 * bI + 64)
                            if half == 0:
                                st_, sp_ = (c == 0), True
                            else:
                                st_, sp_ = True, (c == T - 1)
                        pos = p0
                        while pos < p1:
                            nxt = min(p1, (pos // 512 + 1) * 512)
                            if r == 16:
                                gkey = (hh, oi, res // 4, 0)
                                col0 = 128 * (res % 4) + pos
                                extent = 512
                            else:
                                gkey = (hh, oi, res, pos // 512)
                                col0 = pos % 512
                                extent = min(L, (pos // 512 + 1) * 512) - (pos // 512) * 512
                            if gkey not in grp_bank:
                                grp_bank[gkey] = 5 + cnt["blk"] % 2
                                cnt["blk"] += 1
                                grp_done[gkey] = 0
                            bank = grp_bank[gkey]
                            oap = pb[bank][:, col0:col0 + (nxt - pos)]
                            ja, jb = pos - jbase, nxt - jbase
                            sc.add("pe", (lambda e, oap=oap, ja=ja, jb=jb, st_=st_, sp_=sp_: e.matmul(
                                oap, lhsT=lhsT_v, rhs=pT2[s4][:, cbase + ja:cbase + jb], start=st_, stop=sp_)),
                                reads=[("pT2", s4)] + vkeys, writes=[("pb", bank)])
                            if sp_:
                                grp_done[gkey] += nxt - pos
                                if grp_done[gkey] == extent:
                                    if r == 16:
                                        gi = res // 4
                                        oview = oacc.rearrange("p (q k) -> p q k", k=16)[:, :, 4 * gi:4 * gi + 4]
                                        iview = pb[bank][:, 0:512].rearrange("p (k q) -> p q k", k=4)
                                        okeys = [("oacc", hh, t_) for t_ in range(4)]
                                    else:
                                        g0 = (pos // 512) * 512
                                        oview = oacc[:, slice(res + r * g0, res + r * (g0 + extent - 1) + 1, r)]
                                        iview = pb[bank][:, 0:extent]
                                        okeys = [("oacc", hh, t_) for t_ in (range(4) if r > 1 else [g0 // 512])]
                                    if oi == 0:
                                        sc.add("act", (lambda e, oview=oview, iview=iview: e.activation(out=oview, in_=iview, func=AF.Copy)),
                                               reads=[("pb", bank)], writes=okeys)
                                    else:
                                        sc.add("dve", (lambda e, oview=oview, iview=iview: e.tensor_tensor(
                                            out=oview, in0=iview, in1=oview, op=ALU.add)),
                                            reads=[("pb", bank)] + okeys, writes=okeys)
                            pos = nxt

                def finalize(hh):
                    oacc = oaccb[hh]
                    R = slice(64 * hh, 64 * hh + 64)
                    DR = slice(64 * (1 - hh), 64 * (1 - hh) + 64)
                    for t in range(4):
                        fs = cnt["fin"] % 2
                        cnt["fin"] += 1
                        tc = slice(512 * t, 512 * t + 512)
                        sc.add("act", (lambda e, fs=fs, tc=tc: e.activation(out=rden[fs][DR, :], in_=oacc[DR, tc], func=AF.Ln)),
                               reads=[("oacc", hh, t)], writes=[("rden", fs)])
                        sc.add("act", (lambda e, fs=fs: e.activation(out=rden[fs][DR, :], in_=rden[fs][DR, :], func=AF.Exp, scale=-1.0)),
                               reads=[("rden", fs)], writes=[("rden", fs)])
                        sc.add("dve", (lambda e, fs=fs: e.tensor_copy(out=rden[fs][R, :], in_=rden[fs][DR, :])),
                               reads=[("rden", fs)], writes=[("rden", fs)])
                        sc.add("pool", (lambda e, fs=fs, tc=tc: e.tensor_tensor(
                            out=opair[R, tc], in0=oacc[R, tc], in1=rden[fs][R, :], op=ALU.mult)),
                            reads=[("oacc", hh, t), ("rden", fs)], writes=[("opair", t, hh)])

                units = [tiles[i:i + 2] for i in range(0, len(tiles), 2)]
                for un in units:
                    assert (un[0]["hh"], un[0]["oi"]) == (un[1]["hh"], un[1]["oi"])
                n = len(units)
                for i in range(min(LOOKU, n)):
                    scores(units[i], i)
                yield
                for i0 in range(0, n, BURST):
                    for i in range(i0, min(i0 + BURST, n)):
                        for tl in units[i]:
                            pv(tl)
                            if tl["last_of_head"]:
                                finalize(tl["hh"])
                    for i in range(i0 + LOOKU, min(i0 + LOOKU + BURST, n)):
                        scores(units[i], i)
                    yield

            DBG["pair"] = 0
            merge([in_proj_units(32, mk_ev_q(0)), in_proj_units(40, mk_ev_k(0)), in_proj_units(48, ev_v)], None)
            for p in range(npairs):
                DBG["pair"] = p
                a = []
                if p >= 1:
                    a.append(in_proj_units(56 + p - 1, mk_ev_z(p - 1)))
                if p + 1 < npairs:
                    a += [in_proj_units(32 + p + 1, mk_ev_q(p + 1)), in_proj_units(40 + p + 1, mk_ev_k(p + 1)),
                          in_proj_units(48 + p + 1, ev_v)]
                merge(a, attention_steps(p))
            in_proj(56 + npairs - 1, mk_ev_z(npairs - 1))
            state["nbank"] = 3

            if stop_after == "yTa":
                dump(yTa[:, :, :], 8)
                done = True

        if not done:
            sc.barrier()
            u_sb = uview(0, BF16, [2048])
            cu = uview(4096, F32, [2050])
            acc = uview(12800, F32, [2048])
            yraw = uview(20992, F32, [2048])
            zsc = [uview(29184 + 1024 * i, BF16, [512]) for i in range(2)]
            sqc = [uview(31232 + 1024 * i, BF16, [512]) for i in range(2)]
            yTc = uview(34816, BF16, [8, 2048])
            sc.add("dve", lambda e: e.memset(cu[:, 0:1], 0.0), writes=[("cu", 0)])
            sc.add("dve", lambda e: e.memset(cu[:, 2049:2050], 0.0), writes=[("cu", 3)])
            cz = {"zz": 0}
            allcu = [("cu", t) for t in range(4)]
            allacc = [("acc", t) for t in range(4)]
            for f in range(8):
                def ev_u(t, bank):
                    sc.add("act", (lambda e, t=t, bank=bank: e.activation(out=u_sb[:, 512 * t:512 * t + 512], in_=pb[bank][:, :],
                                                                          func=AF.Copy)),
                           reads=[("pb", bank)], writes=[("u_sb", t)])

                def ev_cg(t, bank):
                    sc.add("dve", (lambda e, t=t, bank=bank: e.tensor_tensor(
                        out=cu[:, 1 + 512 * t:1 + 512 * t + 512], in0=pb[bank][:, :], in1=u_sb[:, 512 * t:512 * t + 512],
                        op=ALU.mult)),
                        reads=[("pb", bank), ("u_sb", t)], writes=[("cu", t)])

                in_proj(f, ev_u)
                in_proj(16 + f, ev_cg)
                sc.add("act", (lambda e, f=f: e.activation(out=acc[:, :], in_=cu[:, 1:2049], func=AF.Identity,
                                                           bias=convb[:, f:f + 1], scale=convw[:, f, 1:2])),
                       reads=allcu + ["convw", "convb"], writes=allacc)
                sc.add("dve", (lambda e, f=f: e.scalar_tensor_tensor(out=acc[:, :], in0=cu[:, 0:2048], scalar=convw[:, f, 0:1],
                                                                    in1=acc[:, :], op0=ALU.mult, op1=ALU.add)),
                       reads=allcu + allacc + ["convw"], writes=allacc)
                sc.add("dve", (lambda e, f=f: e.scalar_tensor_tensor(out=acc[:, :], in0=cu[:, 2:2050], scalar=convw[:, f, 2:3],
                                                                     in1=acc[:, :], op0=ALU.mult, op1=ALU.add)),
                       reads=allcu + allacc + ["convw"], writes=allacc)

                def ev_bg(t, bank):
                    sc.add("dve", (lambda e, t=t, bank=bank: e.tensor_tensor(
                        out=yraw[:, 512 * t:512 * t + 512], in0=pb[bank][:, :], in1=acc[:, 512 * t:512 * t + 512], op=ALU.mult)),
                        reads=[("pb", bank), ("acc", t)], writes=[("yraw", t)])

                in_proj(8 + f, ev_bg)

                def ev_zc(t, bank, f=f):
                    zs_ = cz["zz"] % 2
                    cz["zz"] += 1
                    tc = slice(512 * t, 512 * t + 512)
                    sc.add("act", (lambda e, zs_=zs_, bank=bank: e.activation(out=zsc[zs_], in_=pb[bank][:, :], func=AF.Silu)),
                           reads=[("pb", bank)], writes=[("zsc", zs_)])
                    sc.add("dve", (lambda e, zs_=zs_, tc=tc: e.scalar_tensor_tensor(
                        out=yTc[:, f, tc], in0=yraw[:, tc], scalar=gconv[:, f:f + 1], in1=zsc[zs_],
                        op0=ALU.mult, op1=ALU.mult)),
                        reads=[("yraw", t), ("zsc", zs_), "gconv"], writes=[("yTc", f, t)])
                    sc.add("pool", (lambda e, zs_=zs_, tc=tc: e.tensor_tensor(
                        out=sqc[zs_], in0=yraw[:, tc], in1=yraw[:, tc], op=ALU.mult)),
                        reads=[("yraw", t)], writes=[("sqc", zs_)])
                    for i4 in range(4):
                        col = 16 + 4 * t + i4
                        sc.add("pe", (lambda e, zs_=zs_, i4=i4, col=col: e.matmul(
                            pb[7][:, col:col + 1], lhsT=sqc[zs_][:, 128 * i4:128 * i4 + 128], rhs=ones_b[:, 0:1],
                            start=True, stop=True)),
                            reads=[("sqc", zs_), "ones_b"], writes=[("pb", 7)])

                in_proj(24 + f, ev_zc)
                if f == 0:
                    sc.add("dve", lambda e: e.tensor_copy(out=ssq_c, in_=pb[7][:, 16:32]), reads=[("pb", 7)], writes=["ssq_c"])
                else:
                    sc.add("dve", lambda e: e.tensor_tensor(out=ssq_c, in0=pb[7][:, 16:32], in1=ssq_c, op=ALU.add),
                           reads=[("pb", 7), "ssq_c"], writes=["ssq_c"])

            if stop_after == "yTc":
                sc.barrier()
                sc.add("sp", lambda e: e.dma_start(out=dbg_d[:, 0:8, :], in_=yTc[:, :, :]), dma_key="out")
                sc.add("sp", lambda e: e.dma_start(out=dbg_d[:, 8:16, :], in_=yTa[:, :, :]), dma_key="out")
                done = True

        if not done:
            sc.barrier()
            xt = [uview(8192 * i, F32, [2048]) for i in range(2)]
            yv = [uview(16384 + 8192 * i, F32, [2048]) for i in range(2)]
            junk = uview(32768, BF16, [512])
            gg = wsl[:, :, :, :].rearrange("p a b c -> p (a b c)").bitcast(F32)[:, 0:2048]
            wout = hT
            for src, dst, nm in ((ssq_a, ra, "ra"), (ssq_c, rc, "rc")):
                sc.add("dve", (lambda e, src=src, dst=dst: e.tensor_scalar(out=dst, in0=src, scalar1=1.0 / 1024.0, scalar2=EPS,
                                                                           op0=ALU.mult, op1=ALU.add)), writes=[nm])
                sc.add("act", (lambda e, dst=dst: e.activation(out=dst, in_=dst, func=AF.Sqrt)), reads=[nm], writes=[nm])
                sc.add("dve", (lambda e, dst=dst: e.reciprocal(out=dst, in_=dst)), reads=[nm], writes=[nm])
            for k in range(KC):
                sc.add("pool", (lambda e, k=k: e.dma_start(out=wout[:, k, :], in_=wout_d[128 * k:128 * k + 128, :])),
                       writes=[("wout", k)], dma_key=("wout", k))
            sc.add("sp", lambda e: e.dma_start(out=gg, in_=gpost_d[:, :].partition_broadcast(128)), writes=["gg"], dma_key="gg")
            dg = [yv[1][:, 0:128], yv[1][:, 128:256]]
            for k in range(KC):
                d2 = k % 2
                sc.add("dve", (lambda e, k=k, d2=d2: e.tensor_scalar(out=dg[d2], in0=ident_f[:, :], scalar1=gate_lay[:, k:k + 1],
                                                                    scalar2=None, op0=ALU.mult)),
                       reads=["ident_f", "gate_lay"], writes=[("dg", d2)])
                sc.add("pe", (lambda e, k=k, d2=d2: e.matmul(pb[k // 4][:, 128 * (k % 4):128 * (k % 4) + 128], lhsT=ones_f[:, :],
                                                            rhs=dg[d2], start=True, stop=True)),
                       reads=[("dg", d2), "ones_f"], writes=[("pb", k // 4)])
                if k % 4 == 3:
                    g4 = k // 4
                    sc.add("dve", (lambda e, g4=g4: e.tensor_tensor(out=gg[:, 512 * g4:512 * g4 + 512], in0=pb[g4][:, :],
                                                                   in1=gg[:, 512 * g4:512 * g4 + 512], op=ALU.mult)),
                           reads=[("pb", g4), "gg"], writes=["gg"])
            junk = wsl[:, :, :, :].rearrange("p a b c -> p (a b c)")[:, 4096:6144]

            def p3_mm(i, hf):
                tcol = slice(128 * i, 128 * i + 128)
                for grp in range(2):
                    for n in range(2):
                        bank = 4 * hf + 2 * grp + n
                        dc = slice(1024 * hf + 512 * n, 1024 * hf + 512 * n + 512)
                        for k in range(8 * grp, 8 * grp + 8):
                            lhsT = yTc[:, k, tcol] if k < 8 else yTa[:, k - 8, tcol]
                            sc.add("pe", (lambda e, bank=bank, lhsT=lhsT, k=k, dc=dc: e.matmul(
                                pb[bank][:, :], lhsT=lhsT, rhs=wout[:, k, dc], start=(k % 8 == 0), stop=(k % 8 == 7))),
                                reads=[("wout", k)], writes=[("pb", bank)])

            def p3_evac(i, hf):
                b2 = i % 2
                for n in range(2):
                    dc = slice(1024 * hf + 512 * n, 1024 * hf + 512 * n + 512)
                    q4 = 2 * hf + n
                    extra = [("dg", 0), ("dg", 1)] if b2 == 1 else []
                    sc.add("act", (lambda e, hf=hf, n=n, dc=dc: e.activation(
                        out=yv[b2][:, dc], in_=pb[4 * hf + n][:, :], func=AF.Identity, scale=rc[:, i:i + 1])),
                        reads=[("pb", 4 * hf + n), "rc"] + extra, writes=[("yv", b2, q4)] + extra)
                    sc.add("dve", (lambda e, hf=hf, n=n, dc=dc: e.scalar_tensor_tensor(
                        out=yv[b2][:, dc], in0=pb[4 * hf + 2 + n][:, :], scalar=ra[:, i:i + 1], in1=yv[b2][:, dc],
                        op0=ALU.mult, op1=ALU.add)),
                        reads=[("pb", 4 * hf + 2 + n), "ra", ("yv", b2, q4)], writes=[("yv", b2, q4)])

            def p3_post(i):
                b2 = i % 2
                allyv = [("yv", b2, q4) for q4 in range(4)]
                sc.add("act", (lambda e: e.activation(out=junk, in_=yv[b2], func=AF.Square, accum_out=ssq_p1[:, i:i + 1])),
                       reads=allyv, writes=["junk", ("rp", i)])
                sc.add("dve", (lambda e: e.tensor_scalar(out=rp[:, i:i + 1], in0=ssq_p1[:, i:i + 1], scalar1=1.0 / D,
                                                         scalar2=EPS, op0=ALU.mult, op1=ALU.add)),
                       reads=[("rp", i)], writes=[("rp", i)])
                sc.add("act", (lambda e: e.activation(out=rp[:, i:i + 1], in_=rp[:, i:i + 1], func=AF.Sqrt)),
                       reads=[("rp", i)], writes=[("rp", i)])
                sc.add("dve", (lambda e: e.reciprocal(out=rp[:, i:i + 1], in_=rp[:, i:i + 1])),
                       reads=[("rp", i)], writes=[("rp", i)])
                sc.add("dve", (lambda e: e.scalar_tensor_tensor(
                    out=yv[b2], in0=yv[b2], scalar=rp[:, i:i + 1], in1=gg, op0=ALU.mult, op1=ALU.mult)),
                    reads=allyv + [("rp", i), "gg"], writes=allyv)
                sc.add("pool", (lambda e: e.tensor_tensor(out=xt[b2], in0=yv[b2], in1=xt[b2], op=ALU.add)),
                       reads=allyv + [("xt", b2)], writes=[("xt", b2)])
                sc.add("act", (lambda e: e.dma_start(out=out_d[128 * i:128 * i + 128, :], in_=xt[b2])),
                       reads=[("xt", b2)], writes=[("outdram", i)], dma_key=("out", b2))

            for i in range(16):
                b2 = i % 2
                sc.add("sp", (lambda e, i=i, b2=b2: e.dma_start(out=xt[b2], in_=x_d[128 * i:128 * i + 128, :])),
                       writes=[("xt", b2)], dma_key=("xt", b2))
                p3_mm(i, 0)
                p3_evac(i, 0)
                if i > 0:
                    p3_post(i - 1)
                p3_mm(i, 1)
                p3_evac(i, 1)
            p3_post(15)

        sc.barrier()

        sc.finalize()
        dma_keys = list(sc.dma_cnt.keys())
        eng_sems = {e: es.enter_context(nc.semaphore(f"sem_{e}")) for e in Sched.ENGS}
        dma_sems = {k: es.enter_context(nc.semaphore(f"dsem_{i}")) for i, k in enumerate(dma_keys)}
        block = es.enter_context(nc.Block())

        @block.tensor
        def _(e):
            sc.emit("pe", e, eng_sems, dma_sems)

        @block.scalar
        def _(e):
            sc.emit("act", e, eng_sems, dma_sems)

        @block.vector
        def _(e):
            sc.emit("dve", e, eng_sems, dma_sems)

        @block.gpsimd
        def _(e):
            sc.emit("pool", e, eng_sems, dma_sems)

        @block.sync
        def _(e):
            sc.emit("sp", e, eng_sems, dma_sems)

    return nc


def _lay(v, nchunk):
    return np.ascontiguousarray(np.asarray(v, np.float32).reshape(nchunk, 128).T)


def make_in_maps(x, c, w_ada, b_ada, g_pre, w_in, conv_w, conv_b, g_conv, g_attn, w_out, g_post):
    x = np.asarray(x, np.float32)
    c = np.asarray(c, np.float32)
    w_ada0 = np.ascontiguousarray(np.asarray(w_ada, np.float32)[0])
    w_in0 = np.ascontiguousarray(np.asarray(w_in, np.float32)[0])
    w_out0 = np.ascontiguousarray(np.asarray(w_out, np.float32)[0])
    cw = np.asarray(conv_w, np.float32)[0]
    convw_lay = np.ascontiguousarray(cw.reshape(3, 8, 128).transpose(2, 1, 0))
    shared = {
        "w_ada": w_ada0,
        "b_row": np.ascontiguousarray(np.asarray(b_ada, np.float32)[0].reshape(1, NMOD)),
        "gpre_lay": _lay(np.asarray(g_pre)[0], 16),
        "w_in": w_in0,
        "convw_lay": convw_lay,
        "convb_lay": _lay(np.asarray(conv_b)[0], 8),
        "gconv_lay": _lay(np.asarray(g_conv)[0], 8),
        "gattn_lay": _lay(np.asarray(g_attn)[0], 8),
        "w_out": w_out0,
        "gpost_row": np.ascontiguousarray(np.asarray(g_post, np.float32)[0].reshape(1, D)),
    }
    maps = []
    for b in range(N_CORES):
        m = dict(shared)
        m["x"] = np.ascontiguousarray(x[b])
        m["c_lay"] = _lay(c[b], 16)
        maps.append(m)
    return maps


_NC_CACHE = {}


def kernel(x, c, w_ada, b_ada, g_pre, w_in, conv_w, conv_b, g_conv, g_attn, w_out, g_post):
    if "nc" not in _NC_CACHE:
        _NC_CACHE["nc"] = build_nc()
    nc = _NC_CACHE["nc"]
    in_maps = make_in_maps(x, c, w_ada, b_ada, g_pre, w_in, conv_w, conv_b, g_conv, g_attn, w_out, g_post)
    res = run_bass_kernel_spmd(nc, in_maps, core_ids=list(range(N_CORES)))
    out = np.stack([np.asarray(r["out"], np.float32) for r in res.results], axis=0)
    return out
```

```python
import numpy as np
import concourse.bass as bass
import concourse.mybir as mybir
from concourse.bass_utils import run_bass_kernel_spmd

F32 = mybir.dt.float32
BF16 = mybir.dt.bfloat16
AF = mybir.ActivationFunctionType
ALU = mybir.AluOpType
AX = mybir.AxisListType

D = 2048
S = 2048
NPROJ = 8192
NMOD = 6144
KC = 16
EPS = 1e-6
NEG8 = -8.0 * 30000.0
N_CORES = 8
DBG = {"pair": -1}
SCHED = {"burst": 1, "a_per_step": 3, "scores_first": True}


class _Op:
    __slots__ = ("eng", "fn", "deps", "dma_key", "dma_n", "signal", "sig")

    def __init__(self, eng, fn):
        self.eng = eng
        self.fn = fn
        self.deps = []
        self.dma_key = None
        self.dma_n = 0
        self.signal = False
        self.sig = 0


class _DmaTok:
    __slots__ = ("key", "n")

    def __init__(self, key, n):
        self.key = key
        self.n = n


class Sched:
    ENGS = ("pe", "act", "dve", "pool", "sp")

    def __init__(self):
        self.streams = {e: [] for e in self.ENGS}
        self.res_w = {}
        self.res_r = {}
        self.dma_cnt = {}
        self.last = {}

    @staticmethod
    def _grp(tok):
        return ("dma", tok.key) if isinstance(tok, _DmaTok) else tok.eng

    def add(self, eng, fn, reads=(), writes=(), dma_key=None, extra_deps=()):
        op = _Op(eng, fn)
        deps = {}
        for r in reads:
            for t in self.res_w.get(r, {}).values():
                deps[id(t)] = t
        for w in writes:
            for t in self.res_w.get(w, {}).values():
                deps[id(t)] = t
            for t in self.res_r.get(w, {}).values():
                deps[id(t)] = t
        for t in extra_deps:
            deps[id(t)] = t
        if dma_key is not None:
            n = self.dma_cnt.get(dma_key, 0) + 1
            self.dma_cnt[dma_key] = n
            op.dma_key = dma_key
            op.dma_n = n
            tok = _DmaTok(dma_key, n)
        else:
            tok = op
        g = self._grp(tok)
        for r in reads:
            self.res_r.setdefault(r, {})[g] = tok
        for w in writes:
            self.res_w[w] = {g: tok}
            self.res_r[w] = {}
        deps.pop(id(op), None)
        op.deps = list(deps.values())
        self.streams[eng].append(op)
        if fn is not None:
            self.last[g] = tok
        return tok

    def barrier(self):
        toks = list(self.last.values())
        for e in self.ENGS:
            self.add(e, None, extra_deps=toks)

    def finalize(self):
        for e in self.ENGS:
            for op in self.streams[e]:
                for d in op.deps:
                    if isinstance(d, _Op):
                        if d.eng == "pe" and op.eng == "pe":
                            continue
                        d.signal = True
        for e in self.ENGS:
            c = 0
            for op in self.streams[e]:
                if op.signal:
                    c += 1
                    op.sig = c

    def emit(self, eng_name, e, eng_sems, dma_sems):
        waited = {}
        for op in self.streams[eng_name]:
            need = {}
            for d in op.deps:
                if isinstance(d, _Op):
                    if d.eng == "pe" and eng_name == "pe":
                        continue
                    g, v = d.eng, d.sig
                else:
                    g = ("dma", d.key)
                    v = 16 * (self.dma_cnt[d.key] if d.key == "const" else d.n)
                if v > need.get(g, 0):
                    need[g] = v
            for g, v in need.items():
                if waited.get(g, 0) >= v:
                    continue
                waited[g] = v
                sem = dma_sems[g[1]] if isinstance(g, tuple) else eng_sems[g]
                e.wait_ge(sem, v)
            if op.fn is None:
                continue
            ins = op.fn(e)
            if op.dma_key is not None:
                ins.then_inc(dma_sems[op.dma_key], 16)
            elif op.signal:
                ins.then_inc(eng_sems[eng_name], 1)


def _slopes():
    return [2.0 ** (-8.0 * (h + 1) / 16.0) for h in range(16)]


def build_nc(stop_after=None, npairs=8):
    nc = bass.Bass("TRN2", target_bir_lowering=False)
    x_d = nc.dram_tensor("x", [S, D], F32, kind="ExternalInput").ap()
    c_d = nc.dram_tensor("c_lay", [128, KC], F32, kind="ExternalInput").ap()
    wada_d = nc.dram_tensor("w_ada", [D, NMOD], F32, kind="ExternalInput").ap()
    bada_d = nc.dram_tensor("b_row", [1, NMOD], F32, kind="ExternalInput").ap()
    gpre_d = nc.dram_tensor("gpre_lay", [128, KC], F32, kind="ExternalInput").ap()
    win_d = nc.dram_tensor("w_in", [D, NPROJ], F32, kind="ExternalInput").ap()
    convw_d = nc.dram_tensor("convw_lay", [128, 8, 3], F32, kind="ExternalInput").ap()
    convb_d = nc.dram_tensor("convb_lay", [128, 8], F32, kind="ExternalInput").ap()
    gconv_d = nc.dram_tensor("gconv_lay", [128, 8], F32, kind="ExternalInput").ap()
    gattn_d = nc.dram_tensor("gattn_lay", [128, 8], F32, kind="ExternalInput").ap()
    wout_d = nc.dram_tensor("w_out", [D, D], F32, kind="ExternalInput").ap()
    gpost_d = nc.dram_tensor("gpost_row", [1, D], F32, kind="ExternalInput").ap()
    out_d = nc.dram_tensor("out", [S, D], F32, kind="ExternalOutput").ap()
    dbg_d = None
    if stop_after is not None:
        dbg_d = nc.dram_tensor("dbg", [128, 16, 2048], BF16, kind="ExternalOutput").ap()

    sc = Sched()
    UF = 23040

    import contextlib
    with contextlib.ExitStack() as es:
        def sb(name, shape, dt):
            return es.enter_context(nc.sbuf_tensor(name, shape, dt))

        hT = sb("hT", [128, KC, 2048], BF16)
        yTa = sb("yTa", [128, 8, 2048], BF16)
        U = sb("U", [128, UF], F32)
        wsl = sb("wsl", [128, 3, KC, 128], BF16)
        ident_f = sb("ident_f", [128, 128], F32)
        ident_b = sb("ident_b", [128, 128], BF16)
        ones_f = sb("ones_f", [128, 128], F32)
        ones_b = sb("ones_b", [128, 128], BF16)
        base_f = sb("base_f", [128, 256], F32)
        mask8 = sb("mask8", [128, 256], F32)
        sm = sb("sm", [128, 512], F32)
        cab = sb("cab", [128, KC], BF16)
        pb = [es.enter_context(nc.psum_tensor(f"pb{i}", [128, 512], F32)) for i in range(8)]

        c_sb = sm[:, 0:16]
        gpre = sm[:, 16:32]
        a_sb = sm[:, 32:48]
        s_sb = sm[:, 48:64]
        gate_lay = sm[:, 64:80]
        convw = sm[:, 80:104].rearrange("p (f t) -> p f t", t=3)
        convb = sm[:, 104:112]
        gconv = sm[:, 112:120]
        gattn = sm[:, 120:128]
        ssq_x = sm[:, 128:144]
        rstd_x = sm[:, 144:160]
        ssq_a = sm[:, 160:176]
        ssq_c = sm[:, 176:192]
        ra = sm[:, 192:208]
        rc = sm[:, 208:224]
        ssq_p = sm[:, 224:288]
        ssq_p1 = sm[:, 288:304]
        rp = sm[:, 304:320]
        tmp16 = sm[:, 320:336]

        def uview(off, dt, shape):
            n = 1
            for s_ in shape:
                n *= s_
            nb = n * (4 if dt == F32 else 2)
            assert off % 4 == 0 and nb % 4 == 0 and off + nb <= UF * 4, (off, nb)
            a = U[:, off // 4:(off + nb) // 4]
            if dt != F32:
                a = a.bitcast(dt)
            if len(shape) == 2:
                return a.rearrange("p (a b) -> p a b", b=shape[1])
            if len(shape) == 3:
                return a.rearrange("p (a b c) -> p a b c", b=shape[1], c=shape[2])
            return a

        def psbf(bank):
            return pb[bank][:, :].bitcast(BF16)

        sc.add("pool", lambda e: e.iota(base_f[:, :], [[1, 256]], base=-64, channel_multiplier=-1, allow_small_or_imprecise_dtypes=True),
               writes=["base_f"])
        sc.add("dve", lambda e: e.tensor_single_scalar(out=ident_f[:, :], in_=base_f[:, 64:192], scalar=0.0, op=ALU.is_equal),
               reads=["base_f"], writes=["ident_f"])
        sc.add("dve", lambda e: e.tensor_copy(out=ident_b[:, :], in_=ident_f[:, :]), reads=["ident_f"], writes=["ident_b"])
        sc.add("dve", lambda e: e.memset(ones_f[:, :], 1.0), writes=["ones_f"])
        sc.add("dve", lambda e: e.memset(ones_b[:, :], 1.0), writes=["ones_b"])
        sc.add("dve", lambda e: e.tensor_scalar(out=mask8[:, :], in0=base_f[:, :], scalar1=-1.0, scalar2=None, op0=ALU.mult),
               reads=["ident_f", "base_f"], writes=["mask8"])
        sc.add("dve", lambda e: e.tensor_tensor(out=base_f[:, :], in0=base_f[:, :], in1=mask8[:, :], op=ALU.max),
               reads=["mask8"], writes=["base_f"])
        sc.add("dve", lambda e: e.tensor_scalar(out=mask8[:, :], in0=base_f[:, :], scalar1=64.5, scalar2=NEG8,
                                                op0=ALU.is_gt, op1=ALU.mult),
               reads=["base_f"], writes=["mask8"])

        def cdma(dst, src, key):
            sc.add("sp", lambda e: e.dma_start(out=dst, in_=src), writes=[key], dma_key="const")

        cdma(c_sb, c_d[:, :], "c_sb")
        cdma(gpre, gpre_d[:, :], "gpre")
        cdma(convw, convw_d[:, :, :], "convw")
        cdma(convb, convb_d[:, :], "convb")
        cdma(gconv, gconv_d[:, :], "gconv")
        cdma(gattn, gattn_d[:, :], "gattn")

        def dump_small():
            sc.barrier()
            sc.add("sp", lambda e: e.dma_start(out=dbg_d[:, 0, 0:256], in_=sm[:, 0:256].bitcast(BF16)[:, 0:256]), dma_key="out")
            sc.add("sp", lambda e: e.dma_start(out=out_d[0:128, 0:256], in_=base_f[:, :]), dma_key="out")
            sc.add("sp", lambda e: e.dma_start(out=out_d[128:256, 0:256], in_=mask8[:, :]), dma_key="out")
            sc.add("sp", lambda e: e.dma_start(out=out_d[256:384, 0:128], in_=ident_f[:, :]), dma_key="out")
            sc.add("sp", lambda e: e.dma_start(out=out_d[384:512, 0:512], in_=sm[:, :]), dma_key="out")
        done = False
        if stop_after == "const":
            dump_small()
            done = True
        mod_row = U[0:1, 0:NMOD]
        b_row = U[0:1, NMOD:2 * NMOD]
        xs = [uview(2 * NMOD * 4 + 8192 * i, F32, [2048]) for i in range(2)]
        xn = [yTa[:, i, :] for i in range(2)]
        if not done:
          cdma(b_row, bada_d[:, :], "b_row")
          sc.add("act", lambda e: e.activation(out=cab[:, :], in_=c_sb, func=AF.Silu), reads=["c_sb"], writes=["cab"])

          def wa_slot(s_):
              return yTa[:, 2 + 2 * s_:4 + 2 * s_, :].rearrange("p a (b n) -> p (a b) n", n=256)

          def emit_slice(s_):
              slot = s_ % 3
              dst = wa_slot(slot)
              src = wada_d[:, 256 * s_:256 * s_ + 256].rearrange("(k p) n -> p k n", p=128)
              sc.add("pool", (lambda e: e.dma_start(out=dst, in_=src)), writes=[("waslot", slot)], dma_key=("wa", slot))
              bank = s_ % 2
              for k in range(KC):
                  sc.add("pe", (lambda e, k=k: e.matmul(
                      pb[bank][0:1, 0:256], lhsT=cab[:, k:k + 1], rhs=dst[:, k, :], start=(k == 0), stop=(k == KC - 1))),
                      reads=[("waslot", slot), "cab"], writes=[("pb", bank)])
              sc.add("dve", (lambda e: e.tensor_tensor(
                  out=mod_row[:, 256 * s_:256 * s_ + 256], in0=pb[bank][0:1, 0:256],
                  in1=b_row[:, 256 * s_:256 * s_ + 256], op=ALU.add)),
                  reads=[("pb", bank), "b_row"], writes=[("mod", s_)])

          def emit_tile(i):
              b2 = i % 2
              sc.add("sp", (lambda e: e.dma_start(out=xs[b2], in_=x_d[128 * i:128 * i + 128, :])),
                     writes=[("xs", b2)], dma_key=("xs", b2))
              sc.add("act", (lambda e: e.activation(out=xn[b2], in_=xs[b2], func=AF.Square, accum_out=ssq_x[:, i:i + 1])),
                     reads=[("xs", b2)], writes=[("xn", b2), ("ssq_x", i)])
              sc.add("dve", (lambda e: e.tensor_scalar(out=rstd_x[:, i:i + 1], in0=ssq_x[:, i:i + 1], scalar1=1.0 / D,
                                                       scalar2=EPS, op0=ALU.mult, op1=ALU.add)),
                     reads=[("ssq_x", i)], writes=[("rstd_x", i)])
              sc.add("act", (lambda e: e.activation(out=rstd_x[:, i:i + 1], in_=rstd_x[:, i:i + 1], func=AF.Sqrt)),
                     reads=[("rstd_x", i)], writes=[("rstd_x", i)])
              sc.add("dve", (lambda e: e.reciprocal(out=rstd_x[:, i:i + 1], in_=rstd_x[:, i:i + 1])),
                     reads=[("rstd_x", i)], writes=[("rstd_x", i)])
              sc.add("dve", (lambda e: e.tensor_scalar(out=xn[b2], in0=xs[b2], scalar1=rstd_x[:, i:i + 1],
                                                       scalar2=None, op0=ALU.mult)),
                     reads=[("xs", b2), ("rstd_x", i)], writes=[("xn", b2)])
              for g in range(2):
                  bank = 4 + 2 * b2 + g
                  for kk in range(8):
                      k = 8 * g + kk
                      sc.add("pe", (lambda e, kk=kk, k=k, bank=bank: e.transpose(
                          psbf(bank)[:, 128 * kk:128 * kk + 128], xn[b2][:, 128 * k:128 * k + 128], ident_b[:, :])),
                          reads=[("xn", b2), "ident_b"], writes=[("pb", bank)])
                  src3 = psbf(bank).rearrange("p (c f) -> p c f", f=128)
                  dst3 = hT[:, 8 * g:8 * g + 8, 128 * i:128 * i + 128]
                  hkeys = [("hT", k) for k in range(8 * g, 8 * g + 8)]
                  sc.add("dve", (lambda e, src3=src3, dst3=dst3: e.tensor_copy(out=dst3, in_=src3)),
                         reads=[("pb", bank)], writes=hkeys)

          n_sl = 0
          for i in range(16):
              emit_tile(i)
              while n_sl < (24 * (i + 1)) // 16:
                  emit_slice(n_sl)
                  n_sl += 1
          for j in range(48):
              sc.add("pe", (lambda e, j=j: e.matmul(pb[2][:, j:j + 1], lhsT=mod_row[:, 128 * j:128 * j + 128],
                                                    rhs=ones_f[0:1, 0:1], start=True, stop=True)),
                     reads=[("mod", j // 2), "ones_f"], writes=[("pb", 2)])
          sc.add("dve", lambda e: e.tensor_copy(out=s_sb, in_=pb[2][:, 0:16]), reads=[("pb", 2)], writes=["s_sb"])
          sc.add("dve", lambda e: e.scalar_tensor_tensor(out=a_sb, in0=pb[2][:, 16:32], scalar=1.0, in1=gpre,
                                                         op0=ALU.add, op1=ALU.mult),
                 reads=[("pb", 2), "gpre"], writes=["a_sb"])
          sc.add("dve", lambda e: e.tensor_copy(out=gate_lay, in_=pb[2][:, 32:48]), reads=[("pb", 2)], writes=["gate_lay"])
          for k in range(KC):
              if k % 2 == 0:
                  sc.add("dve", (lambda e, k=k: e.tensor_scalar(out=hT[:, k, :], in0=hT[:, k, :], scalar1=a_sb[:, k:k + 1],
                                                                scalar2=s_sb[:, k:k + 1], op0=ALU.mult, op1=ALU.add)),
                         reads=[("hT", k), "a_sb", "s_sb"], writes=[("hT", k)])
              else:
                  sc.add("act", (lambda e, k=k: e.activation(out=hT[:, k, :], in_=hT[:, k, :], func=AF.Identity,
                                                             bias=s_sb[:, k:k + 1], scale=a_sb[:, k:k + 1])),
                         reads=[("hT", k), "a_sb", "s_sb"], writes=[("hT", k)])

        def dump(src_ap, nk):
            sc.barrier()
            sc.add("sp", lambda e: e.dma_start(out=dbg_d[:, 0:nk, :], in_=src_ap), dma_key="out")
            sc.add("sp", lambda e: e.dma_start(out=out_d[0:128, :], in_=xs[0]), dma_key="out")

        if stop_after == "mod" and not done:
            dump_small()
            done = True
        if stop_after == "hT" and not done:
            dump(hT[:, :, :], 16)
            done = True

        state = {"wslot": 0, "bank": 0, "nbank": 3}

        order = [32, 40, 48]
        for p_ in range(npairs):
            if p_ >= 1:
                order.append(56 + p_ - 1)
            if p_ + 1 < npairs:
                order += [32 + p_ + 1, 40 + p_ + 1, 48 + p_ + 1]
        order.append(56 + npairs - 1)
        for f_ in range(8):
            order += [f_, 16 + f_, 8 + f_, 24 + f_]
        state["issued"] = 0

        def issue_w(upto):
            while state["issued"] < min(upto, len(order)):
                n_ = state["issued"]
                j_ = order[n_]
                slot_ = n_ % 3
                src = win_d[:, 128 * j_:128 * j_ + 128].rearrange("(k p) n -> p k n", p=128)
                sc.add("pool", (lambda e, slot_=slot_, src=src: e.dma_start(out=wsl[:, slot_], in_=src)),
                       writes=[("wsl", slot_)], dma_key=("wsl", slot_))
                state["issued"] += 1

        def in_proj_units(j, evac):
            n_ = state["wslot"]
            assert order[n_] == j, (n_, j, order[n_])
            slot = n_ % 3
            state["wslot"] += 1
            issue_w(n_ + 3)
            for t in range(4):
                bank = state["bank"] % state["nbank"]
                state["bank"] += 1
                for k in range(KC):
                    sc.add("pe", (lambda e, bank=bank, k=k, t=t, slot=slot: e.matmul(
                        pb[bank][:, :], lhsT=wsl[:, slot, k, :], rhs=hT[:, k, 512 * t:512 * t + 512],
                        start=(k == 0), stop=(k == KC - 1))),
                        reads=[("wsl", slot), ("hT", k)], writes=[("pb", bank)])
                    if k == KC - 1:
                        evac(t, bank)
                    if k % 2 == 1:
                        yield

        def in_proj(j, evac):
            for _ in in_proj_units(j, evac):
                pass

        def merge(a_gens, b_gen):
            import itertools
            A = itertools.chain(*a_gens)
            B = b_gen if b_gen is not None else iter(())
            a_done = b_done = False
            while not (a_done and b_done):
                if not b_done:
                    try:
                        next(B)
                    except StopIteration:
                        b_done = True
                for _ in range(SCHED["a_per_step"]):
                    if not a_done:
                        try:
                            next(A)
                        except StopIteration:
                            a_done = True

        slopes = _slopes()

        if not done:
            sc.barrier()
            state["nbank"] = 2
            state["bank"] = 0
            qTb = [uview(0, BF16, [2048]), uview(4096, BF16, [2048])]
            kTb = [uview(8192, BF16, [2048]), uview(12288, BF16, [2048])]
            vT = uview(16384, BF16, [2048])
            Vaug = uview(20480, BF16, [3, 16, 192])
            oaccb = [uview(38912, F32, [2048]), uview(47104, F32, [2048])]
            opair = uview(55296, F32, [2048])
            pT2 = [uview(63488 + 1024 * i, BF16, [512]) for i in range(4)]
            expb2 = [uview(67584 + 1024 * i, BF16, [512]) for i in range(2)]
            bias8 = [uview(69632 + 1024 * i, F32, [256]) for i in range(2)]
            rden = [uview(71680 + 2048 * i, F32, [512]) for i in range(2)]
            sqa = uview(75776, BF16, [2048])
            zraw = uview(79872, BF16, [2048])
            q4b = uview(83968, BF16, [2048])
            q16b = uview(88064, BF16, [2048])

            sc.add("dve", lambda e: e.memset(Vaug[:, :, :, 64:128], 1.0), writes=["Vaug"])
            cnt = {"ss": 0, "blk": 0, "bias": 0, "fin": 0, "zz": 0}
            ORD = ((1, 16), (4, 4), (16, 1))
            LOOKU = 3
            BURST = SCHED["burst"]

            def mk_ev_q(pp):
                def ev(t, bank):
                    sc.add("act", (lambda e: e.activation(out=qTb[pp % 2][:, 512 * t:512 * t + 512], in_=pb[bank][:, :], func=AF.Copy)),
                           reads=[("pb", bank)], writes=[("qT", pp % 2, t)])
                return ev

            def mk_ev_k(pp):
                def ev(t, bank):
                    sc.add("dve", (lambda e: e.tensor_copy(out=kTb[pp % 2][:, 512 * t:512 * t + 512], in_=pb[bank][:, :])),
                           reads=[("pb", bank)], writes=[("kT", pp % 2, t)])
                return ev

            def ev_v(t, bank):
                sc.add("act", (lambda e: e.activation(out=vT[:, 512 * t:512 * t + 512], in_=pb[bank][:, :], func=AF.Copy)),
                       reads=[("pb", bank)], writes=[("vT", t)])

            def mk_ev_z(pp):
                def ev(t, bank):
                    tc = slice(512 * t, 512 * t + 512)
                    sc.add("act", (lambda e: e.activation(out=zraw[:, tc], in_=pb[bank][:, :], func=AF.Copy)),
                           reads=[("pb", bank)], writes=[("zraw", t)])
                    if t < 3:
                        return
                    allz = [("zraw", t_) for t_ in range(4)]
                    allo = [("opair", t_, h_) for t_ in range(4) for h_ in range(2)]
                    sc.add("act", (lambda e: e.activation(out=zraw[:, :], in_=zraw[:, :], func=AF.Silu)),
                           reads=allz, writes=allz)
                    sc.add("dve", (lambda e: e.scalar_tensor_tensor(
                        out=yTa[:, pp, :], in0=opair[:, :], scalar=gattn[:, pp:pp + 1], in1=zraw[:, :],
                        op0=ALU.mult, op1=ALU.mult)),
                        reads=allo + allz + ["gattn"], writes=[("yTa", pp, t_) for t_ in range(4)])
                    sc.add("pool", (lambda e: e.tensor_tensor(out=sqa[:, :], in0=opair[:, :], in1=opair[:, :], op=ALU.mult)),
                           reads=allo, writes=["sqa"])
                    for col in range(16):
                        sc.add("pe", (lambda e, col=col: e.matmul(
                            pb[7][:, col:col + 1], lhsT=sqa[:, 128 * col:128 * col + 128], rhs=ones_b[:, 0:1],
                            start=True, stop=True)),
                            reads=["sqa", "ones_b"], writes=[("pb", 7)])
                    if pp == 0:
                        sc.add("dve", lambda e: e.tensor_copy(out=ssq_a, in_=pb[7][:, 0:16]),
                               reads=[("pb", 7)], writes=["ssq_a"])
                    else:
                        sc.add("dve", lambda e: e.tensor_tensor(out=ssq_a, in0=pb[7][:, 0:16], in1=ssq_a, op=ALU.add),
                               reads=[("pb", 7), "ssq_a"], writes=["ssq_a"])
                return ev

            def attention_steps(p):
                qT = qTb[p % 2]
                kT = kTb[p % 2]
                allv = [("vT", t) for t in range(4)]
                allq = [("qT", p % 2, t) for t in range(4)]
                sc.add("pool", (lambda e: e.tensor_copy(out=q4b.rearrange("p (r q) -> p r q", r=4),
                                                        in_=qT.rearrange("p (q r) -> p r q", r=4))),
                       reads=allq, writes=["q4b"])
                sc.add("pool", (lambda e: e.tensor_copy(out=q16b.rearrange("p (r q) -> p r q", r=16),
                                                        in_=qT.rearrange("p (q r) -> p r q", r=16))),
                       reads=allq, writes=["q16b"])
                for oi, (r, T) in enumerate(ORD):
                    for g in range(2):
                        for cc in range(8):
                            c = 8 * g + cc
                            res, ci = c // T, c % T
                            st0 = res + r * 128 * ci
                            cols = slice(st0, st0 + r * 127 + 1, r)
                            sc.add("pe", (lambda e, cc=cc, cols=cols: e.transpose(
                                psbf(7)[:, 128 * cc:128 * cc + 128], vT[:, cols], ident_b[:, :])),
                                reads=allv + ["ident_b"], writes=[("pb", 7)])
                        src3 = psbf(7).rearrange("p (c f) -> p c f", f=128)
                        sc.add("dve", (lambda e, oi=oi, g=g, src3=src3: e.tensor_copy(
                            out=Vaug[:, oi, 8 * g:8 * g + 8, 0:64], in_=src3[:, :, 0:64])),
                            reads=[("pb", 7)], writes=[("Vaug", oi, g, 0)])
                        sc.add("dve", (lambda e, oi=oi, g=g, src3=src3: e.tensor_copy(
                            out=Vaug[:, oi, 8 * g:8 * g + 8, 128:192], in_=src3[:, :, 64:128])),
                            reads=[("pb", 7)], writes=[("Vaug", oi, g, 1)])
                        if (2 * oi + g) % 3 == 2:
                            yield

                tiles = []
                for hh in range(2):
                    for oi, (r, T) in enumerate(ORD):
                        for res in range(r):
                            for c in range(T):
                                tiles.append(dict(hh=hh, oi=oi, r=r, T=T, res=res, c=c,
                                                  first_of_branch=(res == 0 and c == 0),
                                                  last_of_head=(oi == 2 and res == r - 1 and c == T - 1)))
                grp_bank = {}
                grp_done = {}

                def scores(unit, uidx):
                    tA = unit[0]
                    hh, oi, r = tA["hh"], tA["oi"], tA["r"]
                    h = 2 * p + hh
                    L = S // r
                    R = slice(64 * hh, 64 * hh + 64)
                    if tA["first_of_branch"]:
                        bslot = cnt["bias"] % 2
                        cnt["bias"] += 1
                        coef = -8.0 * slopes[h] * r
                        sc.add("dve", (lambda e: e.scalar_tensor_tensor(
                            out=bias8[bslot], in0=base_f[:, :], scalar=coef, in1=mask8[:, :], op0=ALU.mult, op1=ALU.add)),
                            reads=["base_f", "mask8"], writes=[("bias8", bslot)])
                        for hf in range(2):
                            sc.add("act", (lambda e, hf=hf: e.activation(out=expb2[bslot][:, 256 * hf:256 * hf + 256], in_=bias8[bslot],
                                                                      func=AF.Exp, scale=0.125)),
                                   reads=[("bias8", bslot)], writes=[("expb2", bslot)])
                        cnt["cur_bias", hh, oi] = bslot
                    bslot = cnt["cur_bias", hh, oi]
                    bank = 2 + cnt["ss"] % 3
                    s4 = cnt["ss"] % 4
                    cnt["ss"] += 1
                    lo, hi = None, None
                    for idx, tl in enumerate(unit):
                        T, res, c = tl["T"], tl["res"], tl["c"]
                        q0 = max(0, 128 * c - 64)
                        q1 = min(L, 128 * c + 192)
                        j0 = q0 - (128 * c - 64)
                        nq = q1 - q0
                        kcols = slice(res + r * 128 * c, res + r * (128 * c + 127) + 1, r)
                        qcols = slice(res + r * q0, res + r * (q1 - 1) + 1, r)
                        cb = 256 * idx + j0
                        tl.update(j0=j0, nq=nq, s4=s4, cbase=256 * idx)
                        if lo is None:
                            lo = cb
                        hi = cb + nq
                        ssap = pb[bank][:, cb:cb + nq]
                        tq = sorted({(res + r * q0) // 512, (res + r * (q1 - 1)) // 512} | (set(range(4)) if r > 1 else set()))
                        tk = sorted({(res + r * 128 * c) // 512, (res + r * (128 * c + 127)) // 512} | (set(range(4)) if r > 1 else set()))
                        if r == 1:
                            qsrc, qc2, qkeys = qT, qcols, [("qT", p % 2, t_) for t_ in tq]
                        elif r == 4:
                            qsrc, qc2, qkeys = q4b, slice(res * L + q0, res * L + q1), ["q4b"]
                        else:
                            qsrc, qc2, qkeys = q16b, slice(res * L + q0, res * L + q1), ["q16b"]
                        sc.add("pe", (lambda e, ssap=ssap, kcols=kcols, qsrc=qsrc, qc2=qc2: e.matmul(
                            ssap, lhsT=kT[R, kcols], rhs=qsrc[R, qc2], start=True, stop=True)),
                            reads=qkeys + [("kT", p % 2, t_) for t_ in tk], writes=[("pb", bank)])
                    sc.add("act", (lambda e: e.activation(out=pT2[s4][:, lo:hi], in_=pb[bank][:, lo:hi], func=AF.Exp, scale=0.125)),
                           reads=[("pb", bank)], writes=[("pT2", s4)])
                    meng = "dve"
                    sc.add(meng, (lambda e: e.tensor_tensor(out=pT2[s4][:, lo:hi], in0=pT2[s4][:, lo:hi], in1=expb2[bslot][:, lo:hi], op=ALU.mult)),
                           reads=[("pT2", s4), ("expb2", bslot)], writes=[("pT2", s4)])

                def pv(tl):
                    hh, oi, r, T, res, c = tl["hh"], tl["oi"], tl["r"], tl["T"], tl["res"], tl["c"]
                    j0, nq, s4, cbase = tl["j0"], tl["nq"], tl["s4"], tl["cbase"]
                    L = S // r
                    oacc = oaccb[hh]
                    lhsT_v = Vaug[:, oi, res * T + c, 64 * hh:64 * hh + 128]
                    vkeys = [("Vaug", oi, (res * T + c) // 8, 0), ("Vaug", oi, (res * T + c) // 8, 1), "Vaug"]
                    jbase = 128 * c - 64
                    for half in range(2):
                        bI = c + half
                        if T == 1:
                            if half == 1:
                                continue
                            p0, p1, st_, sp_ = 0, L, True, True
                        else:
                            p0 = max(0, 128 * bI - 64)
                            p1 = min(L, 128 * bI + 64)
                            if half == 0:
                                st_, sp_ = (c == 0), True
                            else:
                                st_, sp_ = True, (c == T - 1)
                        pos = p0
                        while pos < p1:
                            nxt = min(p1, (pos // 512 + 1) * 512)
                            if r == 16:
                                gkey = (hh, oi, res // 4, 0)
                                col0 = 128 * (res % 4) + pos
                                extent = 512
                            else:
                                gkey = (hh, oi, res, pos // 512)
                                col0 = pos % 512
                                extent = min(L, (pos // 512 + 1) * 512) - (pos // 512) * 512
                            if gkey not in grp_bank:
                                grp_bank[gkey] = 5 + cnt["blk"] % 2
                                cnt["blk"] += 1
                                grp_done[gkey] = 0
                            bank = grp_bank[gkey]
                            oap = pb[bank][:, col0:col0 + (nxt - pos)]
                            ja, jb = pos - jbase, nxt - jbase
                            sc.add("pe", (lambda e, oap=oap, ja=ja, jb=jb, st_=st_, sp_=sp_: e.matmul(
                                oap, lhsT=lhsT_v, rhs=pT2[s4][:, cbase + ja:cbase + jb], start=st_, stop=sp_)),
                                reads=[("pT2", s4)] + vkeys, writes=[("pb", bank)])
                            if sp_:
                                grp_done[gkey] += nxt - pos
                                if grp_done[gkey] == extent:
                                    if r == 16:
                                        gi = res // 4
                                        oview = oacc.rearrange("p (q k) -> p q k", k=16)[:, :, 4 * gi:4 * gi + 4]
                                        iview = pb[bank][:, 0:512].rearrange("p (k q) -> p q k", k=4)
                                        okeys = [("oacc", hh, t_) for t_ in range(4)]
                                    else:
                                        g0 = (pos // 512) * 512
                                        oview = oacc[:, slice(res + r * g0, res + r * (g0 + extent - 1) + 1, r)]
                                        iview = pb[bank][:, 0:extent]
                                        okeys = [("oacc", hh, t_) for t_ in (range(4) if r > 1 else [g0 // 512])]
                                    if oi == 0:
                                        sc.add("act", (lambda e, oview=oview, iview=iview: e.activation(out=oview, in_=iview, func=AF.Copy)),
                                               reads=[("pb", bank)], writes=okeys)
                                    else:
                                        sc.add("dve", (lambda e, oview=oview, iview=iview: e.tensor_tensor(
                                            out=oview, in0=iview, in1=oview, op=ALU.add)),
                                            reads=[("pb", bank)] + okeys, writes=okeys)
                            pos = nxt

                def finalize(hh):
                    oacc = oaccb[hh]
                    R = slice(64 * hh, 64 * hh + 64)
                    DR = slice(64 * (1 - hh), 64 * (1 - hh) + 64)
                    for t in range(4):
                        fs = cnt["fin"] % 2
                        cnt["fin"] += 1
                        tc = slice(512 * t, 512 * t + 512)
                        sc.add("act", (lambda e, fs=fs, tc=tc: e.activation(out=rden[fs][DR, :], in_=oacc[DR, tc], func=AF.Ln)),
                               reads=[("oacc", hh, t)], writes=[("rden", fs)])
                        sc.add("act", (lambda e, fs=fs: e.activation(out=rden[fs][DR, :], in_=rden[fs][DR, :], func=AF.Exp, scale=-1.0)),
                               reads=[("rden", fs)], writes=[("rden", fs)])
                        sc.add("dve", (lambda e, fs=fs: e.tensor_copy(out=rden[fs][R, :], in_=rden[fs][DR, :])),
                               reads=[("rden", fs)], writes=[("rden", fs)])
                        sc.add("pool", (lambda e, fs=fs, tc=tc: e.tensor_tensor(
                            out=opair[R, tc], in0=oacc[R, tc], in1=rden[fs][R, :], op=ALU.mult)),
                            reads=[("oacc", hh, t), ("rden", fs)], writes=[("opair", t, hh)])

                units = [tiles[i:i + 2] for i in range(0, len(tiles), 2)]
                for un in units:
                    assert (un[0]["hh"], un[0]["oi"]) == (un[1]["hh"], un[1]["oi"])
                n = len(units)
                for i in range(min(LOOKU, n)):
                    scores(units[i], i)
                yield
                for i0 in range(0, n, BURST):
                    def do_pv():
                        for i in range(i0, min(i0 + BURST, n)):
                            for tl in units[i]:
                                pv(tl)
                                if tl["last_of_head"]:
                                    finalize(tl["hh"])

                    def do_sc():
                        for i in range(i0 + LOOKU, min(i0 + LOOKU + BURST, n)):
                            scores(units[i], i)
                    if SCHED["scores_first"] and BURST < LOOKU:
                        do_sc()
                        do_pv()
                    else:
                        do_pv()
                        do_sc()
                    yield

            DBG["pair"] = 0
            merge([in_proj_units(32, mk_ev_q(0)), in_proj_units(40, mk_ev_k(0)), in_proj_units(48, ev_v)], None)
            for p in range(npairs):
                DBG["pair"] = p
                a = []
                if p >= 1:
                    a.append(in_proj_units(56 + p - 1, mk_ev_z(p - 1)))
                if p + 1 < npairs:
                    a += [in_proj_units(32 + p + 1, mk_ev_q(p + 1)), in_proj_units(40 + p + 1, mk_ev_k(p + 1)),
                          in_proj_units(48 + p + 1, ev_v)]
                merge(a, attention_steps(p))
            in_proj(56 + npairs - 1, mk_ev_z(npairs - 1))
            state["nbank"] = 3

            if stop_after == "yTa":
                dump(yTa[:, :, :], 8)
                done = True

        if not done:
            sc.barrier()
            u_sb = uview(0, BF16, [2048])
            cu = uview(4096, F32, [2050])
            acc = uview(12800, F32, [2048])
            yraw = uview(20992, F32, [2048])
            zsc = [uview(29184 + 1024 * i, BF16, [512]) for i in range(2)]
            sqc = [uview(31232 + 1024 * i, BF16, [512]) for i in range(2)]
            yTc = uview(34816, BF16, [8, 2048])
            sc.add("dve", lambda e: e.memset(cu[:, 0:1], 0.0), writes=[("cu", 0)])
            sc.add("dve", lambda e: e.memset(cu[:, 2049:2050], 0.0), writes=[("cu", 3)])
            cz = {"zz": 0}
            allcu = [("cu", t) for t in range(4)]
            allacc = [("acc", t) for t in range(4)]
            for f in range(8):
                def ev_u(t, bank):
                    sc.add("act", (lambda e, t=t, bank=bank: e.activation(out=u_sb[:, 512 * t:512 * t + 512], in_=pb[bank][:, :],
                                                                          func=AF.Copy)),
                           reads=[("pb", bank)], writes=[("u_sb", t)])

                def ev_cg(t, bank):
                    sc.add("dve", (lambda e, t=t, bank=bank: e.tensor_tensor(
                        out=cu[:, 1 + 512 * t:1 + 512 * t + 512], in0=pb[bank][:, :], in1=u_sb[:, 512 * t:512 * t + 512],
                        op=ALU.mult)),
                        reads=[("pb", bank), ("u_sb", t)], writes=[("cu", t)])

                in_proj(f, ev_u)
                in_proj(16 + f, ev_cg)
                sc.add("act", (lambda e, f=f: e.activation(out=acc[:, :], in_=cu[:, 1:2049], func=AF.Identity,
                                                           bias=convb[:, f:f + 1], scale=convw[:, f, 1:2])),
                       reads=allcu + ["convw", "convb"], writes=allacc)
                sc.add("dve", (lambda e, f=f: e.scalar_tensor_tensor(out=acc[:, :], in0=cu[:, 0:2048], scalar=convw[:, f, 0:1],
                                                                    in1=acc[:, :], op0=ALU.mult, op1=ALU.add)),
                       reads=allcu + allacc + ["convw"], writes=allacc)
                sc.add("dve", (lambda e, f=f: e.scalar_tensor_tensor(out=acc[:, :], in0=cu[:, 2:2050], scalar=convw[:, f, 2:3],
                                                                     in1=acc[:, :], op0=ALU.mult, op1=ALU.add)),
                       reads=allcu + allacc + ["convw"], writes=allacc)

                def ev_bg(t, bank):
                    sc.add("dve", (lambda e, t=t, bank=bank: e.tensor_tensor(
                        out=yraw[:, 512 * t:512 * t + 512], in0=pb[bank][:, :], in1=acc[:, 512 * t:512 * t + 512], op=ALU.mult)),
                        reads=[("pb", bank), ("acc", t)], writes=[("yraw", t)])

                in_proj(8 + f, ev_bg)

                def ev_zc(t, bank, f=f):
                    zs_ = cz["zz"] % 2
                    cz["zz"] += 1
                    tc = slice(512 * t, 512 * t + 512)
                    sc.add("act", (lambda e, zs_=zs_, bank=bank: e.activation(out=zsc[zs_], in_=pb[bank][:, :], func=AF.Silu)),
                           reads=[("pb", bank)], writes=[("zsc", zs_)])
                    sc.add("dve", (lambda e, zs_=zs_, tc=tc: e.scalar_tensor_tensor(
                        out=yTc[:, f, tc], in0=yraw[:, tc], scalar=gconv[:, f:f + 1], in1=zsc[zs_],
                        op0=ALU.mult, op1=ALU.mult)),
                        reads=[("yraw", t), ("zsc", zs_), "gconv"], writes=[("yTc", f, t)])
                    sc.add("pool", (lambda e, zs_=zs_, tc=tc: e.tensor_tensor(
                        out=sqc[zs_], in0=yraw[:, tc], in1=yraw[:, tc], op=ALU.mult)),
                        reads=[("yraw", t)], writes=[("sqc", zs_)])
                    for i4 in range(4):
                        col = 16 + 4 * t + i4
                        sc.add("pe", (lambda e, zs_=zs_, i4=i4, col=col: e.matmul(
                            pb[7][:, col:col + 1], lhsT=sqc[zs_][:, 128 * i4:128 * i4 + 128], rhs=ones_b[:, 0:1],
                            start=True, stop=True)),
                            reads=[("sqc", zs_), "ones_b"], writes=[("pb", 7)])

                in_proj(24 + f, ev_zc)
                if f == 0:
                    sc.add("dve", lambda e: e.tensor_copy(out=ssq_c, in_=pb[7][:, 16:32]), reads=[("pb", 7)], writes=["ssq_c"])
                else:
                    sc.add("dve", lambda e: e.tensor_tensor(out=ssq_c, in0=pb[7][:, 16:32], in1=ssq_c, op=ALU.add),
                           reads=[("pb", 7), "ssq_c"], writes=["ssq_c"])

            if stop_after == "yTc":
                sc.barrier()
                sc.add("sp", lambda e: e.dma_start(out=dbg_d[:, 0:8, :], in_=yTc[:, :, :]), dma_key="out")
                sc.add("sp", lambda e: e.dma_start(out=dbg_d[:, 8:16, :], in_=yTa[:, :, :]), dma_key="out")
                done = True

        if not done:
            sc.barrier()
            xt = [uview(8192 * i, F32, [2048]) for i in range(2)]
            yv = [uview(16384 + 8192 * i, F32, [2048]) for i in range(2)]
            junk = uview(32768, BF16, [512])
            gg = wsl[:, :, :, :].rearrange("p a b c -> p (a b c)").bitcast(F32)[:, 0:2048]
            wout = hT
            for src, dst, nm in ((ssq_a, ra, "ra"), (ssq_c, rc, "rc")):
                sc.add("dve", (lambda e, src=src, dst=dst: e.tensor_scalar(out=dst, in0=src, scalar1=1.0 / 1024.0, scalar2=EPS,
                                                                           op0=ALU.mult, op1=ALU.add)), writes=[nm])
                sc.add("act", (lambda e, dst=dst: e.activation(out=dst, in_=dst, func=AF.Sqrt)), reads=[nm], writes=[nm])
                sc.add("dve", (lambda e, dst=dst: e.reciprocal(out=dst, in_=dst)), reads=[nm], writes=[nm])
            for k in range(KC):
                sc.add("pool", (lambda e, k=k: e.dma_start(out=wout[:, k, :], in_=wout_d[128 * k:128 * k + 128, :])),
                       writes=[("wout", k)], dma_key=("wout", k))
            sc.add("sp", lambda e: e.dma_start(out=gg, in_=gpost_d[:, :].partition_broadcast(128)), writes=["gg"], dma_key="gg")
            dg = [yv[1][:, 0:128], yv[1][:, 128:256]]
            for k in range(KC):
                d2 = k % 2
                sc.add("dve", (lambda e, k=k, d2=d2: e.tensor_scalar(out=dg[d2], in0=ident_f[:, :], scalar1=gate_lay[:, k:k + 1],
                                                                    scalar2=None, op0=ALU.mult)),
                       reads=["ident_f", "gate_lay"], writes=[("dg", d2)])
                sc.add("pe", (lambda e, k=k, d2=d2: e.matmul(pb[k // 4][:, 128 * (k % 4):128 * (k % 4) + 128], lhsT=ones_f[:, :],
                                                            rhs=dg[d2], start=True, stop=True)),
                       reads=[("dg", d2), "ones_f"], writes=[("pb", k // 4)])
                if k % 4 == 3:
                    g4 = k // 4
                    sc.add("dve", (lambda e, g4=g4: e.tensor_tensor(out=gg[:, 512 * g4:512 * g4 + 512], in0=pb[g4][:, :],
                                                                   in1=gg[:, 512 * g4:512 * g4 + 512], op=ALU.mult)),
                           reads=[("pb", g4), "gg"], writes=["gg"])
            junk = wsl[:, :, :, :].rearrange("p a b c -> p (a b c)")[:, 4096:6144]

            def p3_mm(i, hf):
                tcol = slice(128 * i, 128 * i + 128)
                for grp in range(2):
                    for n in range(2):
                        bank = 4 * hf + 2 * grp + n
                        dc = slice(1024 * hf + 512 * n, 1024 * hf + 512 * n + 512)
                        for k in range(8 * grp, 8 * grp + 8):
                            lhsT = yTc[:, k, tcol] if k < 8 else yTa[:, k - 8, tcol]
                            sc.add("pe", (lambda e, bank=bank, lhsT=lhsT, k=k, dc=dc: e.matmul(
                                pb[bank][:, :], lhsT=lhsT, rhs=wout[:, k, dc], start=(k % 8 == 0), stop=(k % 8 == 7))),
                                reads=[("wout", k)], writes=[("pb", bank)])

            def p3_evac(i, hf):
                b2 = i % 2
                for n in range(2):
                    dc = slice(1024 * hf + 512 * n, 1024 * hf + 512 * n + 512)
                    q4 = 2 * hf + n
                    extra = [("dg", 0), ("dg", 1)] if b2 == 1 else []
                    sc.add("act", (lambda e, hf=hf, n=n, dc=dc: e.activation(
                        out=yv[b2][:, dc], in_=pb[4 * hf + n][:, :], func=AF.Identity, scale=rc[:, i:i + 1])),
                        reads=[("pb", 4 * hf + n), "rc"] + extra, writes=[("yv", b2, q4)] + extra)
                    sc.add("dve", (lambda e, hf=hf, n=n, dc=dc: e.scalar_tensor_tensor(
                        out=yv[b2][:, dc], in0=pb[4 * hf + 2 + n][:, :], scalar=ra[:, i:i + 1], in1=yv[b2][:, dc],
                        op0=ALU.mult, op1=ALU.add)),
                        reads=[("pb", 4 * hf + 2 + n), "ra", ("yv", b2, q4)], writes=[("yv", b2, q4)])

            def p3_post(i):
                b2 = i % 2
                allyv = [("yv", b2, q4) for q4 in range(4)]
                sc.add("act", (lambda e: e.activation(out=junk, in_=yv[b2], func=AF.Square, accum_out=ssq_p1[:, i:i + 1])),
                       reads=allyv, writes=["junk", ("rp", i)])
                sc.add("dve", (lambda e: e.tensor_scalar(out=rp[:, i:i + 1], in0=ssq_p1[:, i:i + 1], scalar1=1.0 / D,
                                                         scalar2=EPS, op0=ALU.mult, op1=ALU.add)),
                       reads=[("rp", i)], writes=[("rp", i)])
                sc.add("act", (lambda e: e.activation(out=rp[:, i:i + 1], in_=rp[:, i:i + 1], func=AF.Sqrt)),
                       reads=[("rp", i)], writes=[("rp", i)])
                sc.add("dve", (lambda e: e.reciprocal(out=rp[:, i:i + 1], in_=rp[:, i:i + 1])),
                       reads=[("rp", i)], writes=[("rp", i)])
                sc.add("dve", (lambda e: e.scalar_tensor_tensor(
                    out=yv[b2], in0=yv[b2], scalar=rp[:, i:i + 1], in1=gg, op0=ALU.mult, op1=ALU.mult)),
                    reads=allyv + [("rp", i), "gg"], writes=allyv)
                sc.add("pool", (lambda e: e.tensor_tensor(out=xt[b2], in0=yv[b2], in1=xt[b2], op=ALU.add)),
                       reads=allyv + [("xt", b2)], writes=[("xt", b2)])
                sc.add("act", (lambda e: e.dma_start(out=out_d[128 * i:128 * i + 128, :], in_=xt[b2])),
                       reads=[("xt", b2)], writes=[("outdram", i)], dma_key=("out", b2))

            for i in range(16):
                b2 = i % 2
                sc.add("sp", (lambda e, i=i, b2=b2: e.dma_start(out=xt[b2], in_=x_d[128 * i:128 * i + 128, :])),
                       writes=[("xt", b2)], dma_key=("xt", b2))
                p3_mm(i, 0)
                p3_evac(i, 0)
                if i > 0:
                    p3_post(i - 1)
                p3_mm(i, 1)
                p3_evac(i, 1)
            p3_post(15)

        sc.barrier()

        sc.finalize()
        dma_keys = list(sc.dma_cnt.keys())
        eng_sems = {e: es.enter_context(nc.semaphore(f"sem_{e}")) for e in Sched.ENGS}
        dma_sems = {k: es.enter_context(nc.semaphore(f"dsem_{i}")) for i, k in enumerate(dma_keys)}
        block = es.enter_context(nc.Block())

        @block.tensor
        def _(e):
            sc.emit("pe", e, eng_sems, dma_sems)

        @block.scalar
        def _(e):
            sc.emit("act", e, eng_sems, dma_sems)

        @block.vector
        def _(e):
            sc.emit("dve", e, eng_sems, dma_sems)

        @block.gpsimd
        def _(e):
            sc.emit("pool", e, eng_sems, dma_sems)

        @block.sync
        def _(e):
            sc.emit("sp", e, eng_sems, dma_sems)

    return nc


def _lay(v, nchunk):
    return np.ascontiguousarray(np.asarray(v, np.float32).reshape(nchunk, 128).T)


def make_in_maps(x, c, w_ada, b_ada, g_pre, w_in, conv_w, conv_b, g_conv, g_attn, w_out, g_post):
    x = np.asarray(x, np.float32)
    c = np.asarray(c, np.float32)
    w_ada0 = np.ascontiguousarray(np.asarray(w_ada, np.float32)[0])
    w_in0 = np.ascontiguousarray(np.asarray(w_in, np.float32)[0])
    w_out0 = np.ascontiguousarray(np.asarray(w_out, np.float32)[0])
    cw = np.asarray(conv_w, np.float32)[0]
    convw_lay = np.ascontiguousarray(cw.reshape(3, 8, 128).transpose(2, 1, 0))
    shared = {
        "w_ada": w_ada0,
        "b_row": np.ascontiguousarray(np.asarray(b_ada, np.float32)[0].reshape(1, NMOD)),
        "gpre_lay": _lay(np.asarray(g_pre)[0], 16),
        "w_in": w_in0,
        "convw_lay": convw_lay,
        "convb_lay": _lay(np.asarray(conv_b)[0], 8),
        "gconv_lay": _lay(np.asarray(g_conv)[0], 8),
        "gattn_lay": _lay(np.asarray(g_attn)[0], 8),
        "w_out": w_out0,
        "gpost_row": np.ascontiguousarray(np.asarray(g_post, np.float32)[0].reshape(1, D)),
    }
    maps = []
    for b in range(N_CORES):
        m = dict(shared)
        m["x"] = np.ascontiguousarray(x[b])
        m["c_lay"] = _lay(c[b], 16)
        maps.append(m)
    return maps


_NC_CACHE = {}


def kernel(x, c, w_ada, b_ada, g_pre, w_in, conv_w, conv_b, g_conv, g_attn, w_out, g_post):
    if "nc" not in _NC_CACHE:
        _NC_CACHE["nc"] = build_nc()
    nc = _NC_CACHE["nc"]
    in_maps = make_in_maps(x, c, w_ada, b_ada, g_pre, w_in, conv_w, conv_b, g_conv, g_attn, w_out, g_post)
    res = run_bass_kernel_spmd(nc, in_maps, core_ids=list(range(N_CORES)))
    out = np.stack([np.asarray(r["out"], np.float32) for r in res.results], axis=0)
    return out
```

```python
import numpy as np
import concourse.bass as bass
import concourse.mybir as mybir
from concourse.bass_utils import run_bass_kernel_spmd

F32 = mybir.dt.float32
BF16 = mybir.dt.bfloat16
AF = mybir.ActivationFunctionType
ALU = mybir.AluOpType
AX = mybir.AxisListType

D = 2048
S = 2048
NPROJ = 8192
NMOD = 6144
KC = 16
EPS = 1e-6
NEG8 = -8.0 * 30000.0
N_CORES = 8
DBG = {"pair": -1}
SCHED = {"burst": 1, "a_per_step": 2, "a_tr": 4, "scores_first": True}


class _Op:
    __slots__ = ("eng", "fn", "deps", "dma_key", "dma_n", "signal", "sig")

    def __init__(self, eng, fn):
        self.eng = eng
        self.fn = fn
        self.deps = []
        self.dma_key = None
        self.dma_n = 0
        self.signal = False
        self.sig = 0


class _DmaTok:
    __slots__ = ("key", "n")

    def __init__(self, key, n):
        self.key = key
        self.n = n


class Sched:
    ENGS = ("pe", "act", "dve", "pool", "sp")

    def __init__(self):
        self.streams = {e: [] for e in self.ENGS}
        self.res_w = {}
        self.res_r = {}
        self.dma_cnt = {}
        self.last = {}

    @staticmethod
    def _grp(tok):
        return ("dma", tok.key) if isinstance(tok, _DmaTok) else tok.eng

    def add(self, eng, fn, reads=(), writes=(), dma_key=None, extra_deps=()):
        op = _Op(eng, fn)
        deps = {}
        for r in reads:
            for t in self.res_w.get(r, {}).values():
                deps[id(t)] = t
        for w in writes:
            for t in self.res_w.get(w, {}).values():
                deps[id(t)] = t
            for t in self.res_r.get(w, {}).values():
                deps[id(t)] = t
        for t in extra_deps:
            deps[id(t)] = t
        if dma_key is not None:
            n = self.dma_cnt.get(dma_key, 0) + 1
            self.dma_cnt[dma_key] = n
            op.dma_key = dma_key
            op.dma_n = n
            tok = _DmaTok(dma_key, n)
        else:
            tok = op
        g = self._grp(tok)
        for r in reads:
            self.res_r.setdefault(r, {})[g] = tok
        for w in writes:
            self.res_w[w] = {g: tok}
            self.res_r[w] = {}
        deps.pop(id(op), None)
        op.deps = list(deps.values())
        self.streams[eng].append(op)
        if fn is not None:
            self.last[g] = tok
        return tok

    def barrier(self):
        toks = list(self.last.values())
        for e in self.ENGS:
            self.add(e, None, extra_deps=toks)

    def finalize(self):
        for e in self.ENGS:
            for op in self.streams[e]:
                for d in op.deps:
                    if isinstance(d, _Op):
                        if d.eng == "pe" and op.eng == "pe":
                            continue
                        d.signal = True
        for e in self.ENGS:
            c = 0
            for op in self.streams[e]:
                if op.signal:
                    c += 1
                    op.sig = c

    def emit(self, eng_name, e, eng_sems, dma_sems):
        waited = {}
        for op in self.streams[eng_name]:
            need = {}
            for d in op.deps:
                if isinstance(d, _Op):
                    if d.eng == "pe" and eng_name == "pe":
                        continue
                    g, v = d.eng, d.sig
                else:
                    g = ("dma", d.key)
                    v = 16 * (self.dma_cnt[d.key] if d.key == "const" else d.n)
                if v > need.get(g, 0):
                    need[g] = v
            for g, v in need.items():
                if waited.get(g, 0) >= v:
                    continue
                waited[g] = v
                sem = dma_sems[g[1]] if isinstance(g, tuple) else eng_sems[g]
                e.wait_ge(sem, v)
            if op.fn is None:
                continue
            ins = op.fn(e)
            if op.dma_key is not None:
                ins.then_inc(dma_sems[op.dma_key], 16)
            elif op.signal:
                ins.then_inc(eng_sems[eng_name], 1)


def _slopes():
    return [2.0 ** (-8.0 * (h + 1) / 16.0) for h in range(16)]


def build_nc(stop_after=None, npairs=8):
    nc = bass.Bass("TRN2", target_bir_lowering=False)
    x_d = nc.dram_tensor("x", [S, D], F32, kind="ExternalInput").ap()
    c_d = nc.dram_tensor("c_lay", [128, KC], F32, kind="ExternalInput").ap()
    wada_d = nc.dram_tensor("w_ada", [D, NMOD], F32, kind="ExternalInput").ap()
    bada_d = nc.dram_tensor("b_row", [1, NMOD], F32, kind="ExternalInput").ap()
    gpre_d = nc.dram_tensor("gpre_lay", [128, KC], F32, kind="ExternalInput").ap()
    win_d = nc.dram_tensor("w_in", [D, NPROJ], F32, kind="ExternalInput").ap()
    convw_d = nc.dram_tensor("convw_lay", [128, 8, 3], F32, kind="ExternalInput").ap()
    convb_d = nc.dram_tensor("convb_lay", [128, 8], F32, kind="ExternalInput").ap()
    gconv_d = nc.dram_tensor("gconv_lay", [128, 8], F32, kind="ExternalInput").ap()
    gattn_d = nc.dram_tensor("gattn_lay", [128, 8], F32, kind="ExternalInput").ap()
    wout_d = nc.dram_tensor("w_out", [D, D], F32, kind="ExternalInput").ap()
    gpost_d = nc.dram_tensor("gpost_row", [1, D], F32, kind="ExternalInput").ap()
    out_d = nc.dram_tensor("out", [S, D], F32, kind="ExternalOutput").ap()
    dbg_d = None
    if stop_after is not None:
        dbg_d = nc.dram_tensor("dbg", [128, 16, 2048], BF16, kind="ExternalOutput").ap()

    sc = Sched()
    UF = 23040

    import contextlib
    with contextlib.ExitStack() as es:
        def sb(name, shape, dt):
            return es.enter_context(nc.sbuf_tensor(name, shape, dt))

        hT = sb("hT", [128, KC, 2048], BF16)
        yTa = sb("yTa", [128, 8, 2048], BF16)
        U = sb("U", [128, UF], F32)
        wsl = sb("wsl", [128, 3, KC, 128], BF16)
        ident_f = sb("ident_f", [128, 128], F32)
        ident_b = sb("ident_b", [128, 128], BF16)
        ones_f = sb("ones_f", [128, 128], F32)
        ones_b = sb("ones_b", [128, 128], BF16)
        base_f = sb("base_f", [128, 256], F32)
        mask8 = sb("mask8", [128, 256], F32)
        sm = sb("sm", [128, 512], F32)
        cab = sb("cab", [128, KC], BF16)
        pb = [es.enter_context(nc.psum_tensor(f"pb{i}", [128, 512], F32)) for i in range(8)]

        c_sb = sm[:, 0:16]
        gpre = sm[:, 16:32]
        a_sb = sm[:, 32:48]
        s_sb = sm[:, 48:64]
        gate_lay = sm[:, 64:80]
        convw = sm[:, 80:104].rearrange("p (f t) -> p f t", t=3)
        convb = sm[:, 104:112]
        gconv = sm[:, 112:120]
        gattn = sm[:, 120:128]
        ssq_x = sm[:, 128:144]
        rstd_x = sm[:, 144:160]
        ssq_a = sm[:, 160:176]
        ssq_c = sm[:, 176:192]
        ra = sm[:, 192:208]
        rc = sm[:, 208:224]
        ssq_p = sm[:, 224:288]
        ssq_p1 = sm[:, 288:304]
        rp = sm[:, 304:320]
        tmp16 = sm[:, 320:336]

        def uview(off, dt, shape):
            n = 1
            for s_ in shape:
                n *= s_
            nb = n * (4 if dt == F32 else 2)
            assert off % 4 == 0 and nb % 4 == 0 and off + nb <= UF * 4, (off, nb)
            a = U[:, off // 4:(off + nb) // 4]
            if dt != F32:
                a = a.bitcast(dt)
            if len(shape) == 2:
                return a.rearrange("p (a b) -> p a b", b=shape[1])
            if len(shape) == 3:
                return a.rearrange("p (a b c) -> p a b c", b=shape[1], c=shape[2])
            return a

        def psbf(bank):
            return pb[bank][:, :].bitcast(BF16)

        sc.add("pool", lambda e: e.iota(base_f[:, :], [[1, 256]], base=-64, channel_multiplier=-1, allow_small_or_imprecise_dtypes=True),
               writes=["base_f"])
        sc.add("dve", lambda e: e.tensor_single_scalar(out=ident_f[:, :], in_=base_f[:, 64:192], scalar=0.0, op=ALU.is_equal),
               reads=["base_f"], writes=["ident_f"])
        sc.add("dve", lambda e: e.tensor_copy(out=ident_b[:, :], in_=ident_f[:, :]), reads=["ident_f"], writes=["ident_b"])
        sc.add("dve", lambda e: e.memset(ones_f[:, :], 1.0), writes=["ones_f"])
        sc.add("dve", lambda e: e.memset(ones_b[:, :], 1.0), writes=["ones_b"])
        sc.add("dve", lambda e: e.tensor_scalar(out=mask8[:, :], in0=base_f[:, :], scalar1=-1.0, scalar2=None, op0=ALU.mult),
               reads=["ident_f", "base_f"], writes=["mask8"])
        sc.add("dve", lambda e: e.tensor_tensor(out=base_f[:, :], in0=base_f[:, :], in1=mask8[:, :], op=ALU.max),
               reads=["mask8"], writes=["base_f"])
        sc.add("dve", lambda e: e.tensor_scalar(out=mask8[:, :], in0=base_f[:, :], scalar1=64.5, scalar2=NEG8,
                                                op0=ALU.is_gt, op1=ALU.mult),
               reads=["base_f"], writes=["mask8"])

        def cdma(dst, src, key):
            sc.add("sp", lambda e: e.dma_start(out=dst, in_=src), writes=[key], dma_key="const")

        cdma(c_sb, c_d[:, :], "c_sb")
        cdma(gpre, gpre_d[:, :], "gpre")
        cdma(convw, convw_d[:, :, :], "convw")
        cdma(convb, convb_d[:, :], "convb")
        cdma(gconv, gconv_d[:, :], "gconv")
        cdma(gattn, gattn_d[:, :], "gattn")

        def dump_small():
            sc.barrier()
            sc.add("sp", lambda e: e.dma_start(out=dbg_d[:, 0, 0:256], in_=sm[:, 0:256].bitcast(BF16)[:, 0:256]), dma_key="out")
            sc.add("sp", lambda e: e.dma_start(out=out_d[0:128, 0:256], in_=base_f[:, :]), dma_key="out")
            sc.add("sp", lambda e: e.dma_start(out=out_d[128:256, 0:256], in_=mask8[:, :]), dma_key="out")
            sc.add("sp", lambda e: e.dma_start(out=out_d[256:384, 0:128], in_=ident_f[:, :]), dma_key="out")
            sc.add("sp", lambda e: e.dma_start(out=out_d[384:512, 0:512], in_=sm[:, :]), dma_key="out")
        done = False
        if stop_after == "const":
            dump_small()
            done = True
        mod_row = U[0:1, 0:NMOD]
        b_row = U[0:1, NMOD:2 * NMOD]
        xs = [uview(2 * NMOD * 4 + 8192 * i, F32, [2048]) for i in range(2)]
        xn = [yTa[:, i, :] for i in range(2)]
        if not done:
          cdma(b_row, bada_d[:, :], "b_row")
          sc.add("act", lambda e: e.activation(out=cab[:, :], in_=c_sb, func=AF.Silu), reads=["c_sb"], writes=["cab"])

          def wa_slot(s_):
              return yTa[:, 2 + 2 * s_:4 + 2 * s_, :].rearrange("p a (b n) -> p (a b) n", n=256)

          def emit_slice(s_):
              slot = s_ % 3
              dst = wa_slot(slot)
              src = wada_d[:, 256 * s_:256 * s_ + 256].rearrange("(k p) n -> p k n", p=128)
              sc.add("pool", (lambda e: e.dma_start(out=dst, in_=src)), writes=[("waslot", slot)], dma_key=("wa", slot))
              bank = s_ % 2
              for k in range(KC):
                  sc.add("pe", (lambda e, k=k: e.matmul(
                      pb[bank][0:1, 0:256], lhsT=cab[:, k:k + 1], rhs=dst[:, k, :], start=(k == 0), stop=(k == KC - 1))),
                      reads=[("waslot", slot), "cab"], writes=[("pb", bank)])
              sc.add("dve", (lambda e: e.tensor_tensor(
                  out=mod_row[:, 256 * s_:256 * s_ + 256], in0=pb[bank][0:1, 0:256],
                  in1=b_row[:, 256 * s_:256 * s_ + 256], op=ALU.add)),
                  reads=[("pb", bank), "b_row"], writes=[("mod", s_)])

          def emit_tile(i):
              b2 = i % 2
              sc.add("sp", (lambda e: e.dma_start(out=xs[b2], in_=x_d[128 * i:128 * i + 128, :])),
                     writes=[("xs", b2)], dma_key=("xs", b2))
              sc.add("act", (lambda e: e.activation(out=xn[b2], in_=xs[b2], func=AF.Square, accum_out=ssq_x[:, i:i + 1])),
                     reads=[("xs", b2)], writes=[("xn", b2), ("ssq_x", i)])
              sc.add("dve", (lambda e: e.tensor_scalar(out=rstd_x[:, i:i + 1], in0=ssq_x[:, i:i + 1], scalar1=1.0 / D,
                                                       scalar2=EPS, op0=ALU.mult, op1=ALU.add)),
                     reads=[("ssq_x", i)], writes=[("rstd_x", i)])
              sc.add("act", (lambda e: e.activation(out=rstd_x[:, i:i + 1], in_=rstd_x[:, i:i + 1], func=AF.Sqrt)),
                     reads=[("rstd_x", i)], writes=[("rstd_x", i)])
              sc.add("dve", (lambda e: e.reciprocal(out=rstd_x[:, i:i + 1], in_=rstd_x[:, i:i + 1])),
                     reads=[("rstd_x", i)], writes=[("rstd_x", i)])
              sc.add("dve", (lambda e: e.tensor_scalar(out=xn[b2], in0=xs[b2], scalar1=rstd_x[:, i:i + 1],
                                                       scalar2=None, op0=ALU.mult)),
                     reads=[("xs", b2), ("rstd_x", i)], writes=[("xn", b2)])
              for g in range(2):
                  bank = 4 + 2 * b2 + g
                  for kk in range(8):
                      k = 8 * g + kk
                      sc.add("pe", (lambda e, kk=kk, k=k, bank=bank: e.transpose(
                          psbf(bank)[:, 128 * kk:128 * kk + 128], xn[b2][:, 128 * k:128 * k + 128], ident_b[:, :])),
                          reads=[("xn", b2), "ident_b"], writes=[("pb", bank)])
                  src3 = psbf(bank).rearrange("p (c f) -> p c f", f=128)
                  dst3 = hT[:, 8 * g:8 * g + 8, 128 * i:128 * i + 128]
                  hkeys = [("hT", k) for k in range(8 * g, 8 * g + 8)]
                  sc.add("dve", (lambda e, src3=src3, dst3=dst3: e.tensor_copy(out=dst3, in_=src3)),
                         reads=[("pb", bank)], writes=hkeys)

          n_sl = 0
          for i in range(16):
              emit_tile(i)
              while n_sl < (24 * (i + 1)) // 16:
                  emit_slice(n_sl)
                  n_sl += 1
          for j in range(48):
              sc.add("pe", (lambda e, j=j: e.matmul(pb[2][:, j:j + 1], lhsT=mod_row[:, 128 * j:128 * j + 128],
                                                    rhs=ones_f[0:1, 0:1], start=True, stop=True)),
                     reads=[("mod", j // 2), "ones_f"], writes=[("pb", 2)])
          sc.add("dve", lambda e: e.tensor_copy(out=s_sb, in_=pb[2][:, 0:16]), reads=[("pb", 2)], writes=["s_sb"])
          sc.add("dve", lambda e: e.scalar_tensor_tensor(out=a_sb, in0=pb[2][:, 16:32], scalar=1.0, in1=gpre,
                                                         op0=ALU.add, op1=ALU.mult),
                 reads=[("pb", 2), "gpre"], writes=["a_sb"])
          sc.add("dve", lambda e: e.tensor_copy(out=gate_lay, in_=pb[2][:, 32:48]), reads=[("pb", 2)], writes=["gate_lay"])
          for k in range(KC):
              if k % 2 == 0:
                  sc.add("dve", (lambda e, k=k: e.tensor_scalar(out=hT[:, k, :], in0=hT[:, k, :], scalar1=a_sb[:, k:k + 1],
                                                                scalar2=s_sb[:, k:k + 1], op0=ALU.mult, op1=ALU.add)),
                         reads=[("hT", k), "a_sb", "s_sb"], writes=[("hT", k)])
              else:
                  sc.add("act", (lambda e, k=k: e.activation(out=hT[:, k, :], in_=hT[:, k, :], func=AF.Identity,
                                                             bias=s_sb[:, k:k + 1], scale=a_sb[:, k:k + 1])),
                         reads=[("hT", k), "a_sb", "s_sb"], writes=[("hT", k)])

        def dump(src_ap, nk):
            sc.barrier()
            sc.add("sp", lambda e: e.dma_start(out=dbg_d[:, 0:nk, :], in_=src_ap), dma_key="out")
            sc.add("sp", lambda e: e.dma_start(out=out_d[0:128, :], in_=xs[0]), dma_key="out")

        if stop_after == "mod" and not done:
            dump_small()
            done = True
        if stop_after == "hT" and not done:
            dump(hT[:, :, :], 16)
            done = True

        state = {"wslot": 0, "bank": 0, "nbank": 3}

        order = [32, 40, 48]
        for p_ in range(npairs):
            if p_ >= 1:
                order.append(56 + p_ - 1)
            if p_ + 1 < npairs:
                order += [32 + p_ + 1, 40 + p_ + 1, 48 + p_ + 1]
        order.append(56 + npairs - 1)
        for f_ in range(8):
            order += [f_, 16 + f_, 8 + f_, 24 + f_]
        state["issued"] = 0

        def issue_w(upto):
            while state["issued"] < min(upto, len(order)):
                n_ = state["issued"]
                j_ = order[n_]
                slot_ = n_ % 3
                src = win_d[:, 128 * j_:128 * j_ + 128].rearrange("(k p) n -> p k n", p=128)
                sc.add("pool", (lambda e, slot_=slot_, src=src: e.dma_start(out=wsl[:, slot_], in_=src)),
                       writes=[("wsl", slot_)], dma_key=("wsl", slot_))
                state["issued"] += 1

        def in_proj_units(j, evac):
            n_ = state["wslot"]
            assert order[n_] == j, (n_, j, order[n_])
            slot = n_ % 3
            state["wslot"] += 1
            issue_w(n_ + 3)
            for t in range(4):
                bank = state["bank"] % state["nbank"]
                state["bank"] += 1
                for k in range(KC):
                    sc.add("pe", (lambda e, bank=bank, k=k, t=t, slot=slot: e.matmul(
                        pb[bank][:, :], lhsT=wsl[:, slot, k, :], rhs=hT[:, k, 512 * t:512 * t + 512],
                        start=(k == 0), stop=(k == KC - 1))),
                        reads=[("wsl", slot), ("hT", k)], writes=[("pb", bank)])
                    if k == KC - 1:
                        evac(t, bank)
                    if k % 2 == 1:
                        yield

        def in_proj(j, evac):
            for _ in in_proj_units(j, evac):
                pass

        def merge(a_gens, b_gen):
            import itertools
            A = itertools.chain(*a_gens)
            B = b_gen if b_gen is not None else iter(())
            a_done = b_done = False
            while not (a_done and b_done):
                n_a = SCHED["a_per_step"]
                if not b_done:
                    try:
                        r_ = next(B)
                        if r_ is not None:
                            n_a = r_
                    except StopIteration:
                        b_done = True
                if b_done:
                    n_a = 8
                for _ in range(n_a):
                    if not a_done:
                        try:
                            next(A)
                        except StopIteration:
                            a_done = True

        slopes = _slopes()

        if not done:
            sc.barrier()
            state["nbank"] = 2
            state["bank"] = 0
            qTb = [uview(0, BF16, [2048]), uview(4096, BF16, [2048])]
            kTb = [uview(8192, BF16, [2048]), uview(12288, BF16, [2048])]
            vT = uview(16384, BF16, [2048])
            Vaug = uview(20480, BF16, [3, 16, 192])
            oaccb = [uview(38912, F32, [2048]), uview(47104, F32, [2048])]
            opair = uview(55296, F32, [2048])
            pT2 = [uview(63488 + 1024 * i, BF16, [512]) for i in range(4)]
            expb2 = [uview(67584 + 1024 * i, BF16, [512]) for i in range(2)]
            bias8 = [uview(69632 + 1024 * i, F32, [256]) for i in range(2)]
            rden = [uview(71680 + 2048 * i, F32, [512]) for i in range(2)]
            sqa = uview(75776, BF16, [2048])
            zraw = uview(79872, BF16, [2048])
            q4b = uview(83968, BF16, [2048])
            q16b = uview(88064, BF16, [2048])

            sc.add("dve", lambda e: e.memset(Vaug[:, :, :, 64:128], 1.0), writes=["Vaug"])
            cnt = {"ss": 0, "blk": 0, "bias": 0, "fin": 0, "zz": 0}
            ORD = ((1, 16), (4, 4), (16, 1))
            LOOKU = 3
            BURST = SCHED["burst"]

            def mk_ev_q(pp):
                def ev(t, bank):
                    sc.add("act", (lambda e: e.activation(out=qTb[pp % 2][:, 512 * t:512 * t + 512], in_=pb[bank][:, :], func=AF.Copy)),
                           reads=[("pb", bank)], writes=[("qT", pp % 2, t)])
                return ev

            def mk_ev_k(pp):
                def ev(t, bank):
                    sc.add("dve", (lambda e: e.tensor_copy(out=kTb[pp % 2][:, 512 * t:512 * t + 512], in_=pb[bank][:, :])),
                           reads=[("pb", bank)], writes=[("kT", pp % 2, t)])
                return ev

            def ev_v(t, bank):
                sc.add("act", (lambda e: e.activation(out=vT[:, 512 * t:512 * t + 512], in_=pb[bank][:, :], func=AF.Copy)),
                       reads=[("pb", bank)], writes=[("vT", t)])

            def mk_ev_z(pp):
                def ev(t, bank):
                    tc = slice(512 * t, 512 * t + 512)
                    sc.add("act", (lambda e: e.activation(out=zraw[:, tc], in_=pb[bank][:, :], func=AF.Copy)),
                           reads=[("pb", bank)], writes=[("zraw", t)])
                    if t < 3:
                        return
                    allz = [("zraw", t_) for t_ in range(4)]
                    allo = [("opair", t_, h_) for t_ in range(4) for h_ in range(2)]
                    sc.add("act", (lambda e: e.activation(out=zraw[:, :], in_=zraw[:, :], func=AF.Silu)),
                           reads=allz, writes=allz)
                    sc.add("dve", (lambda e: e.scalar_tensor_tensor(
                        out=yTa[:, pp, :], in0=opair[:, :], scalar=gattn[:, pp:pp + 1], in1=zraw[:, :],
                        op0=ALU.mult, op1=ALU.mult)),
                        reads=allo + allz + ["gattn"], writes=[("yTa", pp, t_) for t_ in range(4)])
                    sc.add("pool", (lambda e: e.tensor_tensor(out=sqa[:, :], in0=opair[:, :], in1=opair[:, :], op=ALU.mult)),
                           reads=allo, writes=["sqa"])
                    for col in range(16):
                        sc.add("pe", (lambda e, col=col: e.matmul(
                            pb[7][:, col:col + 1], lhsT=sqa[:, 128 * col:128 * col + 128], rhs=ones_b[:, 0:1],
                            start=True, stop=True)),
                            reads=["sqa", "ones_b"], writes=[("pb", 7)])
                    if pp == 0:
                        sc.add("dve", lambda e: e.tensor_copy(out=ssq_a, in_=pb[7][:, 0:16]),
                               reads=[("pb", 7)], writes=["ssq_a"])
                    else:
                        sc.add("dve", lambda e: e.tensor_tensor(out=ssq_a, in0=pb[7][:, 0:16], in1=ssq_a, op=ALU.add),
                               reads=[("pb", 7), "ssq_a"], writes=["ssq_a"])
                return ev

            def attention_steps(p):
                qT = qTb[p % 2]
                kT = kTb[p % 2]
                allv = [("vT", t) for t in range(4)]
                allq = [("qT", p % 2, t) for t in range(4)]
                sc.add("pool", (lambda e: e.tensor_copy(out=q4b.rearrange("p (r q) -> p r q", r=4),
                                                        in_=qT.rearrange("p (q r) -> p r q", r=4))),
                       reads=allq, writes=["q4b"])
                sc.add("pool", (lambda e: e.tensor_copy(out=q16b.rearrange("p (r q) -> p r q", r=16),
                                                        in_=qT.rearrange("p (q r) -> p r q", r=16))),
                       reads=allq, writes=["q16b"])
                TR_BANKS = (7, 2, 3, 4)
                for oi, (r, T) in enumerate(ORD):
                    for g in range(2):
                        tb = TR_BANKS[(2 * oi + g) % 4]
                        for cc in range(8):
                            c = 8 * g + cc
                            res, ci = c // T, c % T
                            st0 = res + r * 128 * ci
                            cols = slice(st0, st0 + r * 127 + 1, r)
                            sc.add("pe", (lambda e, cc=cc, cols=cols, tb=tb: e.transpose(
                                psbf(tb)[:, 128 * cc:128 * cc + 128], vT[:, cols], ident_b[:, :])),
                                reads=allv + ["ident_b"], writes=[("pb", tb)])
                        src3 = psbf(tb).rearrange("p (c f) -> p c f", f=128)
                        if (2 * oi + g) % 2 == 0:
                            sc.add("dve", (lambda e, oi=oi, g=g, src3=src3: e.tensor_copy(
                                out=Vaug[:, oi, 8 * g:8 * g + 8, 0:64], in_=src3[:, :, 0:64])),
                                reads=[("pb", tb)], writes=[("Vaug", oi, g, 0)])
                            sc.add("dve", (lambda e, oi=oi, g=g, src3=src3: e.tensor_copy(
                                out=Vaug[:, oi, 8 * g:8 * g + 8, 128:192], in_=src3[:, :, 64:128])),
                                reads=[("pb", tb)], writes=[("Vaug", oi, g, 1)])
                        else:
                            sc.add("act", (lambda e, oi=oi, g=g, src3=src3: e.activation(
                                out=Vaug[:, oi, 8 * g:8 * g + 8, 0:64], in_=src3[:, :, 0:64], func=AF.Copy)),
                                reads=[("pb", tb)], writes=[("Vaug", oi, g, 0)])
                            sc.add("act", (lambda e, oi=oi, g=g, src3=src3: e.activation(
                                out=Vaug[:, oi, 8 * g:8 * g + 8, 128:192], in_=src3[:, :, 64:128], func=AF.Copy)),
                                reads=[("pb", tb)], writes=[("Vaug", oi, g, 1)])
                        yield SCHED["a_tr"]

                tiles = []
                for hh in range(2):
                    for oi, (r, T) in enumerate(ORD):
                        for res in range(r):
                            for c in range(T):
                                tiles.append(dict(hh=hh, oi=oi, r=r, T=T, res=res, c=c,
                                                  first_of_branch=(res == 0 and c == 0),
                                                  last_of_head=(oi == 2 and res == r - 1 and c == T - 1)))
                grp_bank = {}
                grp_done = {}

                def scores(unit, uidx):
                    tA = unit[0]
                    hh, oi, r = tA["hh"], tA["oi"], tA["r"]
                    h = 2 * p + hh
                    L = S // r
                    R = slice(64 * hh, 64 * hh + 64)
                    if tA["first_of_branch"]:
                        bslot = cnt["bias"] % 2
                        cnt["bias"] += 1
                        coef = -8.0 * slopes[h] * r
                        sc.add("dve", (lambda e: e.scalar_tensor_tensor(
                            out=bias8[bslot], in0=base_f[:, :], scalar=coef, in1=mask8[:, :], op0=ALU.mult, op1=ALU.add)),
                            reads=["base_f", "mask8"], writes=[("bias8", bslot)])
                        for hf in range(2):
                            sc.add("act", (lambda e, hf=hf: e.activation(out=expb2[bslot][:, 256 * hf:256 * hf + 256], in_=bias8[bslot],
                                                                      func=AF.Exp, scale=0.125)),
                                   reads=[("bias8", bslot)], writes=[("expb2", bslot)])
                        cnt["cur_bias", hh, oi] = bslot
                    bslot = cnt["cur_bias", hh, oi]
                    bank = 2 + cnt["ss"] % 3
                    s4 = cnt["ss"] % 4
                    cnt["ss"] += 1
                    lo, hi = None, None
                    for idx, tl in enumerate(unit):
                        T, res, c = tl["T"], tl["res"], tl["c"]
                        q0 = max(0, 128 * c - 64)
                        q1 = min(L, 128 * c + 192)
                        j0 = q0 - (128 * c - 64)
                        nq = q1 - q0
                        kcols = slice(res + r * 128 * c, res + r * (128 * c + 127) + 1, r)
                        qcols = slice(res + r * q0, res + r * (q1 - 1) + 1, r)
                        cb = 256 * idx + j0
                        tl.update(j0=j0, nq=nq, s4=s4, cbase=256 * idx)
                        if lo is None:
                            lo = cb
                        hi = cb + nq
                        ssap = pb[bank][:, cb:cb + nq]
                        tq = sorted({(res + r * q0) // 512, (res + r * (q1 - 1)) // 512} | (set(range(4)) if r > 1 else set()))
                        tk = sorted({(res + r * 128 * c) // 512, (res + r * (128 * c + 127)) // 512} | (set(range(4)) if r > 1 else set()))
                        if r == 1:
                            qsrc, qc2, qkeys = qT, qcols, [("qT", p % 2, t_) for t_ in tq]
                        elif r == 4:
                            qsrc, qc2, qkeys = q4b, slice(res * L + q0, res * L + q1), ["q4b"]
                        else:
                            qsrc, qc2, qkeys = q16b, slice(res * L + q0, res * L + q1), ["q16b"]
                        sc.add("pe", (lambda e, ssap=ssap, kcols=kcols, qsrc=qsrc, qc2=qc2: e.matmul(
                            ssap, lhsT=kT[R, kcols], rhs=qsrc[R, qc2], start=True, stop=True)),
                            reads=qkeys + [("kT", p % 2, t_) for t_ in tk], writes=[("pb", bank)])
                    sc.add("act", (lambda e: e.activation(out=pT2[s4][:, lo:hi], in_=pb[bank][:, lo:hi], func=AF.Exp, scale=0.125)),
                           reads=[("pb", bank)], writes=[("pT2", s4)])
                    meng = "dve"
                    sc.add(meng, (lambda e: e.tensor_tensor(out=pT2[s4][:, lo:hi], in0=pT2[s4][:, lo:hi], in1=expb2[bslot][:, lo:hi], op=ALU.mult)),
                           reads=[("pT2", s4), ("expb2", bslot)], writes=[("pT2", s4)])

                def pv(tl):
                    hh, oi, r, T, res, c = tl["hh"], tl["oi"], tl["r"], tl["T"], tl["res"], tl["c"]
                    j0, nq, s4, cbase = tl["j0"], tl["nq"], tl["s4"], tl["cbase"]
                    L = S // r
                    oacc = oaccb[hh]
                    lhsT_v = Vaug[:, oi, res * T + c, 64 * hh:64 * hh + 128]
                    vkeys = [("Vaug", oi, (res * T + c) // 8, 0), ("Vaug", oi, (res * T + c) // 8, 1), "Vaug"]
                    jbase = 128 * c - 64
                    for half in range(2):
                        bI = c + half
                        if T == 1:
                            if half == 1:
                                continue
                            p0, p1, st_, sp_ = 0, L, True, True
                        else:
                            p0 = max(0, 128 * bI - 64)
                            p1 = min(L, 128 * bI + 64)
                            if half == 0:
                                st_, sp_ = (c == 0), True
                            else:
                                st_, sp_ = True, (c == T - 1)
                        pos = p0
                        while pos < p1:
                            nxt = min(p1, (pos // 512 + 1) * 512)
                            if r == 16:
                                gkey = (hh, oi, res // 4, 0)
                                col0 = 128 * (res % 4) + pos
                                extent = 512
                            else:
                                gkey = (hh, oi, res, pos // 512)
                                col0 = pos % 512
                                extent = min(L, (pos // 512 + 1) * 512) - (pos // 512) * 512
                            if gkey not in grp_bank:
                                grp_bank[gkey] = 5 + cnt["blk"] % 2
                                cnt["blk"] += 1
                                grp_done[gkey] = 0
                            bank = grp_bank[gkey]
                            oap = pb[bank][:, col0:col0 + (nxt - pos)]
                            ja, jb = pos - jbase, nxt - jbase
                            sc.add("pe", (lambda e, oap=oap, ja=ja, jb=jb, st_=st_, sp_=sp_: e.matmul(
                                oap, lhsT=lhsT_v, rhs=pT2[s4][:, cbase + ja:cbase + jb], start=st_, stop=sp_)),
                                reads=[("pT2", s4)] + vkeys, writes=[("pb", bank)])
                            if sp_:
                                grp_done[gkey] += nxt - pos
                                if grp_done[gkey] == extent:
                                    if r == 16:
                                        gi = res // 4
                                        oview = oacc.rearrange("p (q k) -> p q k", k=16)[:, :, 4 * gi:4 * gi + 4]
                                        iview = pb[bank][:, 0:512].rearrange("p (k q) -> p q k", k=4)
                                        okeys = [("oacc", hh, t_) for t_ in range(4)]
                                    else:
                                        g0 = (pos // 512) * 512
                                        oview = oacc[:, slice(res + r * g0, res + r * (g0 + extent - 1) + 1, r)]
                                        iview = pb[bank][:, 0:extent]
                                        okeys = [("oacc", hh, t_) for t_ in (range(4) if r > 1 else [g0 // 512])]
                                    if oi == 0:
                                        sc.add("act", (lambda e, oview=oview, iview=iview: e.activation(out=oview, in_=iview, func=AF.Copy)),
                                               reads=[("pb", bank)], writes=okeys)
                                    else:
                                        sc.add("dve", (lambda e, oview=oview, iview=iview: e.tensor_tensor(
                                            out=oview, in0=iview, in1=oview, op=ALU.add)),
                                            reads=[("pb", bank)] + okeys, writes=okeys)
                            pos = nxt

                def finalize(hh):
                    oacc = oaccb[hh]
                    R = slice(64 * hh, 64 * hh + 64)
                    DR = slice(64 * (1 - hh), 64 * (1 - hh) + 64)
                    for t in range(4):
                        fs = cnt["fin"] % 2
                        cnt["fin"] += 1
                        tc = slice(512 * t, 512 * t + 512)
                        sc.add("act", (lambda e, fs=fs, tc=tc: e.activation(out=rden[fs][DR, :], in_=oacc[DR, tc], func=AF.Ln)),
                               reads=[("oacc", hh, t)], writes=[("rden", fs)])
                        sc.add("act", (lambda e, fs=fs: e.activation(out=rden[fs][DR, :], in_=rden[fs][DR, :], func=AF.Exp, scale=-1.0)),
                               reads=[("rden", fs)], writes=[("rden", fs)])
                        sc.add("dve", (lambda e, fs=fs: e.tensor_copy(out=rden[fs][R, :], in_=rden[fs][DR, :])),
                               reads=[("rden", fs)], writes=[("rden", fs)])
                        sc.add("pool", (lambda e, fs=fs, tc=tc: e.tensor_tensor(
                            out=opair[R, tc], in0=oacc[R, tc], in1=rden[fs][R, :], op=ALU.mult)),
                            reads=[("oacc", hh, t), ("rden", fs)], writes=[("opair", t, hh)])

                units = [tiles[i:i + 2] for i in range(0, len(tiles), 2)]
                for un in units:
                    assert (un[0]["hh"], un[0]["oi"]) == (un[1]["hh"], un[1]["oi"])
                n = len(units)
                for i in range(min(LOOKU, n)):
                    scores(units[i], i)
                yield
                for i0 in range(0, n, BURST):
                    def do_pv():
                        for i in range(i0, min(i0 + BURST, n)):
                            for tl in units[i]:
                                pv(tl)
                                if tl["last_of_head"]:
                                    finalize(tl["hh"])

                    def do_sc():
                        for i in range(i0 + LOOKU, min(i0 + LOOKU + BURST, n)):
                            scores(units[i], i)
                    if SCHED["scores_first"] and BURST < LOOKU:
                        do_sc()
                        do_pv()
                    else:
                        do_pv()
                        do_sc()
                    yield

            DBG["pair"] = 0
            merge([in_proj_units(32, mk_ev_q(0)), in_proj_units(40, mk_ev_k(0)), in_proj_units(48, ev_v)], None)
            for p in range(npairs):
                DBG["pair"] = p
                a = []
                if p >= 1:
                    a.append(in_proj_units(56 + p - 1, mk_ev_z(p - 1)))
                if p + 1 < npairs:
                    a += [in_proj_units(32 + p + 1, mk_ev_q(p + 1)), in_proj_units(40 + p + 1, mk_ev_k(p + 1)),
                          in_proj_units(48 + p + 1, ev_v)]
                merge(a, attention_steps(p))
            in_proj(56 + npairs - 1, mk_ev_z(npairs - 1))
            state["nbank"] = 3

            if stop_after == "yTa":
                dump(yTa[:, :, :], 8)
                done = True

        if not done:
            sc.barrier()
            u_sb = uview(0, BF16, [2048])
            cu = uview(4096, F32, [2050])
            acc = uview(12800, F32, [2048])
            yraw = uview(20992, F32, [2048])
            zsc = [uview(29184 + 1024 * i, BF16, [512]) for i in range(2)]
            sqc = [uview(31232 + 1024 * i, BF16, [512]) for i in range(2)]
            yTc = uview(34816, BF16, [8, 2048])
            sc.add("dve", lambda e: e.memset(cu[:, 0:1], 0.0), writes=[("cu", 0)])
            sc.add("dve", lambda e: e.memset(cu[:, 2049:2050], 0.0), writes=[("cu", 3)])
            cz = {"zz": 0}
            allcu = [("cu", t) for t in range(4)]
            allacc = [("acc", t) for t in range(4)]
            for f in range(8):
                def ev_u(t, bank):
                    sc.add("act", (lambda e, t=t, bank=bank: e.activation(out=u_sb[:, 512 * t:512 * t + 512], in_=pb[bank][:, :],
                                                                          func=AF.Copy)),
                           reads=[("pb", bank)], writes=[("u_sb", t)])

                def ev_cg(t, bank):
                    sc.add("dve", (lambda e, t=t, bank=bank: e.tensor_tensor(
                        out=cu[:, 1 + 512 * t:1 + 512 * t + 512], in0=pb[bank][:, :], in1=u_sb[:, 512 * t:512 * t + 512],
                        op=ALU.mult)),
                        reads=[("pb", bank), ("u_sb", t)], writes=[("cu", t)])

                in_proj(f, ev_u)
                in_proj(16 + f, ev_cg)
                sc.add("act", (lambda e, f=f: e.activation(out=acc[:, :], in_=cu[:, 1:2049], func=AF.Identity,
                                                           bias=convb[:, f:f + 1], scale=convw[:, f, 1:2])),
                       reads=allcu + ["convw", "convb"], writes=allacc)
                sc.add("dve", (lambda e, f=f: e.scalar_tensor_tensor(out=acc[:, :], in0=cu[:, 0:2048], scalar=convw[:, f, 0:1],
                                                                    in1=acc[:, :], op0=ALU.mult, op1=ALU.add)),
                       reads=allcu + allacc + ["convw"], writes=allacc)
                sc.add("dve", (lambda e, f=f: e.scalar_tensor_tensor(out=acc[:, :], in0=cu[:, 2:2050], scalar=convw[:, f, 2:3],
                                                                     in1=acc[:, :], op0=ALU.mult, op1=ALU.add)),
                       reads=allcu + allacc + ["convw"], writes=allacc)

                def ev_bg(t, bank):
                    sc.add("dve", (lambda e, t=t, bank=bank: e.tensor_tensor(
                        out=yraw[:, 512 * t:512 * t + 512], in0=pb[bank][:, :], in1=acc[:, 512 * t:512 * t + 512], op=ALU.mult)),
                        reads=[("pb", bank), ("acc", t)], writes=[("yraw", t)])

                in_proj(8 + f, ev_bg)

                def ev_zc(t, bank, f=f):
                    zs_ = cz["zz"] % 2
                    cz["zz"] += 1
                    tc = slice(512 * t, 512 * t + 512)
                    sc.add("act", (lambda e, zs_=zs_, bank=bank: e.activation(out=zsc[zs_], in_=pb[bank][:, :], func=AF.Silu)),
                           reads=[("pb", bank)], writes=[("zsc", zs_)])
                    sc.add("dve", (lambda e, zs_=zs_, tc=tc: e.scalar_tensor_tensor(
                        out=yTc[:, f, tc], in0=yraw[:, tc], scalar=gconv[:, f:f + 1], in1=zsc[zs_],
                        op0=ALU.mult, op1=ALU.mult)),
                        reads=[("yraw", t), ("zsc", zs_), "gconv"], writes=[("yTc", f, t)])
                    sc.add("pool", (lambda e, zs_=zs_, tc=tc: e.tensor_tensor(
                        out=sqc[zs_], in0=yraw[:, tc], in1=yraw[:, tc], op=ALU.mult)),
                        reads=[("yraw", t)], writes=[("sqc", zs_)])
                    for i4 in range(4):
                        col = 16 + 4 * t + i4
                        sc.add("pe", (lambda e, zs_=zs_, i4=i4, col=col: e.matmul(
                            pb[7][:, col:col + 1], lhsT=sqc[zs_][:, 128 * i4:128 * i4 + 128], rhs=ones_b[:, 0:1],
                            start=True, stop=True)),
                            reads=[("sqc", zs_), "ones_b"], writes=[("pb", 7)])

                in_proj(24 + f, ev_zc)
                if f == 0:
                    sc.add("dve", lambda e: e.tensor_copy(out=ssq_c, in_=pb[7][:, 16:32]), reads=[("pb", 7)], writes=["ssq_c"])
                else:
                    sc.add("dve", lambda e: e.tensor_tensor(out=ssq_c, in0=pb[7][:, 16:32], in1=ssq_c, op=ALU.add),
                           reads=[("pb", 7), "ssq_c"], writes=["ssq_c"])

            if stop_after == "yTc":
                sc.barrier()
                sc.add("sp", lambda e: e.dma_start(out=dbg_d[:, 0:8, :], in_=yTc[:, :, :]), dma_key="out")
                sc.add("sp", lambda e: e.dma_start(out=dbg_d[:, 8:16, :], in_=yTa[:, :, :]), dma_key="out")
                done = True

        if not done:
            sc.barrier()
            xt = [uview(8192 * i, F32, [2048]) for i in range(2)]
            yv = [uview(16384 + 8192 * i, F32, [2048]) for i in range(2)]
            junk = uview(32768, BF16, [512])
            gg = wsl[:, :, :, :].rearrange("p a b c -> p (a b c)").bitcast(F32)[:, 0:2048]
            wout = hT
            for src, dst, nm in ((ssq_a, ra, "ra"), (ssq_c, rc, "rc")):
                sc.add("dve", (lambda e, src=src, dst=dst: e.tensor_scalar(out=dst, in0=src, scalar1=1.0 / 1024.0, scalar2=EPS,
                                                                           op0=ALU.mult, op1=ALU.add)), writes=[nm])
                sc.add("act", (lambda e, dst=dst: e.activation(out=dst, in_=dst, func=AF.Sqrt)), reads=[nm], writes=[nm])
                sc.add("dve", (lambda e, dst=dst: e.reciprocal(out=dst, in_=dst)), reads=[nm], writes=[nm])
            for k in range(KC):
                sc.add("pool", (lambda e, k=k: e.dma_start(out=wout[:, k, :], in_=wout_d[128 * k:128 * k + 128, :])),
                       writes=[("wout", k)], dma_key=("wout", k))
            sc.add("sp", lambda e: e.dma_start(out=gg, in_=gpost_d[:, :].partition_broadcast(128)), writes=["gg"], dma_key="gg")
            dg = [yv[1][:, 0:128], yv[1][:, 128:256]]
            for k in range(KC):
                d2 = k % 2
                sc.add("dve", (lambda e, k=k, d2=d2: e.tensor_scalar(out=dg[d2], in0=ident_f[:, :], scalar1=gate_lay[:, k:k + 1],
                                                                    scalar2=None, op0=ALU.mult)),
                       reads=["ident_f", "gate_lay"], writes=[("dg", d2)])
                sc.add("pe", (lambda e, k=k, d2=d2: e.matmul(pb[k // 4][:, 128 * (k % 4):128 * (k % 4) + 128], lhsT=ones_f[:, :],
                                                            rhs=dg[d2], start=True, stop=True)),
                       reads=[("dg", d2), "ones_f"], writes=[("pb", k // 4)])
                if k % 4 == 3:
                    g4 = k // 4
                    sc.add("dve", (lambda e, g4=g4: e.tensor_tensor(out=gg[:, 512 * g4:512 * g4 + 512], in0=pb[g4][:, :],
                                                                   in1=gg[:, 512 * g4:512 * g4 + 512], op=ALU.mult)),
                           reads=[("pb", g4), "gg"], writes=["gg"])
            junk = wsl[:, :, :, :].rearrange("p a b c -> p (a b c)")[:, 4096:6144]

            def p3_mm(i, hf):
                tcol = slice(128 * i, 128 * i + 128)
                for grp in range(2):
                    for n in range(2):
                        bank = 4 * hf + 2 * grp + n
                        dc = slice(1024 * hf + 512 * n, 1024 * hf + 512 * n + 512)
                        for k in range(8 * grp, 8 * grp + 8):
                            lhsT = yTc[:, k, tcol] if k < 8 else yTa[:, k - 8, tcol]
                            sc.add("pe", (lambda e, bank=bank, lhsT=lhsT, k=k, dc=dc: e.matmul(
                                pb[bank][:, :], lhsT=lhsT, rhs=wout[:, k, dc], start=(k % 8 == 0), stop=(k % 8 == 7))),
                                reads=[("wout", k)], writes=[("pb", bank)])

            def p3_evac(i, hf):
                b2 = i % 2
                for n in range(2):
                    dc = slice(1024 * hf + 512 * n, 1024 * hf + 512 * n + 512)
                    q4 = 2 * hf + n
                    extra = [("dg", 0), ("dg", 1)] if b2 == 1 else []
                    sc.add("act", (lambda e, hf=hf, n=n, dc=dc: e.activation(
                        out=yv[b2][:, dc], in_=pb[4 * hf + n][:, :], func=AF.Identity, scale=rc[:, i:i + 1])),
                        reads=[("pb", 4 * hf + n), "rc"] + extra, writes=[("yv", b2, q4)] + extra)
                    sc.add("dve", (lambda e, hf=hf, n=n, dc=dc: e.scalar_tensor_tensor(
                        out=yv[b2][:, dc], in0=pb[4 * hf + 2 + n][:, :], scalar=ra[:, i:i + 1], in1=yv[b2][:, dc],
                        op0=ALU.mult, op1=ALU.add)),
                        reads=[("pb", 4 * hf + 2 + n), "ra", ("yv", b2, q4)], writes=[("yv", b2, q4)])

            def p3_post(i):
                b2 = i % 2
                allyv = [("yv", b2, q4) for q4 in range(4)]
                sc.add("act", (lambda e: e.activation(out=junk, in_=yv[b2], func=AF.Square, accum_out=ssq_p1[:, i:i + 1])),
                       reads=allyv, writes=["junk", ("rp", i)])
                sc.add("dve", (lambda e: e.tensor_scalar(out=rp[:, i:i + 1], in0=ssq_p1[:, i:i + 1], scalar1=1.0 / D,
                                                         scalar2=EPS, op0=ALU.mult, op1=ALU.add)),
                       reads=[("rp", i)], writes=[("rp", i)])
                sc.add("act", (lambda e: e.activation(out=rp[:, i:i + 1], in_=rp[:, i:i + 1], func=AF.Sqrt)),
                       reads=[("rp", i)], writes=[("rp", i)])
                sc.add("dve", (lambda e: e.reciprocal(out=rp[:, i:i + 1], in_=rp[:, i:i + 1])),
                       reads=[("rp", i)], writes=[("rp", i)])
                sc.add("dve", (lambda e: e.scalar_tensor_tensor(
                    out=yv[b2], in0=yv[b2], scalar=rp[:, i:i + 1], in1=gg, op0=ALU.mult, op1=ALU.mult)),
                    reads=allyv + [("rp", i), "gg"], writes=allyv)
                sc.add("pool", (lambda e: e.tensor_tensor(out=xt[b2], in0=yv[b2], in1=xt[b2], op=ALU.add)),
                       reads=allyv + [("xt", b2)], writes=[("xt", b2)])
                sc.add("act", (lambda e: e.dma_start(out=out_d[128 * i:128 * i + 128, :], in_=xt[b2])),
                       reads=[("xt", b2)], writes=[("outdram", i)], dma_key=("out", b2))

            for i in range(16):
                b2 = i % 2
                sc.add("sp", (lambda e, i=i, b2=b2: e.dma_start(out=xt[b2], in_=x_d[128 * i:128 * i + 128, :])),
                       writes=[("xt", b2)], dma_key=("xt", b2))
                p3_mm(i, 0)
                p3_evac(i, 0)
                if i > 0:
                    p3_post(i - 1)
                p3_mm(i, 1)
                p3_evac(i, 1)
            p3_post(15)

        sc.barrier()

        sc.finalize()
        dma_keys = list(sc.dma_cnt.keys())
        eng_sems = {e: es.enter_context(nc.semaphore(f"sem_{e}")) for e in Sched.ENGS}
        dma_sems = {k: es.enter_context(nc.semaphore(f"dsem_{i}")) for i, k in enumerate(dma_keys)}
        block = es.enter_context(nc.Block())

        @block.tensor
        def _(e):
            sc.emit("pe", e, eng_sems, dma_sems)

        @block.scalar
        def _(e):
            sc.emit("act", e, eng_sems, dma_sems)

        @block.vector
        def _(e):
            sc.emit("dve", e, eng_sems, dma_sems)

        @block.gpsimd
        def _(e):
            sc.emit("pool", e, eng_sems, dma_sems)

        @block.sync
        def _(e):
            sc.emit("sp", e, eng_sems, dma_sems)

    return nc


def _lay(v, nchunk):
    return np.ascontiguousarray(np.asarray(v, np.float32).reshape(nchunk, 128).T)


def make_in_maps(x, c, w_ada, b_ada, g_pre, w_in, conv_w, conv_b, g_conv, g_attn, w_out, g_post):
    x = np.asarray(x, np.float32)
    c = np.asarray(c, np.float32)
    w_ada0 = np.ascontiguousarray(np.asarray(w_ada, np.float32)[0])
    w_in0 = np.ascontiguousarray(np.asarray(w_in, np.float32)[0])
    w_out0 = np.ascontiguousarray(np.asarray(w_out, np.float32)[0])
    cw = np.asarray(conv_w, np.float32)[0]
    convw_lay = np.ascontiguousarray(cw.reshape(3, 8, 128).transpose(2, 1, 0))
    shared = {
        "w_ada": w_ada0,
        "b_row": np.ascontiguousarray(np.asarray(b_ada, np.float32)[0].reshape(1, NMOD)),
        "gpre_lay": _lay(np.asarray(g_pre)[0], 16),
        "w_in": w_in0,
        "convw_lay": convw_lay,
        "convb_lay": _lay(np.asarray(conv_b)[0], 8),
        "gconv_lay": _lay(np.asarray(g_conv)[0], 8),
        "gattn_lay": _lay(np.asarray(g_attn)[0], 8),
        "w_out": w_out0,
        "gpost_row": np.ascontiguousarray(np.asarray(g_post, np.float32)[0].reshape(1, D)),
    }
    maps = []
    for b in range(N_CORES):
        m = dict(shared)
        m["x"] = np.ascontiguousarray(x[b])
        m["c_lay"] = _lay(c[b], 16)
        maps.append(m)
    return maps


_NC_CACHE = {}


def kernel(x, c, w_ada, b_ada, g_pre, w_in, conv_w, conv_b, g_conv, g_attn, w_out, g_post):
    if "nc" not in _NC_CACHE:
        _NC_CACHE["nc"] = build_nc()
    nc = _NC_CACHE["nc"]
    in_maps = make_in_maps(x, c, w_ada, b_ada, g_pre, w_in, conv_w, conv_b, g_conv, g_attn, w_out, g_post)
    res = run_bass_kernel_spmd(nc, in_maps, core_ids=list(range(N_CORES)))
    out = np.stack([np.asarray(r["out"], np.float32) for r in res.results], axis=0)
    return out
```

```python
import numpy as np
import concourse.bass as bass
import concourse.mybir as mybir
from concourse.bass_utils import run_bass_kernel_spmd

F32 = mybir.dt.float32
BF16 = mybir.dt.bfloat16
AF = mybir.ActivationFunctionType
ALU = mybir.AluOpType
AX = mybir.AxisListType

D = 2048
S = 2048
NPROJ = 8192
NMOD = 6144
KC = 16
EPS = 1e-6
NEG8 = -8.0 * 30000.0
N_CORES = 8
DBG = {"pair": -1}
SCHED = {"burst": 1, "a_per_step": 2, "a_tr": 4, "scores_first": True}


class _Op:
    __slots__ = ("eng", "fn", "deps", "dma_key", "dma_n", "signal", "sig")

    def __init__(self, eng, fn):
        self.eng = eng
        self.fn = fn
        self.deps = []
        self.dma_key = None
        self.dma_n = 0
        self.signal = False
        self.sig = 0


class _DmaTok:
    __slots__ = ("key", "n")

    def __init__(self, key, n):
        self.key = key
        self.n = n


class Sched:
    ENGS = ("pe", "act", "dve", "pool", "sp")

    def __init__(self):
        self.streams = {e: [] for e in self.ENGS}
        self.res_w = {}
        self.res_r = {}
        self.dma_cnt = {}
        self.last = {}

    @staticmethod
    def _grp(tok):
        return ("dma", tok.key) if isinstance(tok, _DmaTok) else tok.eng

    def add(self, eng, fn, reads=(), writes=(), dma_key=None, extra_deps=()):
        op = _Op(eng, fn)
        deps = {}
        for r in reads:
            for t in self.res_w.get(r, {}).values():
                deps[id(t)] = t
        for w in writes:
            for t in self.res_w.get(w, {}).values():
                deps[id(t)] = t
            for t in self.res_r.get(w, {}).values():
                deps[id(t)] = t
        for t in extra_deps:
            deps[id(t)] = t
        if dma_key is not None:
            n = self.dma_cnt.get(dma_key, 0) + 1
            self.dma_cnt[dma_key] = n
            op.dma_key = dma_key
            op.dma_n = n
            tok = _DmaTok(dma_key, n)
        else:
            tok = op
        g = self._grp(tok)
        for r in reads:
            self.res_r.setdefault(r, {})[g] = tok
        for w in writes:
            self.res_w[w] = {g: tok}
            self.res_r[w] = {}
        deps.pop(id(op), None)
        op.deps = list(deps.values())
        self.streams[eng].append(op)
        if fn is not None:
            self.last[g] = tok
        return tok

    def barrier(self):
        toks = list(self.last.values())
        for e in self.ENGS:
            self.add(e, None, extra_deps=toks)

    def finalize(self):
        for e in self.ENGS:
            for op in self.streams[e]:
                for d in op.deps:
                    if isinstance(d, _Op):
                        if d.eng == "pe" and op.eng == "pe":
                            continue
                        d.signal = True
        for e in self.ENGS:
            c = 0
            for op in self.streams[e]:
                if op.signal:
                    c += 1
                    op.sig = c

    def emit(self, eng_name, e, eng_sems, dma_sems):
        waited = {}
        for op in self.streams[eng_name]:
            need = {}
            for d in op.deps:
                if isinstance(d, _Op):
                    if d.eng == "pe" and eng_name == "pe":
                        continue
                    g, v = d.eng, d.sig
                else:
                    g = ("dma", d.key)
                    v = 16 * (self.dma_cnt[d.key] if d.key == "const" else d.n)
                if v > need.get(g, 0):
                    need[g] = v
            for g, v in need.items():
                if waited.get(g, 0) >= v:
                    continue
                waited[g] = v
                sem = dma_sems[g[1]] if isinstance(g, tuple) else eng_sems[g]
                e.wait_ge(sem, v)
            if op.fn is None:
                continue
            ins = op.fn(e)
            if op.dma_key is not None:
                ins.then_inc(dma_sems[op.dma_key], 16)
            elif op.signal:
                ins.then_inc(eng_sems[eng_name], 1)


def _slopes():
    return [2.0 ** (-8.0 * (h + 1) / 16.0) for h in range(16)]


def build_nc(stop_after=None, npairs=8):
    nc = bass.Bass("TRN2", target_bir_lowering=False)
    x_d = nc.dram_tensor("x", [S, D], F32, kind="ExternalInput").ap()
    c_d = nc.dram_tensor("c_lay", [128, KC], F32, kind="ExternalInput").ap()
    wada_d = nc.dram_tensor("w_ada", [D, NMOD], F32, kind="ExternalInput").ap()
    bada_d = nc.dram_tensor("b_row", [1, NMOD], F32, kind="ExternalInput").ap()
    gpre_d = nc.dram_tensor("gpre_lay", [128, KC], F32, kind="ExternalInput").ap()
    win_d = nc.dram_tensor("w_in", [D, NPROJ], F32, kind="ExternalInput").ap()
    convw_d = nc.dram_tensor("convw_lay", [128, 8, 3], F32, kind="ExternalInput").ap()
    convb_d = nc.dram_tensor("convb_lay", [128, 8], F32, kind="ExternalInput").ap()
    gconv_d = nc.dram_tensor("gconv_lay", [128, 8], F32, kind="ExternalInput").ap()
    gattn_d = nc.dram_tensor("gattn_lay", [128, 8], F32, kind="ExternalInput").ap()
    wout_d = nc.dram_tensor("w_out", [D, D], F32, kind="ExternalInput").ap()
    gpost_d = nc.dram_tensor("gpost_row", [1, D], F32, kind="ExternalInput").ap()
    out_d = nc.dram_tensor("out", [S, D], F32, kind="ExternalOutput").ap()
    dbg_d = None
    if stop_after is not None:
        dbg_d = nc.dram_tensor("dbg", [128, 16, 2048], BF16, kind="ExternalOutput").ap()

    sc = Sched()
    UF = 23040

    import contextlib
    with contextlib.ExitStack() as es:
        def sb(name, shape, dt):
            return es.enter_context(nc.sbuf_tensor(name, shape, dt))

        hT = sb("hT", [128, KC, 2048], BF16)
        yTa = sb("yTa", [128, 8, 2048], BF16)
        U = sb("U", [128, UF], F32)
        wsl = sb("wsl", [128, 3, KC, 128], BF16)
        ident_f = sb("ident_f", [128, 128], F32)
        ident_b = sb("ident_b", [128, 128], BF16)
        ones_f = sb("ones_f", [128, 128], F32)
        ones_b = sb("ones_b", [128, 128], BF16)
        base_f = sb("base_f", [128, 256], F32)
        mask8 = sb("mask8", [128, 256], F32)
        sm = sb("sm", [128, 512], F32)
        cab = sb("cab", [128, KC], BF16)
        pb = [es.enter_context(nc.psum_tensor(f"pb{i}", [128, 512], F32)) for i in range(8)]

        c_sb = sm[:, 0:16]
        gpre = sm[:, 16:32]
        a_sb = sm[:, 32:48]
        s_sb = sm[:, 48:64]
        gate_lay = sm[:, 64:80]
        convw = sm[:, 80:104].rearrange("p (f t) -> p f t", t=3)
        convb = sm[:, 104:112]
        gconv = sm[:, 112:120]
        gattn = sm[:, 120:128]
        ssq_x = sm[:, 128:144]
        rstd_x = sm[:, 144:160]
        ssq_a = sm[:, 160:176]
        ssq_c = sm[:, 176:192]
        ra = sm[:, 192:208]
        rc = sm[:, 208:224]
        ssq_p = sm[:, 224:288]
        ssq_p1 = sm[:, 288:304]
        rp = sm[:, 304:320]
        tmp16 = sm[:, 320:336]

        def uview(off, dt, shape):
            n = 1
            for s_ in shape:
                n *= s_
            nb = n * (4 if dt == F32 else 2)
            assert off % 4 == 0 and nb % 4 == 0 and off + nb <= UF * 4, (off, nb)
            a = U[:, off // 4:(off + nb) // 4]
            if dt != F32:
                a = a.bitcast(dt)
            if len(shape) == 2:
                return a.rearrange("p (a b) -> p a b", b=shape[1])
            if len(shape) == 3:
                return a.rearrange("p (a b c) -> p a b c", b=shape[1], c=shape[2])
            return a

        def psbf(bank):
            return pb[bank][:, :].bitcast(BF16)

        sc.add("pool", lambda e: e.iota(base_f[:, :], [[1, 256]], base=-64, channel_multiplier=-1, allow_small_or_imprecise_dtypes=True),
               writes=["base_f"])
        sc.add("dve", lambda e: e.tensor_single_scalar(out=ident_f[:, :], in_=base_f[:, 64:192], scalar=0.0, op=ALU.is_equal),
               reads=["base_f"], writes=["ident_f"])
        sc.add("dve", lambda e: e.tensor_copy(out=ident_b[:, :], in_=ident_f[:, :]), reads=["ident_f"], writes=["ident_b"])
        sc.add("dve", lambda e: e.memset(ones_f[:, :], 1.0), writes=["ones_f"])
        sc.add("dve", lambda e: e.memset(ones_b[:, :], 1.0), writes=["ones_b"])
        sc.add("dve", lambda e: e.tensor_scalar(out=mask8[:, :], in0=base_f[:, :], scalar1=-1.0, scalar2=None, op0=ALU.mult),
               reads=["ident_f", "base_f"], writes=["mask8"])
        sc.add("dve", lambda e: e.tensor_tensor(out=base_f[:, :], in0=base_f[:, :], in1=mask8[:, :], op=ALU.max),
               reads=["mask8"], writes=["base_f"])
        sc.add("dve", lambda e: e.tensor_scalar(out=mask8[:, :], in0=base_f[:, :], scalar1=64.5, scalar2=NEG8,
                                                op0=ALU.is_gt, op1=ALU.mult),
               reads=["base_f"], writes=["mask8"])

        def cdma(dst, src, key):
            sc.add("sp", lambda e: e.dma_start(out=dst, in_=src), writes=[key], dma_key="const")

        cdma(c_sb, c_d[:, :], "c_sb")
        cdma(gpre, gpre_d[:, :], "gpre")
        cdma(convw, convw_d[:, :, :], "convw")
        cdma(convb, convb_d[:, :], "convb")
        cdma(gconv, gconv_d[:, :], "gconv")
        cdma(gattn, gattn_d[:, :], "gattn")

        def dump_small():
            sc.barrier()
            sc.add("sp", lambda e: e.dma_start(out=dbg_d[:, 0, 0:256], in_=sm[:, 0:256].bitcast(BF16)[:, 0:256]), dma_key="out")
            sc.add("sp", lambda e: e.dma_start(out=out_d[0:128, 0:256], in_=base_f[:, :]), dma_key="out")
            sc.add("sp", lambda e: e.dma_start(out=out_d[128:256, 0:256], in_=mask8[:, :]), dma_key="out")
            sc.add("sp", lambda e: e.dma_start(out=out_d[256:384, 0:128], in_=ident_f[:, :]), dma_key="out")
            sc.add("sp", lambda e: e.dma_start(out=out_d[384:512, 0:512], in_=sm[:, :]), dma_key="out")
        done = False
        if stop_after == "const":
            dump_small()
            done = True
        mod_row = U[0:1, 0:NMOD]
        b_row = U[0:1, NMOD:2 * NMOD]
        xs = [uview(2 * NMOD * 4 + 8192 * i, F32, [2048]) for i in range(2)]
        xn = [yTa[:, i, :] for i in range(2)]
        if not done:
          cdma(b_row, bada_d[:, :], "b_row")
          sc.add("act", lambda e: e.activation(out=cab[:, :], in_=c_sb, func=AF.Silu), reads=["c_sb"], writes=["cab"])

          wa_slots = [yTa[:, 2:6, :].rearrange("p a (b n) -> p (a b) n", n=512), uview(65536, BF16, [16, 512])]

          def emit_slice(s_):
              slot = s_ % 2
              dst = wa_slots[slot]
              src = wada_d[:, 512 * s_:512 * s_ + 512].rearrange("(k p) n -> p k n", p=128)
              sc.add("pool", (lambda e: e.dma_start(out=dst, in_=src)), writes=[("waslot", slot)], dma_key=("wa", slot))
              bank = s_ % 2
              for k in range(KC):
                  sc.add("pe", (lambda e, k=k: e.matmul(
                      pb[bank][0:1, 0:512], lhsT=cab[:, k:k + 1], rhs=dst[:, k, :], start=(k == 0), stop=(k == KC - 1))),
                      reads=[("waslot", slot), "cab"], writes=[("pb", bank)])
              sc.add("dve", (lambda e: e.tensor_tensor(
                  out=mod_row[:, 512 * s_:512 * s_ + 512], in0=pb[bank][0:1, 0:512],
                  in1=b_row[:, 512 * s_:512 * s_ + 512], op=ALU.add)),
                  reads=[("pb", bank), "b_row"], writes=[("mod", s_)])

          def emit_tile(i):
              b2 = i % 2
              sc.add("sp", (lambda e: e.dma_start(out=xs[b2], in_=x_d[128 * i:128 * i + 128, :])),
                     writes=[("xs", b2)], dma_key=("xs", b2))
              sc.add("act", (lambda e: e.activation(out=xn[b2], in_=xs[b2], func=AF.Square, accum_out=ssq_x[:, i:i + 1])),
                     reads=[("xs", b2)], writes=[("xn", b2), ("ssq_x", i)])
              sc.add("dve", (lambda e: e.tensor_scalar(out=rstd_x[:, i:i + 1], in0=ssq_x[:, i:i + 1], scalar1=1.0 / D,
                                                       scalar2=EPS, op0=ALU.mult, op1=ALU.add)),
                     reads=[("ssq_x", i)], writes=[("rstd_x", i)])
              sc.add("act", (lambda e: e.activation(out=rstd_x[:, i:i + 1], in_=rstd_x[:, i:i + 1], func=AF.Sqrt)),
                     reads=[("rstd_x", i)], writes=[("rstd_x", i)])
              sc.add("dve", (lambda e: e.reciprocal(out=rstd_x[:, i:i + 1], in_=rstd_x[:, i:i + 1])),
                     reads=[("rstd_x", i)], writes=[("rstd_x", i)])
              sc.add("dve", (lambda e: e.tensor_scalar(out=xn[b2], in0=xs[b2], scalar1=rstd_x[:, i:i + 1],
                                                       scalar2=None, op0=ALU.mult)),
                     reads=[("xs", b2), ("rstd_x", i)], writes=[("xn", b2)])
              for g in range(2):
                  bank = 4 + 2 * b2 + g
                  for kk in range(8):
                      k = 8 * g + kk
                      sc.add("pe", (lambda e, kk=kk, k=k, bank=bank: e.transpose(
                          psbf(bank)[:, 128 * kk:128 * kk + 128], xn[b2][:, 128 * k:128 * k + 128], ident_b[:, :])),
                          reads=[("xn", b2), "ident_b"], writes=[("pb", bank)])
                  src3 = psbf(bank).rearrange("p (c f) -> p c f", f=128)
                  dst3 = hT[:, 8 * g:8 * g + 8, 128 * i:128 * i + 128]
                  hkeys = [("hT", k) for k in range(8 * g, 8 * g + 8)]
                  sc.add("dve", (lambda e, src3=src3, dst3=dst3: e.tensor_copy(out=dst3, in_=src3)),
                         reads=[("pb", bank)], writes=hkeys)

          n_sl = 0
          for i in range(16):
              emit_tile(i)
              while n_sl < (12 * (i + 1)) // 16:
                  emit_slice(n_sl)
                  n_sl += 1
          for j in range(48):
              sc.add("pe", (lambda e, j=j: e.matmul(pb[2][:, j:j + 1], lhsT=mod_row[:, 128 * j:128 * j + 128],
                                                    rhs=ones_f[0:1, 0:1], start=True, stop=True)),
                     reads=[("mod", j // 4), "ones_f"], writes=[("pb", 2)])
          sc.add("dve", lambda e: e.tensor_copy(out=s_sb, in_=pb[2][:, 0:16]), reads=[("pb", 2)], writes=["s_sb"])
          sc.add("dve", lambda e: e.scalar_tensor_tensor(out=a_sb, in0=pb[2][:, 16:32], scalar=1.0, in1=gpre,
                                                         op0=ALU.add, op1=ALU.mult),
                 reads=[("pb", 2), "gpre"], writes=["a_sb"])
          sc.add("dve", lambda e: e.tensor_copy(out=gate_lay, in_=pb[2][:, 32:48]), reads=[("pb", 2)], writes=["gate_lay"])
          for k in range(KC):
              if k % 2 == 0:
                  sc.add("dve", (lambda e, k=k: e.tensor_scalar(out=hT[:, k, :], in0=hT[:, k, :], scalar1=a_sb[:, k:k + 1],
                                                                scalar2=s_sb[:, k:k + 1], op0=ALU.mult, op1=ALU.add)),
                         reads=[("hT", k), "a_sb", "s_sb"], writes=[("hT", k)])
              else:
                  sc.add("act", (lambda e, k=k: e.activation(out=hT[:, k, :], in_=hT[:, k, :], func=AF.Identity,
                                                             bias=s_sb[:, k:k + 1], scale=a_sb[:, k:k + 1])),
                         reads=[("hT", k), "a_sb", "s_sb"], writes=[("hT", k)])

        def dump(src_ap, nk):
            sc.barrier()
            sc.add("sp", lambda e: e.dma_start(out=dbg_d[:, 0:nk, :], in_=src_ap), dma_key="out")
            sc.add("sp", lambda e: e.dma_start(out=out_d[0:128, :], in_=xs[0]), dma_key="out")

        if stop_after == "mod" and not done:
            dump_small()
            done = True
        if stop_after == "hT" and not done:
            dump(hT[:, :, :], 16)
            done = True

        state = {"wslot": 0, "bank": 0, "nbank": 3}

        order = [32, 40, 48]
        for p_ in range(npairs):
            if p_ >= 1:
                order.append(56 + p_ - 1)
            if p_ + 1 < npairs:
                order += [32 + p_ + 1, 40 + p_ + 1, 48 + p_ + 1]
        order.append(56 + npairs - 1)
        for f_ in range(8):
            order += [f_, 16 + f_, 8 + f_, 24 + f_]
        state["issued"] = 0

        def issue_w(upto):
            while state["issued"] < min(upto, len(order)):
                n_ = state["issued"]
                j_ = order[n_]
                slot_ = n_ % 3
                src = win_d[:, 128 * j_:128 * j_ + 128].rearrange("(k p) n -> p k n", p=128)
                sc.add("pool", (lambda e, slot_=slot_, src=src: e.dma_start(out=wsl[:, slot_], in_=src)),
                       writes=[("wsl", slot_)], dma_key=("wsl", slot_))
                state["issued"] += 1

        def in_proj_units(j, evac):
            n_ = state["wslot"]
            assert order[n_] == j, (n_, j, order[n_])
            slot = n_ % 3
            state["wslot"] += 1
            issue_w(n_ + 3)
            for t in range(4):
                bank = state["bank"] % state["nbank"]
                state["bank"] += 1
                for k in range(KC):
                    sc.add("pe", (lambda e, bank=bank, k=k, t=t, slot=slot: e.matmul(
                        pb[bank][:, :], lhsT=wsl[:, slot, k, :], rhs=hT[:, k, 512 * t:512 * t + 512],
                        start=(k == 0), stop=(k == KC - 1))),
                        reads=[("wsl", slot), ("hT", k)], writes=[("pb", bank)])
                    if k == KC - 1:
                        evac(t, bank)
                    if k % 2 == 1:
                        yield

        def in_proj(j, evac):
            for _ in in_proj_units(j, evac):
                pass

        def merge(a_gens, b_gen):
            import itertools
            A = itertools.chain(*a_gens)
            B = b_gen if b_gen is not None else iter(())
            a_done = b_done = False
            while not (a_done and b_done):
                n_a = SCHED["a_per_step"]
                if not b_done:
                    try:
                        r_ = next(B)
                        if r_ is not None:
                            n_a = r_
                    except StopIteration:
                        b_done = True
                if b_done:
                    n_a = 8
                for _ in range(n_a):
                    if not a_done:
                        try:
                            next(A)
                        except StopIteration:
                            a_done = True

        slopes = _slopes()

        if not done:
            sc.barrier()
            state["nbank"] = 2
            state["bank"] = 0
            qTb = [uview(0, BF16, [2048]), uview(4096, BF16, [2048])]
            kTb = [uview(8192, BF16, [2048]), uview(12288, BF16, [2048])]
            vT = uview(16384, BF16, [2048])
            Vaug = uview(20480, BF16, [3, 16, 192])
            oaccb = [uview(38912, F32, [2048]), uview(47104, F32, [2048])]
            opair = uview(55296, F32, [2048])
            pT2 = [uview(63488 + 1024 * i, BF16, [512]) for i in range(4)]
            expb2 = [uview(67584 + 1024 * i, BF16, [512]) for i in range(2)]
            bias8 = [uview(69632 + 1024 * i, F32, [256]) for i in range(2)]
            rden = [uview(71680 + 2048 * i, F32, [512]) for i in range(2)]
            sqa = uview(75776, BF16, [2048])
            zraw = uview(79872, BF16, [2048])
            q4b = uview(83968, BF16, [2048])
            q16b = uview(88064, BF16, [2048])

            sc.add("dve", lambda e: e.memset(Vaug[:, :, :, 64:128], 1.0), writes=["Vaug"])
            cnt = {"ss": 0, "blk": 0, "bias": 0, "fin": 0, "zz": 0}
            ORD = ((1, 16), (4, 4), (16, 1))
            LOOKU = 3
            BURST = SCHED["burst"]

            def mk_ev_q(pp):
                def ev(t, bank):
                    sc.add("act", (lambda e: e.activation(out=qTb[pp % 2][:, 512 * t:512 * t + 512], in_=pb[bank][:, :], func=AF.Copy)),
                           reads=[("pb", bank)], writes=[("qT", pp % 2, t)])
                return ev

            def mk_ev_k(pp):
                def ev(t, bank):
                    sc.add("dve", (lambda e: e.tensor_copy(out=kTb[pp % 2][:, 512 * t:512 * t + 512], in_=pb[bank][:, :])),
                           reads=[("pb", bank)], writes=[("kT", pp % 2, t)])
                return ev

            def ev_v(t, bank):
                sc.add("act", (lambda e: e.activation(out=vT[:, 512 * t:512 * t + 512], in_=pb[bank][:, :], func=AF.Copy)),
                       reads=[("pb", bank)], writes=[("vT", t)])

            def mk_ev_z(pp, deferred=None):
                def ev(t, bank):
                    tc = slice(512 * t, 512 * t + 512)
                    sc.add("act", (lambda e: e.activation(out=zraw[:, tc], in_=pb[bank][:, :], func=AF.Copy)),
                           reads=[("pb", bank)], writes=[("zraw", t)])
                    if t < 3:
                        return
                    if deferred is not None:
                        deferred.append(post)
                    else:
                        post()

                def post():
                    allz = [("zraw", t_) for t_ in range(4)]
                    allo = [("opair", t_, h_) for t_ in range(4) for h_ in range(2)]
                    sc.add("act", (lambda e: e.activation(out=zraw[:, :], in_=zraw[:, :], func=AF.Silu)),
                           reads=allz, writes=allz)
                    sc.add("dve", (lambda e: e.scalar_tensor_tensor(
                        out=yTa[:, pp, :], in0=opair[:, :], scalar=gattn[:, pp:pp + 1], in1=zraw[:, :],
                        op0=ALU.mult, op1=ALU.mult)),
                        reads=allo + allz + ["gattn"], writes=[("yTa", pp, t_) for t_ in range(4)])
                    sc.add("pool", (lambda e: e.tensor_tensor(out=sqa[:, :], in0=opair[:, :], in1=opair[:, :], op=ALU.mult)),
                           reads=allo, writes=["sqa"])
                    for col in range(16):
                        sc.add("pe", (lambda e, col=col: e.matmul(
                            pb[7][:, col:col + 1], lhsT=sqa[:, 128 * col:128 * col + 128], rhs=ones_b[:, 0:1],
                            start=True, stop=True)),
                            reads=["sqa", "ones_b"], writes=[("pb", 7)])
                    if pp == 0:
                        sc.add("dve", lambda e: e.tensor_copy(out=ssq_a, in_=pb[7][:, 0:16]),
                               reads=[("pb", 7)], writes=["ssq_a"])
                    else:
                        sc.add("dve", lambda e: e.tensor_tensor(out=ssq_a, in0=pb[7][:, 0:16], in1=ssq_a, op=ALU.add),
                               reads=[("pb", 7), "ssq_a"], writes=["ssq_a"])
                return ev

            def attention_steps(p):
                qT = qTb[p % 2]
                kT = kTb[p % 2]
                allv = [("vT", t) for t in range(4)]
                allq = [("qT", p % 2, t) for t in range(4)]
                sc.add("pool", (lambda e: e.tensor_copy(out=q4b.rearrange("p (r q) -> p r q", r=4),
                                                        in_=qT.rearrange("p (q r) -> p r q", r=4))),
                       reads=allq, writes=["q4b"])
                sc.add("pool", (lambda e: e.tensor_copy(out=q16b.rearrange("p (r q) -> p r q", r=16),
                                                        in_=qT.rearrange("p (q r) -> p r q", r=16))),
                       reads=allq, writes=["q16b"])
                TR_BANKS = (7, 2, 3, 4)
                for oi, (r, T) in enumerate(ORD):
                    for g in range(2):
                        tb = TR_BANKS[(2 * oi + g) % 4]
                        for cc in range(8):
                            c = 8 * g + cc
                            res, ci = c // T, c % T
                            st0 = res + r * 128 * ci
                            cols = slice(st0, st0 + r * 127 + 1, r)
                            sc.add("pe", (lambda e, cc=cc, cols=cols, tb=tb: e.transpose(
                                psbf(tb)[:, 128 * cc:128 * cc + 128], vT[:, cols], ident_b[:, :])),
                                reads=allv + ["ident_b"], writes=[("pb", tb)])
                        src3 = psbf(tb).rearrange("p (c f) -> p c f", f=128)
                        if (2 * oi + g) % 2 == 0:
                            sc.add("dve", (lambda e, oi=oi, g=g, src3=src3: e.tensor_copy(
                                out=Vaug[:, oi, 8 * g:8 * g + 8, 0:64], in_=src3[:, :, 0:64])),
                                reads=[("pb", tb)], writes=[("Vaug", oi, g, 0)])
                            sc.add("dve", (lambda e, oi=oi, g=g, src3=src3: e.tensor_copy(
                                out=Vaug[:, oi, 8 * g:8 * g + 8, 128:192], in_=src3[:, :, 64:128])),
                                reads=[("pb", tb)], writes=[("Vaug", oi, g, 1)])
                        else:
                            sc.add("act", (lambda e, oi=oi, g=g, src3=src3: e.activation(
                                out=Vaug[:, oi, 8 * g:8 * g + 8, 0:64], in_=src3[:, :, 0:64], func=AF.Copy)),
                                reads=[("pb", tb)], writes=[("Vaug", oi, g, 0)])
                            sc.add("act", (lambda e, oi=oi, g=g, src3=src3: e.activation(
                                out=Vaug[:, oi, 8 * g:8 * g + 8, 128:192], in_=src3[:, :, 64:128], func=AF.Copy)),
                                reads=[("pb", tb)], writes=[("Vaug", oi, g, 1)])
                        yield SCHED["a_tr"]

                tiles = []
                for hh in range(2):
                    for oi, (r, T) in enumerate(ORD):
                        for res in range(r):
                            for c in range(T):
                                tiles.append(dict(hh=hh, oi=oi, r=r, T=T, res=res, c=c,
                                                  first_of_branch=(res == 0 and c == 0),
                                                  last_of_head=(oi == 2 and res == r - 1 and c == T - 1)))
                grp_bank = {}
                grp_done = {}

                def scores(unit, uidx):
                    tA = unit[0]
                    hh, oi, r = tA["hh"], tA["oi"], tA["r"]
                    h = 2 * p + hh
                    L = S // r
                    R = slice(64 * hh, 64 * hh + 64)
                    if tA["first_of_branch"]:
                        bslot = cnt["bias"] % 2
                        cnt["bias"] += 1
                        coef = -8.0 * slopes[h] * r
                        sc.add("dve", (lambda e: e.scalar_tensor_tensor(
                            out=bias8[bslot], in0=base_f[:, :], scalar=coef, in1=mask8[:, :], op0=ALU.mult, op1=ALU.add)),
                            reads=["base_f", "mask8"], writes=[("bias8", bslot)])
                        for hf in range(2):
                            sc.add("act", (lambda e, hf=hf: e.activation(out=expb2[bslot][:, 256 * hf:256 * hf + 256], in_=bias8[bslot],
                                                                      func=AF.Exp, scale=0.125)),
                                   reads=[("bias8", bslot)], writes=[("expb2", bslot)])
                        cnt["cur_bias", hh, oi] = bslot
                    bslot = cnt["cur_bias", hh, oi]
                    bank = 2 + cnt["ss"] % 3
                    s4 = cnt["ss"] % 4
                    cnt["ss"] += 1
                    lo, hi = None, None
                    for idx, tl in enumerate(unit):
                        T, res, c = tl["T"], tl["res"], tl["c"]
                        q0 = max(0, 128 * c - 64)
                        q1 = min(L, 128 * c + 192)
                        j0 = q0 - (128 * c - 64)
                        nq = q1 - q0
                        kcols = slice(res + r * 128 * c, res + r * (128 * c + 127) + 1, r)
                        qcols = slice(res + r * q0, res + r * (q1 - 1) + 1, r)
                        cb = 256 * idx + j0
                        tl.update(j0=j0, nq=nq, s4=s4, cbase=256 * idx)
                        if lo is None:
                            lo = cb
                        hi = cb + nq
                        ssap = pb[bank][:, cb:cb + nq]
                        tq = sorted({(res + r * q0) // 512, (res + r * (q1 - 1)) // 512} | (set(range(4)) if r > 1 else set()))
                        tk = sorted({(res + r * 128 * c) // 512, (res + r * (128 * c + 127)) // 512} | (set(range(4)) if r > 1 else set()))
                        if r == 1:
                            qsrc, qc2, qkeys = qT, qcols, [("qT", p % 2, t_) for t_ in tq]
                        elif r == 4:
                            qsrc, qc2, qkeys = q4b, slice(res * L + q0, res * L + q1), ["q4b"]
                        else:
                            qsrc, qc2, qkeys = q16b, slice(res * L + q0, res * L + q1), ["q16b"]
                        sc.add("pe", (lambda e, ssap=ssap, kcols=kcols, qsrc=qsrc, qc2=qc2: e.matmul(
                            ssap, lhsT=kT[R, kcols], rhs=qsrc[R, qc2], start=True, stop=True)),
                            reads=qkeys + [("kT", p % 2, t_) for t_ in tk], writes=[("pb", bank)])
                    sc.add("act", (lambda e: e.activation(out=pT2[s4][:, lo:hi], in_=pb[bank][:, lo:hi], func=AF.Exp, scale=0.125)),
                           reads=[("pb", bank)], writes=[("pT2", s4)])
                    meng = "dve"
                    sc.add(meng, (lambda e: e.tensor_tensor(out=pT2[s4][:, lo:hi], in0=pT2[s4][:, lo:hi], in1=expb2[bslot][:, lo:hi], op=ALU.mult)),
                           reads=[("pT2", s4), ("expb2", bslot)], writes=[("pT2", s4)])

                def pv(tl):
                    hh, oi, r, T, res, c = tl["hh"], tl["oi"], tl["r"], tl["T"], tl["res"], tl["c"]
                    j0, nq, s4, cbase = tl["j0"], tl["nq"], tl["s4"], tl["cbase"]
                    L = S // r
                    oacc = oaccb[hh]
                    lhsT_v = Vaug[:, oi, res * T + c, 64 * hh:64 * hh + 128]
                    vkeys = [("Vaug", oi, (res * T + c) // 8, 0), ("Vaug", oi, (res * T + c) // 8, 1), "Vaug"]
                    jbase = 128 * c - 64
                    for half in range(2):
                        bI = c + half
                        if T == 1:
                            if half == 1:
                                continue
                            p0, p1, st_, sp_ = 0, L, True, True
                        else:
                            p0 = max(0, 128 * bI - 64)
                            p1 = min(L, 128 * bI + 64)
                            if half == 0:
                                st_, sp_ = (c == 0), True
                            else:
                                st_, sp_ = True, (c == T - 1)
                        pos = p0
                        while pos < p1:
                            nxt = min(p1, (pos // 512 + 1) * 512)
                            if r == 16:
                                gkey = (hh, oi, res // 4, 0)
                                col0 = 128 * (res % 4) + pos
                                extent = 512
                            else:
                                gkey = (hh, oi, res, pos // 512)
                                col0 = pos % 512
                                extent = min(L, (pos // 512 + 1) * 512) - (pos // 512) * 512
                            if gkey not in grp_bank:
                                grp_bank[gkey] = 5 + cnt["blk"] % 2
                                cnt["blk"] += 1
                                grp_done[gkey] = 0
                            bank = grp_bank[gkey]
                            oap = pb[bank][:, col0:col0 + (nxt - pos)]
                            ja, jb = pos - jbase, nxt - jbase
                            sc.add("pe", (lambda e, oap=oap, ja=ja, jb=jb, st_=st_, sp_=sp_: e.matmul(
                                oap, lhsT=lhsT_v, rhs=pT2[s4][:, cbase + ja:cbase + jb], start=st_, stop=sp_)),
                                reads=[("pT2", s4)] + vkeys, writes=[("pb", bank)])
                            if sp_:
                                grp_done[gkey] += nxt - pos
                                if grp_done[gkey] == extent:
                                    if r == 16:
                                        gi = res // 4
                                        oview = oacc.rearrange("p (q k) -> p q k", k=16)[:, :, 4 * gi:4 * gi + 4]
                                        iview = pb[bank][:, 0:512].rearrange("p (k q) -> p q k", k=4)
                                        okeys = [("oacc", hh, t_) for t_ in range(4)]
                                    else:
                                        g0 = (pos // 512) * 512
                                        oview = oacc[:, slice(res + r * g0, res + r * (g0 + extent - 1) + 1, r)]
                                        iview = pb[bank][:, 0:extent]
                                        okeys = [("oacc", hh, t_) for t_ in (range(4) if r > 1 else [g0 // 512])]
                                    if oi == 0:
                                        sc.add("act", (lambda e, oview=oview, iview=iview: e.activation(out=oview, in_=iview, func=AF.Copy)),
                                               reads=[("pb", bank)], writes=okeys)
                                    else:
                                        sc.add("dve", (lambda e, oview=oview, iview=iview: e.tensor_tensor(
                                            out=oview, in0=iview, in1=oview, op=ALU.add)),
                                            reads=[("pb", bank)] + okeys, writes=okeys)
                            pos = nxt

                def finalize(hh):
                    oacc = oaccb[hh]
                    R = slice(64 * hh, 64 * hh + 64)
                    DR = slice(64 * (1 - hh), 64 * (1 - hh) + 64)
                    for t in range(4):
                        fs = cnt["fin"] % 2
                        cnt["fin"] += 1
                        tc = slice(512 * t, 512 * t + 512)
                        sc.add("act", (lambda e, fs=fs, tc=tc: e.activation(out=rden[fs][DR, :], in_=oacc[DR, tc], func=AF.Ln)),
                               reads=[("oacc", hh, t)], writes=[("rden", fs)])
                        sc.add("act", (lambda e, fs=fs: e.activation(out=rden[fs][DR, :], in_=rden[fs][DR, :], func=AF.Exp, scale=-1.0)),
                               reads=[("rden", fs)], writes=[("rden", fs)])
                        sc.add("dve", (lambda e, fs=fs: e.tensor_copy(out=rden[fs][R, :], in_=rden[fs][DR, :])),
                               reads=[("rden", fs)], writes=[("rden", fs)])
                        sc.add("pool", (lambda e, fs=fs, tc=tc: e.tensor_tensor(
                            out=opair[R, tc], in0=oacc[R, tc], in1=rden[fs][R, :], op=ALU.mult)),
                            reads=[("oacc", hh, t), ("rden", fs)], writes=[("opair", t, hh)])

                units = [tiles[i:i + 2] for i in range(0, len(tiles), 2)]
                for un in units:
                    assert (un[0]["hh"], un[0]["oi"]) == (un[1]["hh"], un[1]["oi"])
                n = len(units)
                for i in range(min(LOOKU, n)):
                    scores(units[i], i)
                yield
                for i0 in range(0, n, BURST):
                    def do_pv():
                        for i in range(i0, min(i0 + BURST, n)):
                            for tl in units[i]:
                                pv(tl)
                                if tl["last_of_head"]:
                                    finalize(tl["hh"])

                    def do_sc():
                        for i in range(i0 + LOOKU, min(i0 + LOOKU + BURST, n)):
                            scores(units[i], i)
                    if SCHED["scores_first"] and BURST < LOOKU:
                        do_sc()
                        do_pv()
                    else:
                        do_pv()
                        do_sc()
                    yield

            DBG["pair"] = 0
            merge([in_proj_units(32, mk_ev_q(0)), in_proj_units(40, mk_ev_k(0)), in_proj_units(48, ev_v)], None)
            deferred = []
            for p in range(npairs):
                DBG["pair"] = p
                a = []
                if p >= 1:
                    a.append(in_proj_units(56 + p - 1, mk_ev_z(p - 1)))
                if p + 1 < npairs:
                    a += [in_proj_units(32 + p + 1, mk_ev_q(p + 1)), in_proj_units(40 + p + 1, mk_ev_k(p + 1)),
                          in_proj_units(48 + p + 1, ev_v)]
                else:
                    a.append(in_proj_units(56 + p, mk_ev_z(p, deferred)))
                merge(a, attention_steps(p))
            for post_ in deferred:
                post_()
            state["nbank"] = 3

            if stop_after == "yTa":
                dump(yTa[:, :, :], 8)
                done = True

        if not done:
            sc.barrier()
            u_sb = uview(0, BF16, [2048])
            cu = uview(4096, F32, [2050])
            acc = uview(12800, F32, [2048])
            yraw = uview(20992, F32, [2048])
            zsc = [uview(29184 + 1024 * i, BF16, [512]) for i in range(2)]
            sqc = [uview(31232 + 1024 * i, BF16, [512]) for i in range(2)]
            yTc = uview(34816, BF16, [8, 2048])
            sc.add("dve", lambda e: e.memset(cu[:, 0:1], 0.0), writes=[("cu", 0)])
            sc.add("dve", lambda e: e.memset(cu[:, 2049:2050], 0.0), writes=[("cu", 3)])
            cz = {"zz": 0}
            allcu = [("cu", t) for t in range(4)]
            allacc = [("acc", t) for t in range(4)]
            for f in range(8):
                def ev_u(t, bank):
                    sc.add("act", (lambda e, t=t, bank=bank: e.activation(out=u_sb[:, 512 * t:512 * t + 512], in_=pb[bank][:, :],
                                                                          func=AF.Copy)),
                           reads=[("pb", bank)], writes=[("u_sb", t)])

                def ev_cg(t, bank):
                    sc.add("dve", (lambda e, t=t, bank=bank: e.tensor_tensor(
                        out=cu[:, 1 + 512 * t:1 + 512 * t + 512], in0=pb[bank][:, :], in1=u_sb[:, 512 * t:512 * t + 512],
                        op=ALU.mult)),
                        reads=[("pb", bank), ("u_sb", t)], writes=[("cu", t)])

                in_proj(f, ev_u)
                in_proj(16 + f, ev_cg)
                sc.add("act", (lambda e, f=f: e.activation(out=acc[:, :], in_=cu[:, 1:2049], func=AF.Identity,
                                                           bias=convb[:, f:f + 1], scale=convw[:, f, 1:2])),
                       reads=allcu + ["convw", "convb"], writes=allacc)
                sc.add("dve", (lambda e, f=f: e.scalar_tensor_tensor(out=acc[:, :], in0=cu[:, 0:2048], scalar=convw[:, f, 0:1],
                                                                    in1=acc[:, :], op0=ALU.mult, op1=ALU.add)),
                       reads=allcu + allacc + ["convw"], writes=allacc)
                sc.add("dve", (lambda e, f=f: e.scalar_tensor_tensor(out=acc[:, :], in0=cu[:, 2:2050], scalar=convw[:, f, 2:3],
                                                                     in1=acc[:, :], op0=ALU.mult, op1=ALU.add)),
                       reads=allcu + allacc + ["convw"], writes=allacc)

                def ev_bg(t, bank):
                    sc.add("dve", (lambda e, t=t, bank=bank: e.tensor_tensor(
                        out=yraw[:, 512 * t:512 * t + 512], in0=pb[bank][:, :], in1=acc[:, 512 * t:512 * t + 512], op=ALU.mult)),
                        reads=[("pb", bank), ("acc", t)], writes=[("yraw", t)])

                in_proj(8 + f, ev_bg)

                def ev_zc(t, bank, f=f):
                    zs_ = cz["zz"] % 2
                    cz["zz"] += 1
                    tc = slice(512 * t, 512 * t + 512)
                    sc.add("act", (lambda e, zs_=zs_, bank=bank: e.activation(out=zsc[zs_], in_=pb[bank][:, :], func=AF.Silu)),
                           reads=[("pb", bank)], writes=[("zsc", zs_)])
                    sc.add("dve", (lambda e, zs_=zs_, tc=tc: e.scalar_tensor_tensor(
                        out=yTc[:, f, tc], in0=yraw[:, tc], scalar=gconv[:, f:f + 1], in1=zsc[zs_],
                        op0=ALU.mult, op1=ALU.mult)),
                        reads=[("yraw", t), ("zsc", zs_), "gconv"], writes=[("yTc", f, t)])
                    sc.add("pool", (lambda e, zs_=zs_, tc=tc: e.tensor_tensor(
                        out=sqc[zs_], in0=yraw[:, tc], in1=yraw[:, tc], op=ALU.mult)),
                        reads=[("yraw", t)], writes=[("sqc", zs_)])
                    for i4 in range(4):
                        col = 16 + 4 * t + i4
                        sc.add("pe", (lambda e, zs_=zs_, i4=i4, col=col: e.matmul(
                            pb[7][:, col:col + 1], lhsT=sqc[zs_][:, 128 * i4:128 * i4 + 128], rhs=ones_b[:, 0:1],
                            start=True, stop=True)),
                            reads=[("sqc", zs_), "ones_b"], writes=[("pb", 7)])

                in_proj(24 + f, ev_zc)
                if f == 0:
                    sc.add("dve", lambda e: e.tensor_copy(out=ssq_c, in_=pb[7][:, 16:32]), reads=[("pb", 7)], writes=["ssq_c"])
                else:
                    sc.add("dve", lambda e: e.tensor_tensor(out=ssq_c, in0=pb[7][:, 16:32], in1=ssq_c, op=ALU.add),
                           reads=[("pb", 7), "ssq_c"], writes=["ssq_c"])

            if stop_after == "yTc":
                sc.barrier()
                sc.add("sp", lambda e: e.dma_start(out=dbg_d[:, 0:8, :], in_=yTc[:, :, :]), dma_key="out")
                sc.add("sp", lambda e: e.dma_start(out=dbg_d[:, 8:16, :], in_=yTa[:, :, :]), dma_key="out")
                done = True

        if not done:
            sc.barrier()
            xt = [uview(8192 * i, F32, [2048]) for i in range(2)]
            yv = [uview(16384 + 8192 * i, F32, [2048]) for i in range(2)]
            junk = uview(32768, BF16, [512])
            gg = wsl[:, :, :, :].rearrange("p a b c -> p (a b c)").bitcast(F32)[:, 0:2048]
            wout = hT
            for src, dst, nm in ((ssq_a, ra, "ra"), (ssq_c, rc, "rc")):
                sc.add("dve", (lambda e, src=src, dst=dst: e.tensor_scalar(out=dst, in0=src, scalar1=1.0 / 1024.0, scalar2=EPS,
                                                                           op0=ALU.mult, op1=ALU.add)), writes=[nm])
                sc.add("act", (lambda e, dst=dst: e.activation(out=dst, in_=dst, func=AF.Sqrt)), reads=[nm], writes=[nm])
                sc.add("dve", (lambda e, dst=dst: e.reciprocal(out=dst, in_=dst)), reads=[nm], writes=[nm])
            for k in range(KC):
                sc.add("pool", (lambda e, k=k: e.dma_start(out=wout[:, k, :], in_=wout_d[128 * k:128 * k + 128, :])),
                       writes=[("wout", k)], dma_key=("wout", k))
            sc.add("sp", lambda e: e.dma_start(out=gg, in_=gpost_d[:, :].partition_broadcast(128)), writes=["gg"], dma_key="gg")
            dg = [yv[1][:, 0:128], yv[1][:, 128:256]]
            for k in range(KC):
                d2 = k % 2
                sc.add("dve", (lambda e, k=k, d2=d2: e.tensor_scalar(out=dg[d2], in0=ident_f[:, :], scalar1=gate_lay[:, k:k + 1],
                                                                    scalar2=None, op0=ALU.mult)),
                       reads=["ident_f", "gate_lay"], writes=[("dg", d2)])
                sc.add("pe", (lambda e, k=k, d2=d2: e.matmul(pb[k // 4][:, 128 * (k % 4):128 * (k % 4) + 128], lhsT=ones_f[:, :],
                                                            rhs=dg[d2], start=True, stop=True)),
                       reads=[("dg", d2), "ones_f"], writes=[("pb", k // 4)])
                if k % 4 == 3:
                    g4 = k // 4
                    sc.add("dve", (lambda e, g4=g4: e.tensor_tensor(out=gg[:, 512 * g4:512 * g4 + 512], in0=pb[g4][:, :],
                                                                   in1=gg[:, 512 * g4:512 * g4 + 512], op=ALU.mult)),
                           reads=[("pb", g4), "gg"], writes=["gg"])
            junk = wsl[:, :, :, :].rearrange("p a b c -> p (a b c)")[:, 4096:6144]

            def p3_mm(i, hf):
                tcol = slice(128 * i, 128 * i + 128)
                for grp in range(2):
                    for n in range(2):
                        bank = 4 * hf + 2 * grp + n
                        dc = slice(1024 * hf + 512 * n, 1024 * hf + 512 * n + 512)
                        for k in range(8 * grp, 8 * grp + 8):
                            lhsT = yTc[:, k, tcol] if k < 8 else yTa[:, k - 8, tcol]
                            sc.add("pe", (lambda e, bank=bank, lhsT=lhsT, k=k, dc=dc: e.matmul(
                                pb[bank][:, :], lhsT=lhsT, rhs=wout[:, k, dc], start=(k % 8 == 0), stop=(k % 8 == 7))),
                                reads=[("wout", k)], writes=[("pb", bank)])

            def p3_evac(i, hf):
                b2 = i % 2
                for n in range(2):
                    dc = slice(1024 * hf + 512 * n, 1024 * hf + 512 * n + 512)
                    q4 = 2 * hf + n
                    extra = [("dg", 0), ("dg", 1)] if b2 == 1 else []
                    sc.add("act", (lambda e, hf=hf, n=n, dc=dc: e.activation(
                        out=yv[b2][:, dc], in_=pb[4 * hf + n][:, :], func=AF.Identity, scale=rc[:, i:i + 1])),
                        reads=[("pb", 4 * hf + n), "rc"] + extra, writes=[("yv", b2, q4)] + extra)
                    sc.add("dve", (lambda e, hf=hf, n=n, dc=dc: e.scalar_tensor_tensor(
                        out=yv[b2][:, dc], in0=pb[4 * hf + 2 + n][:, :], scalar=ra[:, i:i + 1], in1=yv[b2][:, dc],
                        op0=ALU.mult, op1=ALU.add)),
                        reads=[("pb", 4 * hf + 2 + n), "ra", ("yv", b2, q4)], writes=[("yv", b2, q4)])

            def p3_post(i):
                b2 = i % 2
                allyv = [("yv", b2, q4) for q4 in range(4)]
                sc.add("act", (lambda e: e.activation(out=junk, in_=yv[b2], func=AF.Square, accum_out=ssq_p1[:, i:i + 1])),
                       reads=allyv, writes=["junk", ("rp", i)])
                sc.add("dve", (lambda e: e.tensor_scalar(out=rp[:, i:i + 1], in0=ssq_p1[:, i:i + 1], scalar1=1.0 / D,
                                                         scalar2=EPS, op0=ALU.mult, op1=ALU.add)),
                       reads=[("rp", i)], writes=[("rp", i)])
                sc.add("act", (lambda e: e.activation(out=rp[:, i:i + 1], in_=rp[:, i:i + 1], func=AF.Sqrt)),
                       reads=[("rp", i)], writes=[("rp", i)])
                sc.add("dve", (lambda e: e.reciprocal(out=rp[:, i:i + 1], in_=rp[:, i:i + 1])),
                       reads=[("rp", i)], writes=[("rp", i)])
                sc.add("dve", (lambda e: e.scalar_tensor_tensor(
                    out=yv[b2], in0=yv[b2], scalar=rp[:, i:i + 1], in1=gg, op0=ALU.mult, op1=ALU.mult)),
                    reads=allyv + [("rp", i), "gg"], writes=allyv)
                sc.add("pool", (lambda e: e.tensor_tensor(out=xt[b2], in0=yv[b2], in1=xt[b2], op=ALU.add)),
                       reads=allyv + [("xt", b2)], writes=[("xt", b2)])
                sc.add("act", (lambda e: e.dma_start(out=out_d[128 * i:128 * i + 128, :], in_=xt[b2])),
                       reads=[("xt", b2)], writes=[("outdram", i)], dma_key=("out", b2))

            for i in range(16):
                b2 = i % 2
                sc.add("sp", (lambda e, i=i, b2=b2: e.dma_start(out=xt[b2], in_=x_d[128 * i:128 * i + 128, :])),
                       writes=[("xt", b2)], dma_key=("xt", b2))
                p3_mm(i, 0)
                p3_evac(i, 0)
                if i > 0:
                    p3_post(i - 1)
                p3_mm(i, 1)
                p3_evac(i, 1)
            p3_post(15)

        sc.barrier()

        sc.finalize()
        dma_keys = list(sc.dma_cnt.keys())
        eng_sems = {e: es.enter_context(nc.semaphore(f"sem_{e}")) for e in Sched.ENGS}
        dma_sems = {k: es.enter_context(nc.semaphore(f"dsem_{i}")) for i, k in enumerate(dma_keys)}
        block = es.enter_context(nc.Block())

        @block.tensor
        def _(e):
            sc.emit("pe", e, eng_sems, dma_sems)

        @block.scalar
        def _(e):
            sc.emit("act", e, eng_sems, dma_sems)

        @block.vector
        def _(e):
            sc.emit("dve", e, eng_sems, dma_sems)

        @block.gpsimd
        def _(e):
            sc.emit("pool", e, eng_sems, dma_sems)

        @block.sync
        def _(e):
            sc.emit("sp", e, eng_sems, dma_sems)

    return nc


def _lay(v, nchunk):
    return np.ascontiguousarray(np.asarray(v, np.float32).reshape(nchunk, 128).T)


def make_in_maps(x, c, w_ada, b_ada, g_pre, w_in, conv_w, conv_b, g_conv, g_attn, w_out, g_post):
    x = np.asarray(x, np.float32)
    c = np.asarray(c, np.float32)
    w_ada0 = np.ascontiguousarray(np.asarray(w_ada, np.float32)[0])
    w_in0 = np.ascontiguousarray(np.asarray(w_in, np.float32)[0])
    w_out0 = np.ascontiguousarray(np.asarray(w_out, np.float32)[0])
    cw = np.asarray(conv_w, np.float32)[0]
    convw_lay = np.ascontiguousarray(cw.reshape(3, 8, 128).transpose(2, 1, 0))
    shared = {
        "w_ada": w_ada0,
        "b_row": np.ascontiguousarray(np.asarray(b_ada, np.float32)[0].reshape(1, NMOD)),
        "gpre_lay": _lay(np.asarray(g_pre)[0], 16),
        "w_in": w_in0,
        "convw_lay": convw_lay,
        "convb_lay": _lay(np.asarray(conv_b)[0], 8),
        "gconv_lay": _lay(np.asarray(g_conv)[0], 8),
        "gattn_lay": _lay(np.asarray(g_attn)[0], 8),
        "w_out": w_out0,
        "gpost_row": np.ascontiguousarray(np.asarray(g_post, np.float32)[0].reshape(1, D)),
    }
    maps = []
    for b in range(N_CORES):
        m = dict(shared)
        m["x"] = np.ascontiguousarray(x[b])
        m["c_lay"] = _lay(c[b], 16)
        maps.append(m)
    return maps


_NC_CACHE = {}


def kernel(x, c, w_ada, b_ada, g_pre, w_in, conv_w, conv_b, g_conv, g_attn, w_out, g_post):
    if "nc" not in _NC_CACHE:
        _NC_CACHE["nc"] = build_nc()
    nc = _NC_CACHE["nc"]
    in_maps = make_in_maps(x, c, w_ada, b_ada, g_pre, w_in, conv_w, conv_b, g_conv, g_attn, w_out, g_post)
    res = run_bass_kernel_spmd(nc, in_maps, core_ids=list(range(N_CORES)))
    out = np.stack([np.asarray(r["out"], np.float32) for r in res.results], axis=0)
    return out
```
